# Optimizing a Trainium2 kernel written in Bass

```python
import jax, jax.numpy as jnp
from jax import lax
import numpy as np

D_MODEL = 1024
BATCH = 8
SEQ = 4096
DEPTH = 4

CTX_LEN = 256
GRID_W = 64
N_MOD = 6
ATTN_WIDTH = 512
ATTN_HEADS = 8
ATTN_HEAD_DIM = ATTN_WIDTH // ATTN_HEADS
ATTN_KV_HEADS = 2
ATTN_GROUP = ATTN_HEADS // ATTN_KV_HEADS
KV_WIDTH = ATTN_KV_HEADS * ATTN_HEAD_DIM
Q_BLOCK = 128
ROPE_THETA = 10000.0
ROPE_AXIS_DIM = ATTN_HEAD_DIM // 2
MLSTM_WIDTH = D_MODEL - ATTN_WIDTH
MLSTM_HEADS = 4
MLSTM_HEAD_DIM = MLSTM_WIDTH // MLSTM_HEADS
MLSTM_CHUNK = 128
CONV_WIDTH = 3
N_GATES = 4 * MLSTM_HEADS
FFN_DIM = 4 * D_MODEL
NORM_EPS = 1e-6
IN_COLS = ATTN_WIDTH + 2 * KV_WIDTH + 4 * MLSTM_WIDTH + N_GATES
SPLIT_IDX = (
    ATTN_WIDTH,
    ATTN_WIDTH + KV_WIDTH,
    ATTN_WIDTH + 2 * KV_WIDTH,
    ATTN_WIDTH + 2 * KV_WIDTH + 2 * MLSTM_WIDTH,
    ATTN_WIDTH + 2 * KV_WIDTH + 3 * MLSTM_WIDTH,
    ATTN_WIDTH + 2 * KV_WIDTH + 4 * MLSTM_WIDTH,
)

kernel_name = "hybrid_mlstm_gqa_dit_block"


def _rmsnorm(x, gain):
    xf = x.astype(jnp.float32)
    y = xf * lax.rsqrt(jnp.mean(xf * xf, axis=-1, keepdims=True) + NORM_EPS)
    return (y * gain.astype(jnp.float32)).astype(x.dtype)


def _modulate(x, gain, shift, scale):
    return _rmsnorm(x, gain) * (1 + scale) + shift


def _axial_rope_tables(n_tokens):
    rows = n_tokens // GRID_W
    row_idx = jnp.repeat(jnp.arange(rows, dtype=jnp.float32), GRID_W)
    col_idx = jnp.tile(jnp.arange(GRID_W, dtype=jnp.float32), rows)
    inv_freq = jnp.power(ROPE_THETA, -jnp.arange(0, ROPE_AXIS_DIM, 2, dtype=jnp.float32) / ROPE_AXIS_DIM)
    ang = jnp.concatenate([row_idx[:, None] * inv_freq, col_idx[:, None] * inv_freq], axis=-1)
    return jnp.cos(ang), jnp.sin(ang)


def _apply_rope(x, cos, sin):
    B, T, H, hd = x.shape
    xf = x.astype(jnp.float32).reshape(B, T, H, hd // 2, 2)
    x0, x1 = xf[..., 0], xf[..., 1]
    c = cos[None, :, None, :]
    s = sin[None, :, None, :]
    return jnp.stack([x0 * c - x1 * s, x0 * s + x1 * c], axis=-1).reshape(B, T, H, hd).astype(x.dtype)


def _centred_dwconv(x, w):
    T = x.shape[1]
    pad = (CONV_WIDTH - 1) // 2
    xp = jnp.pad(x, ((0, 0), (pad, pad), (0, 0)))
    return sum(xp[:, j:j + T] * w[j] for j in range(CONV_WIDTH))


def _project_stream(h, w_in, b_gates, conv_w, q_gain, k_gain, rope):
    B, T, _ = h.shape
    p = h @ w_in
    a_q, a_k, a_v, m_qk, m_v, m_o, gates = jnp.split(p, SPLIT_IDX, axis=-1)
    a_q = _rmsnorm(a_q.reshape(B, T, ATTN_HEADS, ATTN_HEAD_DIM), q_gain)
    a_k = _rmsnorm(a_k.reshape(B, T, ATTN_KV_HEADS, ATTN_HEAD_DIM), k_gain)
    a_v = a_v.reshape(B, T, ATTN_KV_HEADS, ATTN_HEAD_DIM)
    if rope is not None:
        cos, sin = rope
        a_q = _apply_rope(a_q, cos, sin)
        a_k = _apply_rope(a_k, cos, sin)
    m_q, m_k = jnp.split(jax.nn.silu(_centred_dwconv(m_qk, conv_w)), 2, axis=-1)

    def heads(t):
        return t.reshape(B, T, MLSTM_HEADS, MLSTM_HEAD_DIM).transpose(0, 2, 1, 3).astype(jnp.float32)

    m_q, m_k, m_v = heads(m_q), heads(m_k) * (MLSTM_HEAD_DIM ** -0.5), heads(m_v)
    g = (gates.astype(jnp.float32) + b_gates.astype(jnp.float32)).transpose(0, 2, 1)
    i_f, f_f, i_b, f_b = jnp.split(g, 4, axis=1)
    fwd = (m_q, m_k, m_v, i_f, jax.nn.log_sigmoid(f_f))
    bwd = (jnp.flip(m_q, 2), jnp.flip(m_k, 2), jnp.flip(m_v, 2),
           jnp.flip(i_b, -1), jax.nn.log_sigmoid(jnp.flip(f_b, -1)))
    return (a_q, a_k, a_v), fwd, bwd, jax.nn.sigmoid(m_o)


def _blocked_attention(q, k, v):
    B, T, HQ, hd = q.shape
    nb = T // Q_BLOCK
    qb = q.reshape(B, nb, Q_BLOCK, ATTN_KV_HEADS, ATTN_GROUP, hd).transpose(1, 0, 2, 3, 4, 5)
    scale = hd ** -0.5

    def block(qblk):
        s = jnp.einsum('bqkgd,bskd->bkgqs', qblk, k, preferred_element_type=jnp.float32) * scale
        p = jax.nn.softmax(s, axis=-1).astype(v.dtype)
        return jnp.einsum('bkgqs,bskd->bqkgd', p, v)

    o = lax.map(block, qb)
    return o.transpose(1, 0, 2, 3, 4, 5).reshape(B, T, HQ * hd)


def _mlstm_init(batch):
    return (jnp.zeros((batch, MLSTM_HEADS, MLSTM_HEAD_DIM, MLSTM_HEAD_DIM), jnp.float32),
            jnp.zeros((batch, MLSTM_HEADS, MLSTM_HEAD_DIM), jnp.float32),
            jnp.zeros((batch, MLSTM_HEADS), jnp.float32))


def _mlstm_scan(q, k, v, ig, lf, state, emit):
    B, H, T, d = q.shape
    nc = T // MLSTM_CHUNK

    def chunks(a):
        return jnp.moveaxis(a.reshape(B, H, nc, MLSTM_CHUNK, *a.shape[3:]), 2, 0)

    tril = jnp.tril(jnp.ones((MLSTM_CHUNK, MLSTM_CHUNK), dtype=bool))

    def step(carry, xs):
        C, n, m = carry
        qc, kc, vc, igc, lfc = xs
        b = jnp.cumsum(lfc, axis=-1)
        b_last = b[..., -1]
        g = b_last[..., None] - b + igc
        m_new = jnp.maximum(b_last + m, jnp.max(g, axis=-1))
        decay = jnp.exp(b_last + m - m_new)
        w = jnp.exp(g - m_new[..., None])
        C_new = decay[..., None, None] * C + jnp.einsum('bhsv,bhsk->bhvk', vc * w[..., None], kc)
        n_new = decay[..., None] * n + jnp.einsum('bhs,bhsk->bhk', w, kc)
        if not emit:
            return (C_new, n_new, m_new), None
        dmat = jnp.where(tril, b[..., :, None] - b[..., None, :] + igc[..., None, :], -jnp.inf)
        m_inter = b + m[..., None]
        m_t = jnp.maximum(m_inter, jnp.max(dmat, axis=-1))
        w_inter = jnp.exp(m_inter - m_t)
        s = jnp.einsum('bhtk,bhsk->bhts', qc, kc) * jnp.exp(dmat - m_t[..., None])
        num = w_inter[..., None] * jnp.einsum('bhvk,bhtk->bhtv', C, qc) + jnp.einsum('bhts,bhsv->bhtv', s, vc)
        den = w_inter * jnp.einsum('bhk,bhtk->bht', n, qc) + jnp.sum(s, axis=-1)
        h = num / jnp.maximum(jnp.abs(den), jnp.exp(-m_t))[..., None]
        return (C_new, n_new, m_new), h

    state, hs = lax.scan(step, state, tuple(chunks(a) for a in (q, k, v, ig, lf)))
    if not emit:
        return None, state
    return jnp.moveaxis(hs, 0, 2).reshape(B, H, T, d), state


def _merge_mlstm(h_f, h_b, o, gain):
    B, H, T, d = h_f.shape
    h = (h_f + h_b).transpose(0, 2, 1, 3)
    h = _rmsnorm(h, gain.reshape(H, d)).reshape(B, T, H * d)
    return (o * h).astype(o.dtype)


def _hybrid_mixer(h_lat, h_ctx, w_in, b_gates, conv_w, q_gain, k_gain, ml_gain, w_out, rope, emit_ctx):
    attn_l, fwd_l, bwd_l, o_l = _project_stream(h_lat, w_in, b_gates, conv_w, q_gain, k_gain, rope)
    attn_c, fwd_c, bwd_c, o_c = _project_stream(h_ctx, w_in, b_gates, conv_w, q_gain, k_gain, None)
    q_l, k_l, v_l = attn_l
    q_c, k_c, v_c = attn_c
    att_l = _blocked_attention(q_l, jnp.concatenate([k_c, k_l], axis=1), jnp.concatenate([v_c, v_l], axis=1))
    init = _mlstm_init(h_lat.shape[0])
    hc_f, st_f = _mlstm_scan(*fwd_c, init, emit_ctx)
    hc_b, st_b = _mlstm_scan(*bwd_c, init, emit_ctx)
    hl_f, _ = _mlstm_scan(*fwd_l, st_f, True)
    hl_b, _ = _mlstm_scan(*bwd_l, st_b, True)
    mem_l = _merge_mlstm(hl_f, jnp.flip(hl_b, 2), o_l, ml_gain)
    y_lat = jnp.concatenate([att_l, mem_l.astype(att_l.dtype)], axis=-1) @ w_out
    if not emit_ctx:
        return y_lat, None
    att_c = _blocked_attention(q_c, k_c, v_c)
    mem_c = _merge_mlstm(hc_f, jnp.flip(hc_b, 2), o_c, ml_gain)
    y_ctx = jnp.concatenate([att_c, mem_c.astype(att_c.dtype)], axis=-1) @ w_out
    return y_lat, y_ctx


def _sqrelu_mlp(h, w1, w2):
    return jnp.square(jax.nn.relu(h @ w1)) @ w2


def setup_inputs(seed: int = 0) -> dict:
    key = jax.random.key(seed)
    ks = jax.random.split(key, 20)

    def nrm(k, shape, scale):
        return jax.random.normal(k, shape, jnp.float32) * scale

    f_bias = jnp.linspace(3.0, 6.0, MLSTM_HEADS, dtype=jnp.float32)
    b_gates = jnp.concatenate([
        nrm(ks[9], (DEPTH, MLSTM_HEADS), 0.1),
        f_bias + nrm(ks[10], (DEPTH, MLSTM_HEADS), 0.1),
        nrm(ks[11], (DEPTH, MLSTM_HEADS), 0.1),
        f_bias + nrm(ks[12], (DEPTH, MLSTM_HEADS), 0.1),
    ], axis=-1)
    return {
        "x": nrm(ks[0], (BATCH, SEQ, D_MODEL), 1.0),
        "c": nrm(ks[1], (BATCH, D_MODEL), 1.0),
        "ctx": nrm(ks[2], (BATCH, CTX_LEN, D_MODEL), 1.0),
        "c_ctx": nrm(ks[3], (D_MODEL,), 1.0),
        "w_ada": nrm(ks[4], (DEPTH, D_MODEL, N_MOD * D_MODEL), 0.5 * D_MODEL ** -0.5),
        "b_ada": nrm(ks[5], (DEPTH, N_MOD * D_MODEL), 0.02),
        "norm_mix": 1.0 + nrm(ks[6], (DEPTH, D_MODEL), 0.02),
        "norm_mlp": 1.0 + nrm(ks[7], (DEPTH, D_MODEL), 0.02),
        "w_in": nrm(ks[8], (DEPTH, D_MODEL, IN_COLS), D_MODEL ** -0.5),
        "b_gates": b_gates,
        "conv_qk": nrm(ks[13], (DEPTH, CONV_WIDTH, 2 * MLSTM_WIDTH), CONV_WIDTH ** -0.5),
        "q_norm": 1.0 + nrm(ks[14], (DEPTH, ATTN_HEAD_DIM), 0.02),
        "k_norm": 1.0 + nrm(ks[15], (DEPTH, ATTN_HEAD_DIM), 0.02),
        "mlstm_norm": 1.0 + nrm(ks[16], (DEPTH, MLSTM_WIDTH), 0.02),
        "w_out": nrm(ks[17], (DEPTH, D_MODEL, D_MODEL), D_MODEL ** -0.5),
        "w_mlp_in": nrm(ks[18], (DEPTH, D_MODEL, FFN_DIM), D_MODEL ** -0.5),
        "w_mlp_out": nrm(ks[19], (DEPTH, FFN_DIM, D_MODEL), FFN_DIM ** -0.5),
        "norm_final": 1.0 + nrm(jax.random.fold_in(key, 99), (D_MODEL,), 0.02),
    }


def reference(x, c, ctx, c_ctx, w_ada, b_ada, norm_mix, norm_mlp, w_in, b_gates, conv_qk,
              q_norm, k_norm, mlstm_norm, w_out, w_mlp_in, w_mlp_out, norm_final):
    rope = _axial_rope_tables(x.shape[1])
    silu_c = jax.nn.silu(c)
    silu_cc = jax.nn.silu(c_ctx)
    for layer in range(DEPTH):
        emit_ctx = layer < DEPTH - 1
        mod_l = (silu_c @ w_ada[layer] + b_ada[layer])[:, None, :]
        mod_c = silu_cc @ w_ada[layer] + b_ada[layer]
        sh1, sc1, g1, sh2, sc2, g2 = jnp.split(mod_l, N_MOD, axis=-1)
        csh1, csc1, cg1, csh2, csc2, cg2 = jnp.split(mod_c, N_MOD, axis=-1)
        y_l, y_c = _hybrid_mixer(
            _modulate(x, norm_mix[layer], sh1, sc1),
            _modulate(ctx, norm_mix[layer], csh1, csc1),
            w_in[layer], b_gates[layer], conv_qk[layer], q_norm[layer], k_norm[layer],
            mlstm_norm[layer], w_out[layer], rope, emit_ctx)
        x = x + g1 * y_l
        x = x + g2 * _sqrelu_mlp(_modulate(x, norm_mlp[layer], sh2, sc2), w_mlp_in[layer], w_mlp_out[layer])
        if emit_ctx:
            ctx = ctx + cg1 * y_c
            ctx = ctx + cg2 * _sqrelu_mlp(_modulate(ctx, norm_mlp[layer], csh2, csc2), w_mlp_in[layer], w_mlp_out[layer])
    return _rmsnorm(x, norm_final)
```

```python
import math
import os
KSKIP = os.environ.get('KSKIP', '')
from contextlib import ExitStack
import numpy as np
import concourse.bass as bass
import concourse.mybir as mybir
from concourse.bass_utils import run_bass_kernel_spmd

F32 = mybir.dt.float32
BF16 = mybir.dt.bfloat16
ALU = mybir.AluOpType
AF = mybir.ActivationFunctionType
AX = mybir.AxisListType

DEPTH = 4
D = 1024
TC = 256
TL = 4096
T = TC + TL
NCH = T // 128
INC = 2832
INCP = 2944
FFN = 4096
EPS = 1e-6
ARENA = 210944

CH = 2048
RROT = 8
NDMA = {"sp": 28, "pool": 20}
CENG = ("pe", "act", "dve", "pool")
ENGS = ("pe", "act", "dve", "pool", "sp")


class Res:
    __slots__ = ("writers", "readers")

    def __init__(self):
        self.writers = []
        self.readers = []


class Rec:
    __slots__ = ("eng", "fn", "idx", "cwaits", "dwaits", "signal", "is_dma", "dma_id", "sig_k")

    def __init__(self, eng, fn, is_dma):
        self.eng = eng
        self.fn = fn
        self.is_dma = is_dma
        self.cwaits = {}
        self.dwaits = {}
        self.signal = False
        self.dma_id = None
        self.sig_k = None


class Prog:
    def __init__(self, nc):
        self.nc = nc
        self.streams = {e: [] for e in ENGS}
        self.maxw = {e: {} for e in ENGS}
        self.dw = {e: set() for e in ENGS}
        self.ndma = {q: 0 for q in NDMA}
        self.res = {}
        self.last_real = {e: None for e in CENG}
        self.dma_live = {q: {} for q in NDMA}

    def R(self, name):
        r = self.res.get(name)
        if r is None:
            r = Res()
            self.res[name] = r
        return r

    def _need(self, rec, tok, same_ok):
        kind, e, i, prod = tok
        if kind == "c":
            if e == rec.eng and (same_ok or e == "pe"):
                return
            if self.maxw[rec.eng].get(e, -1) >= i:
                return
            if rec.cwaits.get(e, (-1, None))[0] < i:
                rec.cwaits[e] = (i, prod)
        else:
            key = (e, i)
            if key in self.dw[rec.eng]:
                return
            rec.dwaits[key] = prod

    def _commit(self, rec):
        for e, (i, prod) in rec.cwaits.items():
            self.maxw[rec.eng][e] = i
            prod.signal = True
        for key, prod in rec.dwaits.items():
            self.dw[rec.eng].add(key)

    def op(self, eng, fn, reads=(), writes=(), is_dma=False):
        rec = Rec(eng, fn, is_dma)
        st = self.streams[eng]
        rec.idx = len(st)
        psn = sorted({n for n in list(reads) + list(writes) if n.startswith("pb") or n == "modps"})
        reads = [self.R(r) for r in reads if r not in psn]
        writes = [self.R(r) for r in writes if r not in psn]
        psr = [self.R(n) for n in psn]
        for r in psr:
            for tok in r.writers:
                self._need(rec, tok, True)
        for r in reads:
            for tok in r.writers:
                self._need(rec, tok, False)
        for w in writes:
            for tok in w.writers:
                self._need(rec, tok, True)
            for tok in w.readers:
                self._need(rec, tok, True)
        self._commit(rec)
        if is_dma:
            rec.dma_id = self.ndma[eng]
            self.ndma[eng] += 1
            self.dma_live[eng][rec.dma_id % NDMA[eng]] = rec
            tok = ("d", eng, rec.dma_id, rec)
        else:
            tok = ("c", eng, rec.idx, rec)
            self.last_real[eng] = rec
        for r in reads:
            r.readers.append(tok)
        for w in writes:
            w.writers = [tok]
            w.readers = []
        for r in psr:
            r.writers = [tok]
            r.readers = []
        st.append(rec)
        return rec

    def barrier(self):
        for eng in ENGS:
            rec = Rec(eng, None, False)
            rec.idx = len(self.streams[eng])
            for e in CENG:
                lr = self.last_real[e]
                if lr is not None:
                    self._need(rec, ("c", e, lr.idx, lr), True)
            for q in NDMA:
                for slot, d in self.dma_live[q].items():
                    self._need(rec, ("d", q, d.dma_id, d), True)
            self._commit(rec)
            self.streams[eng].append(rec)
        self.res = {}

    def emit(self, es):
        nc = self.nc
        sems = {e: [es.enter_context(nc.semaphore(f"c_{e}_{i}")) for i in range(RROT)] for e in CENG}
        dsems = {q: [es.enter_context(nc.semaphore(f"d_{q}_{i}")) for i in range(n)] for q, n in NDMA.items()}
        for e in ENGS:
            k = 0
            for rec in self.streams[e]:
                if rec.is_dma or rec.fn is None:
                    continue
                if rec.signal:
                    rec.sig_k = k
                    k += 1

        def csem(e, k):
            return sems[e][(k // CH) % RROT], (k // (CH * RROT)) * CH + (k % CH) + 1

        def dsem(q, i):
            n = NDMA[q]
            return dsems[q][i % n], 16 * (i // n + 1)

        block = es.enter_context(nc.Block())

        def run(ename):
            def body(eng):
                for rec in self.streams[ename]:
                    for e, (i, prod) in rec.cwaits.items():
                        s, v = csem(e, prod.sig_k)
                        eng.wait_ge(s, v)
                    for (q, i) in rec.dwaits:
                        s, v = dsem(q, i)
                        eng.wait_ge(s, v)
                    if rec.fn is None:
                        continue
                    if rec.is_dma:
                        n = NDMA[ename]
                        if rec.dma_id >= n:
                            s, v = dsem(ename, rec.dma_id - n)
                            eng.wait_ge(s, v)
                    ins = rec.fn(eng)
                    if rec.is_dma:
                        s, v = dsem(ename, rec.dma_id)
                        ins.then_inc(s, 16)
                    elif rec.signal:
                        s, v = csem(ename, rec.sig_k)
                        ins.then_inc(s, 1)
                if ename in NDMA and self.ndma[ename] > 0:
                    n = NDMA[ename]
                    tot = self.ndma[ename]
                    for slot in range(min(n, tot)):
                        last = ((tot - 1 - slot) // n) * n + slot
                        s, v = dsem(ename, last)
                        eng.wait_ge(s, v)
            return body

        block.sync(run("sp"))
        block.tensor(run("pe"))
        block.scalar(run("act"))
        block.vector(run("dve"))
        block.gpsimd(run("pool"))


def tres(name, t0, n):
    return [f"{name}:{i}" for i in range(t0 // 128, (t0 + n + 127) // 128)]


def build(n_layers=DEPTH, debug=False, stop=None):
    nc = bass.Bass("TRN2", target_bir_lowering=False)

    def din(name, shape, dt=F32):
        return nc.dram_tensor(name, list(shape), dt, kind="ExternalInput").ap()

    def dscr(name, shape, dt):
        return nc.dram_tensor(name, list(shape), dt, kind="ExternalOutput" if debug else "Internal").ap()

    x_d = din("x", [TL, D])
    ctx_d = din("ctx", [TC, D])
    cc_d = din("cc", [128, 8, 2])
    w_ada_d = din("w_ada", [DEPTH, D, 6 * D])
    b_ada_d = din("b_ada_c", [128, DEPTH, 48])
    nmix_d = din("nmix_c", [128, DEPTH, 8])
    nmlp_d = din("nmlp_c", [128, DEPTH, 8])
    w_in_d = din("w_in_p", [DEPTH, D, INCP])
    bg_d = din("bg_c", [16, DEPTH])
    conv_d = din("conv_c", [128, DEPTH, 3, 8])
    qkg_d = din("qk_gain", [128, DEPTH, 640])
    mlg_d = din("ml_gain", [128, DEPTH, 512])
    w_out_d = din("w_out_p", [DEPTH, D, D])
    w1_d = din("w_mlp_in", [DEPTH, D, FFN])
    w2_d = din("w_mlp_out", [DEPTH, FFN, D])
    nfin_d = din("nfin_c", [128, 8])
    ident_d = din("ident", [128, 128])
    maskf_d = din("maskf", [128, 128])
    maskb_d = din("maskb", [128, 128])
    sel_d = din("sel4", [128, 4, 128])
    cos_d = din("rope_cos", [128, 32, 32])
    sin_d = din("rope_sin", [128, 32, 32])
    out_d = nc.dram_tensor("out", [TL, D], F32, kind="ExternalOutput").ap()

    xT_d = dscr("xT", [8, 128, T], F32)
    mqkT_d = dscr("mqkT", [8, 128, T], F32)
    gT_d = dscr("gT", [16, T], F32)
    QT_d = dscr("QT", [4, 128, T], BF16)
    KT_d = dscr("KT", [128, T], BF16)
    tmv_d = dscr("tmv", [T, 1152], BF16)
    attT_d = dscr("attT", [8, 128, T], BF16)
    hf_d = dscr("hf", [T, 512], F32)
    hb_d = dscr("hb", [T, 512], F32)

    P = Prog(nc)
    es = ExitStack()
    arena = es.enter_context(nc.sbuf_tensor("arena", [128, ARENA // 4], F32))
    psum = es.enter_context(nc.psum_tensor("psum", [128, 4096], F32))

    class Alloc:
        def __init__(self, base=0):
            self.off = base

        def get(self, shape, dt, parts=128):
            n = 1
            for s in shape:
                n *= s
            es_ = 4 if dt == F32 else 2
            nb = (n * es_ + 3) // 4 * 4
            w0 = self.off // 4
            ap = arena[0:parts, w0:w0 + nb // 4]
            self.off += nb
            assert self.off <= ARENA, f"arena overflow {self.off}"
            if dt == BF16:
                ap = ap.bitcast(BF16)
            if len(shape) == 2:
                ap = ap.rearrange("p (a b) -> p a b", a=shape[0])
            elif len(shape) == 3:
                ap = ap.rearrange("p (a b c) -> p a b c", a=shape[0], b=shape[1])
            return ap

    def bank(i):
        return psum[:, i * 512:(i + 1) * 512]

    def bank_bf(i):
        return psum[:, i * 512:(i + 1) * 512].bitcast(BF16)

    class Rot:
        def __init__(self, items):
            self.items = items
            self.i = 0

        def next(self):
            it = self.items[self.i % len(self.items)]
            self.i += 1
            return it

    def dma(q, out, in_, reads=(), writes=()):
        P.op(q, lambda e: e.dma_start(out=out, in_=in_), reads=reads, writes=writes, is_dma=True)

    A0 = Alloc(0)
    identF = A0.get([128], F32)
    identB = A0.get([128], BF16)
    onesB = A0.get([128], BF16)
    maskT = [A0.get([128], F32), A0.get([128], F32)]
    selF = A0.get([4, 128], F32)
    selB = A0.get([4, 128], BF16)
    cst = A0.get([4], F32)
    cc = A0.get([8, 2], F32)
    sc = A0.get([8, 2], F32)
    mod = A0.get([DEPTH * 48, 2], F32)
    s1 = A0.get([DEPTH * 8, 2], F32)
    s2 = A0.get([DEPTH * 8, 2], F32)
    nmix = A0.get([DEPTH * 8], F32)
    nmlp = A0.get([DEPTH * 8], F32)
    bada = A0.get([DEPTH * 48], F32)
    bgc = A0.get([DEPTH], F32, parts=16)
    convc = A0.get([DEPTH * 3 * 8], F32)
    nfin = A0.get([8], F32)
    PBASE = A0.off

    def modcol(L, j, r):
        return mod[:, L * 48 + j, r:r + 1]

    dma("sp", identF, ident_d, writes=["identF"])
    dma("sp", maskT[0], maskf_d, writes=["maskf"])
    dma("sp", maskT[1], maskb_d, writes=["maskb"])
    dma("sp", selF, sel_d, writes=["selF"])
    dma("sp", cc, cc_d, writes=["cc"])
    dma("sp", nmix, nmix_d.rearrange("p l j -> p (l j)"), writes=["nmix"])
    dma("sp", nmlp, nmlp_d.rearrange("p l j -> p (l j)"), writes=["nmlp"])
    dma("sp", bada, b_ada_d.rearrange("p l j -> p (l j)"), writes=["bada"])
    dma("sp", bgc, bg_d, writes=["bgc"])
    dma("sp", convc, conv_d.rearrange("p l a c -> p (l a c)"), writes=["convc"])
    dma("sp", nfin, nfin_d, writes=["nfin"])
    P.op("act", lambda e: e.activation(out=identB, in_=identF, func=AF.Copy), reads=["identF"], writes=["identB"])
    P.op("pool", lambda e: e.memset(onesB, 1.0), writes=["onesB"])
    P.op("act", lambda e: e.activation(out=selB, in_=selF, func=AF.Copy), reads=["selF"], writes=["selB"])
    P.op("pool", lambda e: e.memset(cst[:, 0:1], EPS), writes=["cst0"])
    P.op("pool", lambda e: e.memset(cst[:, 1:2], 1.0), writes=["cst1"])
    P.op("pool", lambda e: e.memset(cst[:, 2:3], -0.5 * math.log(128.0)), writes=["cst2"])
    P.op("pool", lambda e: e.memset(cst[:, 3:4], 0.0), writes=["cst3"])
    CST = ["cst0", "cst1", "cst2", "cst3"]
    P.op("act", lambda e: e.activation(out=sc, in_=cc, func=AF.Silu), reads=["cc"], writes=["sc"])

    def phase_mod():
        A = Alloc(PBASE)
        wa = [A.get([8, 512], F32), A.get([8, 512], F32)]
        modps = bank(0)
        it = 0
        for L in range(n_layers):
            wsrc = w_ada_d[L].rearrange("(k p) c -> p k c", p=128)
            for pc in range(12):
                buf = wa[it % 2]
                bn = f"wa{it % 2}"
                it += 1
                dma("sp", buf, wsrc[:, :, pc * 512:(pc + 1) * 512], writes=[bn])
                for cjj in range(4):
                    j = pc * 4 + cjj
                    col = (L * 48 + j) * 2
                    for k in range(8):
                        P.op("pe", (lambda buf=buf, k=k, cjj=cjj, col=col: lambda e: e.matmul(
                            modps[:, col:col + 2], lhsT=buf[:, k, cjj * 128:(cjj + 1) * 128], rhs=sc[:, k, :],
                            start=(k == 0), stop=(k == 7)))(),
                            reads=[bn, "sc"], writes=["modps"])
        nl = n_layers * 48
        P.op("dve", lambda e: e.tensor_tensor(
            out=mod[:, 0:nl, :], in0=modps[:, 0:nl * 2].rearrange("p (a b) -> p a b", b=2),
            in1=bada[:, 0:nl].unsqueeze(2).to_broadcast([128, nl, 2]), op=ALU.add),
            reads=["modps", "bada"], writes=["mod"])
        for L in range(n_layers):
            for (dst, nm, nmn, jo) in ((s1, nmix, "nmix", 8), (s2, nmlp, "nmlp", 32)):
                P.op("dve", (lambda dst=dst, L=L, jo=jo: lambda e: e.tensor_scalar_add(
                    out=dst[:, L * 8:(L + 1) * 8, :], in0=mod[:, L * 48 + jo:L * 48 + jo + 8, :], scalar1=1.0))(),
                    reads=["mod"], writes=["s12"])
                P.op("dve", (lambda dst=dst, L=L, nm=nm: lambda e: e.tensor_tensor(
                    out=dst[:, L * 8:(L + 1) * 8, :], in0=dst[:, L * 8:(L + 1) * 8, :],
                    in1=nm[:, L * 8:(L + 1) * 8].unsqueeze(2).to_broadcast([128, 8, 2]), op=ALU.mult))(),
                    reads=["s12", nmn], writes=["s12"])
        P.barrier()

    def phase_xT():
        A = Alloc(PBASE)
        xin = [A.get([1024], F32), A.get([1024], F32)]
        xst = [A.get([8, 128], F32), A.get([8, 128], F32)]
        rot = Rot([1, 2, 3, 4])
        for g in range(NCH):
            src = ctx_d[g * 128:(g + 1) * 128, :] if g < 2 else x_d[(g - 2) * 128:(g - 1) * 128, :]
            xb, xn_ = xin[g % 2], f"xin{g % 2}"
            st, stn = xst[g % 2], f"xst{g % 2}"
            dma("sp", xb, src, writes=[xn_])
            for half in range(2):
                b = rot.next()
                bk = bank(b)
                for jj in range(4):
                    j = half * 4 + jj
                    P.op("pe", (lambda bk=bk, jj=jj, j=j, xb=xb: lambda e: e.transpose(
                        out=bk[:, jj * 128:(jj + 1) * 128], in_=xb[:, j * 128:(j + 1) * 128], identity=identF))(),
                        reads=[xn_, "identF"], writes=[f"pb{b}"])
                eng = "act" if half == 0 else "dve"
                if eng == "act":
                    P.op("act", (lambda bk=bk, st=st, half=half: lambda e: e.activation(
                        out=st[:, half * 4:half * 4 + 4, :], in_=bk.rearrange("p (a b) -> p a b", a=4), func=AF.Copy))(),
                        reads=[f"pb{b}"], writes=[stn])
                else:
                    P.op("dve", (lambda bk=bk, st=st, half=half: lambda e: e.tensor_copy(
                        out=st[:, half * 4:half * 4 + 4, :], in_=bk.rearrange("p (a b) -> p a b", a=4)))(),
                        reads=[f"pb{b}"], writes=[stn])
            dma("sp", xT_d[:, :, g * 128:(g + 1) * 128].rearrange("j p t -> p j t"), st, reads=[stn],
                writes=tres("xT", g * 128, 128))
        P.barrier()

    def norm_mod(xTb, xname, n, sq, rstd, xn, hT, hname, scol, bcol, ssb):
        ssps = bank(ssb)
        for j in range(8):
            s_, sn = sq[j % 2], f"sq{j % 2}"
            P.op("act", (lambda s_=s_, j=j: lambda e: e.activation(out=s_[:, 0:n], in_=xTb[:, j, 0:n], func=AF.Square))(),
                 reads=[xname], writes=[sn])
            P.op("pe", (lambda s_=s_, j=j: lambda e: e.matmul(ssps[:, 0:n], lhsT=onesB, rhs=s_[:, 0:n],
                                                              start=(j == 0), stop=(j == 7)))(),
                 reads=[sn, "onesB"], writes=[f"pb{ssb}"])
        P.op("act", lambda e: e.activation(out=rstd[:, 0:n], in_=ssps[:, 0:n], func=AF.Sqrt, scale=1.0 / D, bias=cst[:, 0:1]),
             reads=[f"pb{ssb}"] + CST, writes=["rstd"])
        P.op("dve", lambda e: e.reciprocal(out=rstd[:, 0:n], in_=rstd[:, 0:n]), reads=["rstd"], writes=["rstd"])
        for j in range(8):
            x_, xn_ = xn[j % 2], f"xn{j % 2}"
            P.op("dve", (lambda x_=x_, j=j: lambda e: e.tensor_tensor(out=x_[:, 0:n], in0=xTb[:, j, 0:n], in1=rstd[:, 0:n],
                                                                      op=ALU.mult))(),
                 reads=[xname, "rstd"], writes=[xn_])
            P.op("act", (lambda x_=x_, j=j: lambda e: e.activation(out=hT[:, j, 0:n], in_=x_[:, 0:n], func=AF.Identity,
                                                                   scale=scol(j), bias=bcol(j)))(),
                 reads=[xn_, "s12", "mod"], writes=[hname])

    def phase_inproj(L):
        A = Alloc(PBASE)
        w_in = A.get([8, INCP], BF16)
        qkg = A.get([640], F32)
        cosT = A.get([32, 32], F32)
        sinT = A.get([32, 32], F32)
        dma("sp", cosT, cos_d, writes=["cosT"])
        dma("sp", sinT, sin_d, writes=["sinT"])
        xTb = [A.get([8, 512], F32), A.get([8, 512], F32)]
        sq = [A.get([512], BF16), A.get([512], BF16)]
        rstd = A.get([512], F32)
        xn = [A.get([512], F32), A.get([512], F32)]
        hT = A.get([8, 512], BF16)
        qk_sb = A.get([640], F32)
        sqq = A.get([640], F32)
        ss = A.get([10], F32)
        qn = A.get([640], F32)
        ra = A.get([320], F32)
        rb = A.get([320], F32)
        qr = A.get([640], BF16)
        QTst = A.get([4, 512], BF16)
        KTst = A.get([512], BF16)
        tmvst = [A.get([1152], BF16), A.get([1152], BF16)]
        fmst = A.get([8, 512], F32)
        gst = A.get([512], F32, parts=16)

        wsrc = w_in_d[L].rearrange("(k p) c -> p k c", p=128)
        for (c0, c1) in ((0, 512), (512, 768), (768, 1792), (1792, 2304), (2304, 2816), (2816, 2944)):
            if 'w' not in KSKIP:
                dma("pool", w_in[:, :, c0:c1], wsrc[:, :, c0:c1], writes=[f"w_in{c0}"])
        dma("sp", qkg, qkg_d[:, L, :], writes=["qkg"])
        rot = Rot([2, 3, 4] if 'b' in KSKIP else [2, 3, 4, 5, 6, 7])
        blocks = [(0, 256)] + [(256 + 512 * i, 512) for i in range(8)]
        def do_block(bi, t0, n):
            r = 1 if t0 < TC else 0
            xb, xname = xTb[bi % 2], f"xTb{bi % 2}"
            dma("sp", xb[:, :, 0:n], xT_d[:, :, t0:t0 + n].rearrange("j p t -> p j t"),
                reads=tres("xT", t0, n), writes=[xname])
            if 'n' not in KSKIP:
                norm_mod(xb, xname, n, sq, rstd, xn, hT, "hT",
                         lambda j: s1[:, L * 8 + j, r:r + 1], lambda j: modcol(L, j, r), 0)
            for m in range(0 if 'm' in KSKIP else n // 128):
                tok0 = t0 + m * 128
                ms = slice(m * 128, (m + 1) * 128)
                tv, tvn = tmvst[m % 2], f"tmvst{m % 2}"
                groups = ((0, 512, "w_in0"), (512, 768, "w_in512"), (1792, 2304, "w_in1792"), (2304, 2816, "w_in2304"))
                pbs = []
                for (c0, c1, wn) in groups:
                    b = rot.next()
                    pbs.append(b)
                    for k in range(8):
                        P.op("pe", (lambda b=b, k=k, c0=c0, c1=c1, ms=ms: lambda e: e.matmul(
                            bank(b)[:, 0:c1 - c0], lhsT=hT[:, k, ms], rhs=w_in[:, k, c0:c1], start=(k == 0), stop=(k == 7)))(),
                            reads=["hT", wn], writes=[f"pb{b}"])
                b0, b1, b2, b3 = pbs
                P.op("act", (lambda b0=b0: lambda e: e.activation(out=qk_sb[:, 0:512], in_=bank(b0), func=AF.Copy))(),
                     reads=[f"pb{b0}"], writes=["qk_sb_a"])
                P.op("act", (lambda b1=b1: lambda e: e.activation(out=qk_sb[:, 512:640], in_=bank(b1)[:, 0:128], func=AF.Copy))(),
                     reads=[f"pb{b1}"], writes=["qk_sb_b"])
                P.op("dve", (lambda b1=b1, tv=tv: lambda e: e.tensor_copy(out=tv[:, 0:128], in_=bank(b1)[:, 128:256]))(),
                     reads=[f"pb{b1}"], writes=[tvn + "a"])
                P.op("dve", (lambda b2=b2, tv=tv: lambda e: e.tensor_copy(out=tv[:, 128:640], in_=bank(b2)))(),
                     reads=[f"pb{b2}"], writes=[tvn + "b"])
                P.op("act", (lambda b3=b3, tv=tv: lambda e: e.activation(out=tv[:, 640:1152], in_=bank(b3), func=(AF.Copy if 's' in KSKIP else AF.Sigmoid)))(),
                     reads=[f"pb{b3}"], writes=[tvn + "c"])
                dma("sp", tmv_d[tok0:tok0 + 128, :], tv, reads=[tvn + "a", tvn + "b", tvn + "c"], writes=tres("tmv", tok0, 128))
                if 'q' not in KSKIP:
                    QKS = ["qk_sb_a", "qk_sb_b"]
                    P.op("dve", lambda e: e.tensor_tensor(out=sqq, in0=qk_sb, in1=qk_sb, op=ALU.mult), reads=QKS, writes=["sqq"])
                    P.op("dve", lambda e: e.reduce_sum(out=ss, in_=sqq.rearrange("p (h d) -> p h d", h=10), axis=AX.X),
                         reads=["sqq"], writes=["ss"])
                    P.op("act", lambda e: e.activation(out=ss, in_=ss, func=AF.Sqrt, scale=1.0 / 64, bias=cst[:, 0:1]),
                         reads=["ss"] + CST, writes=["ss"])
                    P.op("dve", lambda e: e.reciprocal(out=ss, in_=ss), reads=["ss"], writes=["ss"])
                    P.op("pool", lambda e: e.tensor_tensor(out=qn.rearrange("p (h d) -> p h d", h=10),
                                                           in0=qk_sb.rearrange("p (h d) -> p h d", h=10),
                                                           in1=ss.unsqueeze(2).to_broadcast([128, 10, 64]), op=ALU.mult),
                         reads=QKS + ["ss"], writes=["qn"])
                    P.op("pool", lambda e: e.tensor_tensor(out=qn, in0=qn, in1=qkg, op=ALU.mult), reads=["qn", "qkg"], writes=["qn"])
                    if r == 0:
                        gi = (tok0 - TC) // 128
                        q4 = qn.rearrange("p (h i two) -> p h i two", h=10, two=2)
                        o4 = qr.rearrange("p (h i two) -> p h i two", h=10, two=2)
                        x0, x1 = q4[:, :, :, 0], q4[:, :, :, 1]
                        cb = cosT[:, gi, :].unsqueeze(1).to_broadcast([128, 10, 32])
                        sb_ = sinT[:, gi, :].unsqueeze(1).to_broadcast([128, 10, 32])
                        ra3 = ra.rearrange("p (h i) -> p h i", h=10)
                        rb3 = rb.rearrange("p (h i) -> p h i", h=10)
                        P.op("pool", (lambda x0=x0, cb=cb: lambda e: e.tensor_tensor(out=ra3, in0=x0, in1=cb, op=ALU.mult))(),
                             reads=["qn", "cosT"], writes=["ra"])
                        P.op("pool", (lambda x1=x1, sb_=sb_: lambda e: e.tensor_tensor(out=rb3, in0=x1, in1=sb_, op=ALU.mult))(),
                             reads=["qn", "sinT"], writes=["rb"])
                        P.op("pool", (lambda o4=o4: lambda e: e.tensor_tensor(out=o4[:, :, :, 0], in0=ra3, in1=rb3, op=ALU.subtract))(),
                             reads=["ra", "rb"], writes=["qr0"])
                        P.op("pool", (lambda x0=x0, sb_=sb_: lambda e: e.tensor_tensor(out=ra3, in0=x0, in1=sb_, op=ALU.mult))(),
                             reads=["qn", "sinT", "qr0"], writes=["ra"])
                        P.op("pool", (lambda x1=x1, cb=cb: lambda e: e.tensor_tensor(out=rb3, in0=x1, in1=cb, op=ALU.mult))(),
                             reads=["qn", "cosT", "qr0"], writes=["rb"])
                        P.op("pool", (lambda o4=o4: lambda e: e.tensor_tensor(out=o4[:, :, :, 1], in0=ra3, in1=rb3, op=ALU.add))(),
                             reads=["ra", "rb"], writes=["qr1"])
                        QR = ["qr0", "qr1"]
                    else:
                        P.op("pool", lambda e: e.tensor_copy(out=qr, in_=qn), reads=["qn"], writes=["qr0"])
                        QR = ["qr0"]
                    tb = bank_bf(1)
                    for jq in range(5):
                        P.op("pe", (lambda jq=jq: lambda e: e.transpose(out=tb[:, jq * 128:(jq + 1) * 128],
                                                                        in_=qr[:, jq * 128:(jq + 1) * 128], identity=identB))(),
                             reads=QR + ["identB"], writes=["pb1"])
                    P.op("dve", (lambda ms=ms: lambda e: e.tensor_copy(out=QTst[:, :, ms],
                                                                       in_=tb[:, 0:512].rearrange("p (a b) -> p a b", a=4)))(),
                         reads=["pb1"], writes=["QTst"])
                    P.op("act", (lambda ms=ms: lambda e: e.activation(out=KTst[:, ms], in_=tb[:, 512:640], func=AF.Copy))(),
                         reads=["pb1"], writes=["KTst"])
            dma("sp", QT_d[:, :, t0:t0 + n].rearrange("j p t -> p j t"), QTst[:, :, 0:n], reads=["QTst"], writes=tres("QT", t0, n))
            dma("sp", KT_d[:, t0:t0 + n], KTst[:, 0:n], reads=["KTst"], writes=tres("KT", t0, n))
            if 'f' not in KSKIP:
                for cj in range(8):
                    b = rot.next()
                    for k in range(8):
                        P.op("pe", (lambda b=b, k=k, cj=cj: lambda e: e.matmul(
                            bank(b)[:, 0:n], lhsT=w_in[:, k, 768 + cj * 128:768 + (cj + 1) * 128], rhs=hT[:, k, 0:n],
                            start=(k == 0), stop=(k == 7)))(), reads=["hT", "w_in768"], writes=[f"pb{b}"])
                    if cj % 2 == 0:
                        P.op("act", (lambda b=b, cj=cj: lambda e: e.activation(out=fmst[:, cj, 0:n], in_=bank(b)[:, 0:n], func=AF.Copy))(),
                             reads=[f"pb{b}"], writes=[f"fmst{cj}"])
                    else:
                        P.op("dve", (lambda b=b, cj=cj: lambda e: e.tensor_copy(out=fmst[:, cj, 0:n], in_=bank(b)[:, 0:n]))(),
                             reads=[f"pb{b}"], writes=[f"fmst{cj}"])
                dma("sp", mqkT_d[:, :, t0:t0 + n].rearrange("j p t -> p j t"), fmst[:, :, 0:n],
                    reads=[f"fmst{cj}" for cj in range(8)], writes=tres("mqkT", t0, n))
                b = rot.next()
                for k in range(8):
                    P.op("pe", (lambda b=b, k=k: lambda e: e.matmul(bank(b)[:, 0:n], lhsT=w_in[:, k, 2816:2944], rhs=hT[:, k, 0:n],
                                                                  start=(k == 0), stop=(k == 7)))(),
                         reads=["hT", "w_in2816"], writes=[f"pb{b}"])
                P.op("act", (lambda b=b: lambda e: e.activation(out=gst[:, 0:n], in_=bank(b)[0:16, 0:n], func=AF.Identity,
                                                                bias=bgc[:, L:L + 1], scale=1.0))(),
                     reads=[f"pb{b}", "bgc"], writes=["gst"])
                dma("sp", gT_d[:, t0:t0 + n], gst[:, 0:n], reads=["gst"], writes=tres("gT", t0, n))
        for bi, (t0, n) in enumerate(blocks):
            do_block(bi, t0, n)
        P.barrier()

    WOFF = ARENA - 131072

    def w4_views():
        A = Alloc(WOFF)
        return A.get([8, FFN], BF16), A.get([32, D], BF16)

    def phase_attn(L, emit_ctx, prefetch):
        A = Alloc(PBASE)
        KTs = A.get([T], BF16)
        Vaug = A.get([NCH, 2, 128], BF16)
        QA = [[A.get([4, 512], BF16), A.get([4, 512], BF16)] for _ in range(2)]
        PT = [A.get([512], BF16) for _ in range(4)]
        rec = A.get([512], F32)
        attst = [A.get([4, 512], BF16), A.get([4, 512], BF16)]
        assert A.off <= WOFF
        if prefetch:
            w1, w2 = w4_views()
            for k in range(8):
                dma("pool", w1[:, k, :], w1_d[L, k * 128:(k + 1) * 128, :], writes=[f"w1_{k}"])
            for k4 in range(8):
                dma("pool", w2[:, k4 * 4:(k4 + 1) * 4, :],
                    w2_d[L, k4 * 512:(k4 + 1) * 512, :].rearrange("(k p) c -> p k c", p=128), writes=[f"w2_{k4}"])
        dma("sp", KTs, KT_d, writes=["KTs"])
        for bq in range(2):
            P.op("pool", (lambda bq=bq: lambda e: e.memset(QA[bq][0][64:128], 0.0))(), writes=[f"QAz{bq}0"])
            P.op("pool", (lambda bq=bq: lambda e: e.memset(QA[bq][1][0:64], 0.0))(), writes=[f"QAz{bq}1"])
        P.op("pool", lambda e: e.memset(Vaug, 1.0), writes=["Vaug"])
        vsrc = tmv_d.rearrange("(c p) f -> p c f", p=128)
        dma("sp", Vaug[:, :, 0, 0:64], vsrc[:, :, 0:64], reads=["Vaug"], writes=["Vaug0"])
        dma("sp", Vaug[:, :, 1, 64:128], vsrc[:, :, 64:128], reads=["Vaug"], writes=["Vaug1"])
        srot = Rot([0, 1, 2])
        orot = Rot([3, 4])
        prot = Rot([0, 1, 2, 3])
        qblocks = [(256 + 512 * i, 512, list(range(NCH))) for i in range(8)]
        if emit_ctx:
            qblocks = [(0, 256, [0, 1])] + qblocks
        def do_q(qi, t0, n, kbs):
            qa, qbn = QA[qi % 2], f"QA{qi % 2}"
            ast, astn = attst[qi % 2], f"attst{qi % 2}"
            dma("sp", qa[0][0:64, :, 0:n], QT_d[:, 0:64, t0:t0 + n].rearrange("j p t -> p j t"), reads=[f"QAz{qi % 2}0"], writes=[qbn + "h0"])
            dma("sp", qa[1][64:128, :, 0:n], QT_d[:, 64:128, t0:t0 + n].rearrange("j p t -> p j t"), reads=[f"QAz{qi % 2}1"], writes=[qbn + "h1"])
            for j in range(4):
                for hh in range(2):
                    hs = slice(hh * 64, hh * 64 + 64)
                    os_ = slice((1 - hh) * 64, (1 - hh) * 64 + 64)
                    ob = orot.next()
                    for ki, kb in enumerate(kbs):
                        sbk = srot.next()
                        pi = prot.next()
                        pt = PT[pi]
                        P.op("pe", (lambda sbk=sbk, kb=kb, hh=hh, j=j, qa=qa: lambda e: e.matmul(
                            bank(sbk)[:, 0:n], lhsT=KTs[:, kb * 128:(kb + 1) * 128], rhs=qa[hh][:, j, 0:n], start=True, stop=True))(),
                            reads=["KTs", qbn + f"h{hh}"], writes=[f"pb{sbk}"])
                        P.op("act", (lambda sbk=sbk, pt=pt: lambda e: e.activation(out=pt[:, 0:n], in_=bank(sbk)[:, 0:n], func=AF.Exp,
                                                                                   scale=0.125))(),
                             reads=[f"pb{sbk}"], writes=[f"PT{pi}"])
                        P.op("pe", (lambda ob=ob, kb=kb, hh=hh, pt=pt, ki=ki: lambda e: e.matmul(
                            bank(ob)[:, 0:n], lhsT=Vaug[:, kb, hh, :], rhs=pt[:, 0:n], start=(ki == 0), stop=(ki == len(kbs) - 1)))(),
                            reads=["Vaug0", "Vaug1", f"PT{pi}"], writes=[f"pb{ob}"])
                    P.op("dve", (lambda ob=ob, hs=hs, os_=os_: lambda e: e.reciprocal(out=rec[hs, 0:n], in_=bank(ob)[os_, 0:n]))(),
                         reads=[f"pb{ob}"], writes=["rec"])
                    P.op("dve", (lambda ob=ob, hs=hs, j=j, ast=ast: lambda e: e.tensor_tensor(
                        out=ast[hs, j, 0:n], in0=bank(ob)[hs, 0:n], in1=rec[hs, 0:n], op=ALU.mult))(),
                        reads=[f"pb{ob}", "rec"], writes=[astn])
            dma("sp", attT_d[0:4, :, t0:t0 + n].rearrange("j p t -> p j t"), ast[:, :, 0:n], reads=[astn],
                writes=tres("attT_a", t0, n))
        for qi, (t0, n, kbs) in enumerate(qblocks):
            do_q(qi, t0, n, kbs)
        P.barrier()

    def phase_mlstm(L, emit_ctx):
        ATOP = Alloc(PBASE)
        Rr128 = [ATOP.get([NCH, 2, 128], BF16), ATOP.get([NCH, 2, 128], BF16)]
        Rr = [Rr128[0][0:4], Rr128[1][0:4]]
        emtT = [ATOP.get([NCH, 4], F32), ATOP.get([NCH, 4], F32)]
        decbc = [ATOP.get([4, NCH], F32), ATOP.get([4, NCH], F32)]
        PB2 = ATOP.off
        A = Alloc(PB2)
        Gi = [A.get([T], F32, parts=4), A.get([T], F32, parts=4)]
        Gf = [A.get([T], F32, parts=4), A.get([T], F32, parts=4)]
        zer = A.get([T], F32, parts=4)
        Nb = A.get([T], F32, parts=4)
        Ut = A.get([T], F32, parts=4)
        tmp128 = A.get([T], F32)
        tmp = tmp128[0:4]
        Uend = A.get([NCH], F32, parts=4)
        Uprev = A.get([NCH], F32, parts=4)
        dec128 = A.get([NCH], F32)
        dec = dec128[0:4]
        dma("sp", Gi[0], gT_d[0:4, :], writes=["Gi0"])
        dma("sp", Gf[0], gT_d[4:8, :], writes=["Gf0"])
        dma("sp", Gi[1], gT_d[8:12, :], writes=["Gi1"])
        dma("sp", Gf[1], gT_d[12:16, :], writes=["Gf1"])
        P.op("pool", lambda e: e.memset(zer, 0.0), writes=["zer"])
        P.op("pool", lambda e: e.memset(tmp128, 0.0), writes=["tmp"])
        P.op("pool", lambda e: e.memset(dec128, 0.0), writes=["dec"])
        P.op("pool", lambda e: e.memset(Rr128[0], 0.0), writes=["Rr0a", "Rr0b"])
        P.op("pool", lambda e: e.memset(Rr128[1], 0.0), writes=["Rr1a", "Rr1b"])
        c1 = cst[0:4, 1:2]
        for d in range(2):
            gi, gf = Gi[d], Gf[d]
            gin, gfn = f"Gi{d}", f"Gf{d}"
            P.op("act", (lambda gf=gf: lambda e: e.activation(out=gf, in_=gf, func=AF.Exp, scale=-1.0))(), reads=[gfn], writes=[gfn])
            P.op("act", (lambda gf=gf: lambda e: e.activation(out=gf, in_=gf, func=AF.Ln, bias=c1, scale=1.0))(),
                 reads=[gfn] + CST, writes=[gfn])

            def seg(ap, lo, hi, d=d):
                v = ap[:, lo:hi]
                return v[:, ::-1] if d == 1 else v
            if d == 0:
                P.op("dve", (lambda gf=gf: lambda e: e.tensor_tensor_scan(out=Nb, data0=gf, data1=zer, initial=0.0,
                                                                          op0=ALU.add, op1=ALU.add))(),
                     reads=[gfn, "zer"], writes=["Nb"])
            else:
                P.op("dve", (lambda gf=gf: lambda e: e.tensor_tensor_scan(out=seg(Nb, 0, TC), data0=seg(gf, 0, TC), data1=zer[:, 0:TC],
                                                                          initial=0.0, op0=ALU.add, op1=ALU.add))(),
                     reads=[gfn, "zer"], writes=["Nb"])
                P.op("dve", (lambda gf=gf: lambda e: e.tensor_tensor_scan(out=seg(Nb, TC, T), data0=seg(gf, TC, T), data1=zer[:, TC:T],
                                                                          initial=Nb[:, 0:1], op0=ALU.add, op1=ALU.add))(),
                     reads=[gfn, "zer", "Nb"], writes=["Nb"])
            P.op("dve", (lambda gi=gi: lambda e: e.tensor_tensor(out=gi, in0=gi, in1=Nb, op=ALU.add))(), reads=[gin, "Nb"], writes=[gin])
            if d == 0:
                P.op("dve", (lambda gi=gi: lambda e: e.tensor_tensor_scan(out=Ut, data0=gi, data1=zer, initial=0.0,
                                                                          op0=ALU.max, op1=ALU.max))(),
                     reads=[gin, "zer"], writes=["Ut"])
            else:
                P.op("dve", (lambda gi=gi: lambda e: e.tensor_tensor_scan(out=seg(Ut, 0, TC), data0=seg(gi, 0, TC), data1=zer[:, 0:TC],
                                                                          initial=0.0, op0=ALU.max, op1=ALU.max))(),
                     reads=[gin, "zer"], writes=["Ut"])
                P.op("dve", (lambda gi=gi: lambda e: e.tensor_tensor_scan(out=seg(Ut, TC, T), data0=seg(gi, TC, T), data1=zer[:, TC:T],
                                                                          initial=Ut[:, 0:1], op0=ALU.max, op1=ALU.max))(),
                     reads=[gin, "zer", "Ut"], writes=["Ut"])
            U3 = Ut.rearrange("p (c t) -> p c t", t=128)
            endcol = 127 if d == 0 else 0
            P.op("dve", (lambda endcol=endcol: lambda e: e.tensor_copy(out=Uend, in_=U3[:, :, endcol]))(), reads=["Ut"], writes=["Uend"])
            P.op("pool", lambda e: e.memset(Uprev, 0.0), reads=[], writes=["Uprev"])
            if d == 0:
                P.op("dve", lambda e: e.tensor_copy(out=Uprev[:, 1:NCH], in_=Uend[:, 0:NCH - 1]), reads=["Uend", "Uprev"], writes=["Uprev"])
            else:
                P.op("dve", lambda e: e.tensor_copy(out=Uprev[:, 2:NCH - 1], in_=Uend[:, 3:NCH]), reads=["Uend", "Uprev"], writes=["Uprev"])
                P.op("dve", lambda e: e.tensor_copy(out=Uprev[:, NCH - 1:NCH], in_=Uend[:, 0:1]), reads=["Uend", "Uprev"], writes=["Uprev"])
                P.op("dve", lambda e: e.tensor_copy(out=Uprev[:, 0:1], in_=Uend[:, 1:2]), reads=["Uend", "Uprev"], writes=["Uprev"])
            upb = Uprev.unsqueeze(2).to_broadcast([4, NCH, 128])
            t3 = tmp.rearrange("p (c t) -> p c t", t=128)
            g3 = gi.rearrange("p (c t) -> p c t", t=128)
            Rd = Rr[d]
            P.op("dve", (lambda g3=g3: lambda e: e.tensor_tensor(out=t3, in0=g3, in1=upb, op=ALU.subtract))(),
                 reads=[gin, "Uprev"], writes=["tmp"])
            P.op("act", (lambda Rd=Rd: lambda e: e.activation(out=Rd[:, :, 0, :], in_=t3, func=AF.Exp, bias=cst[0:4, 2:3], scale=1.0))(),
                 reads=["tmp"] + CST, writes=[f"Rr{d}a"])
            P.op("dve", lambda e: e.tensor_tensor(out=t3, in0=upb, in1=U3, op=ALU.subtract), reads=["Ut", "Uprev", f"Rr{d}a"], writes=["tmp"])
            P.op("act", (lambda Rd=Rd: lambda e: e.activation(out=Rd[:, :, 1, :], in_=t3, func=AF.Exp))(),
                 reads=["tmp"], writes=[f"Rr{d}b"])
            P.op("dve", lambda e: e.tensor_tensor(out=tmp, in0=Nb, in1=Ut, op=ALU.subtract), reads=["Nb", "Ut", f"Rr{d}b"], writes=["tmp"])
            P.op("act", lambda e: e.activation(out=tmp, in_=tmp, func=AF.Exp), reads=["tmp"], writes=["tmp"])
            eb = 6 + d
            for c in range(NCH):
                P.op("pe", (lambda c=c, eb=eb: lambda e: e.matmul(bank(eb)[:, c * 4:c * 4 + 4], lhsT=tmp128[:, c * 128:(c + 1) * 128],
                                                                  rhs=identF[:, 0:4], start=True, stop=True))(),
                     reads=["tmp", "identF"], writes=[f"pb{eb}"])
            P.op("dve", (lambda d=d, eb=eb: lambda e: e.tensor_copy(out=emtT[d], in_=bank(eb)[:, 0:NCH * 4].rearrange("p (c h) -> p c h", h=4)))(),
                 reads=[f"pb{eb}"], writes=[f"emtT{d}"])
            P.op("dve", lambda e: e.tensor_tensor(out=dec, in0=Uprev, in1=Uend, op=ALU.subtract), reads=["Uprev", "Uend"], writes=["dec"])
            P.op("act", lambda e: e.activation(out=dec, in_=dec, func=AF.Exp), reads=["dec"], writes=["dec"])
            db = 4 + d
            for h in range(4):
                P.op("pe", (lambda h=h, db=db: lambda e: e.matmul(bank(db)[:, h * NCH:(h + 1) * NCH], lhsT=selF[:, h, :], rhs=dec128,
                                                                  start=True, stop=True))(),
                     reads=["dec", "selF"], writes=[f"pb{db}"])
            P.op("dve", (lambda d=d, db=db: lambda e: e.tensor_copy(out=decbc[d], in_=bank(db)[:, 0:4 * NCH].rearrange("p (h c) -> p h c", h=4)))(),
                 reads=[f"pb{db}"], writes=[f"decbc{d}"])
        P.barrier()

        A = Alloc(PB2)
        qkT = A.get([8, T], BF16)
        vaug = A.get([NCH, 4, 129], BF16)
        Cst = A.get([8, 129], F32)
        Cbf = A.get([8, 129], BF16)
        PB3 = A.off
        A2 = Alloc(PB3)
        PL = 2048
        xc = [A2.get([PL + 2], F32), A2.get([PL + 2], F32)]
        acc = A2.get([PL], F32)
        P.op("pool", lambda e: e.memset(vaug, 1.0), writes=["vaug"])
        for h in range(4):
            dma("sp", vaug[:, :, h, 0:128], tmv_d[:, 128 + h * 128:256 + h * 128].rearrange("(c p) d -> p c d", p=128),
                reads=["vaug"], writes=[f"vaugv{h}"])
        P.op("pool", lambda e: e.memset(Cst, 0.0), writes=["Cst"])
        P.op("pool", lambda e: e.memset(Cbf, 0.0), writes=["Cbf"])
        pieces = [(0, TC, 0, TC), (TC, TC + PL, TC, T), (TC + PL, T, TC, T)]
        it = 0
        for cr in range(8):
            w = [convc[:, (L * 3 + a) * 8 + cr:(L * 3 + a) * 8 + cr + 1] for a in range(3)]
            for (a_, b_, sa, sb) in pieces:
                ln_ = b_ - a_
                lh = a_ > sa
                rh = b_ < sb
                x_, xn_ = xc[it % 2], f"xc{it % 2}"
                it += 1
                lo = a_ - 1 if lh else a_
                hi = b_ + 1 if rh else b_
                c0 = 0 if lh else 1
                dma("sp", x_[:, c0:c0 + hi - lo], mqkT_d[cr, :, lo:hi], writes=[xn_])
                P.op("pool", (lambda x_=x_, w=w, ln_=ln_: lambda e: e.tensor_scalar_mul(out=acc[:, 0:ln_], in0=x_[:, 1:ln_ + 1], scalar1=w[1]))(),
                     reads=[xn_], writes=["acc"])
                o0 = 0 if lh else 1
                P.op("dve", (lambda x_=x_, w=w, o0=o0, ln_=ln_: lambda e: e.scalar_tensor_tensor(
                    out=acc[:, o0:ln_], in0=x_[:, o0:ln_], scalar=w[0], in1=acc[:, o0:ln_], op0=ALU.mult, op1=ALU.add))(),
                    reads=[xn_, "acc"], writes=["acc"])
                o1 = ln_ if rh else ln_ - 1
                P.op("dve", (lambda x_=x_, w=w, o1=o1: lambda e: e.scalar_tensor_tensor(
                    out=acc[:, 0:o1], in0=x_[:, 2:o1 + 2], scalar=w[2], in1=acc[:, 0:o1], op0=ALU.mult, op1=ALU.add))(),
                    reads=[xn_, "acc"], writes=["acc"])
                P.op("act", (lambda cr=cr, a_=a_, b_=b_, ln_=ln_: lambda e: e.activation(out=qkT[:, cr, a_:b_], in_=acc[:, 0:ln_], func=AF.Silu))(),
                     reads=["acc"], writes=[f"qkT{cr}"])
        P.barrier()

        A3 = Alloc(PB3)
        NR = 8
        kTs = [A3.get([128], BF16) for _ in range(NR)]
        qTs = [A3.get([128], BF16) for _ in range(NR)]
        Sw = [A3.get([128], BF16) for _ in range(NR)]
        k2 = [A3.get([128], BF16) for _ in range(NR)]
        dmx = [A3.get([2], F32) for _ in range(NR)]
        hst = [[A3.get([4, 128], F32) for _ in range(2)] for _ in range(2)]
        prot = Rot(list(range(8)))
        order = [list(range(NCH)), [1, 0] + list(range(NCH - 1, 1, -1))]
        step = 0
        hcnt = [0, 0]
        for i in range(NCH):
            for d in range(2):
                c = order[d][i]
                emit_out = emit_ctx or c >= 2
                hs_, hsn = hst[d][hcnt[d] % 2], f"hst{d}{hcnt[d] % 2}"
                for h in range(4):
                    ri = step % NR
                    step += 1
                    idx = d * 4 + h
                    ts = slice(c * 128, (c + 1) * 128)
                    bb, sb_, tb_, nb_, db_ = prot.next(), prot.next(), prot.next(), prot.next(), prot.next()
                    Rd = Rr128[d]
                    P.op("pe", (lambda bb=bb, h=h, c=c, Rd=Rd: lambda e: e.matmul(
                        bank(bb)[:, 0:256], lhsT=selB[:, h, :], rhs=Rd[:, c, :, :].rearrange("p a b -> p (a b)"),
                        start=True, stop=True))(), reads=[], writes=[f"pb{bb}"])
                    P.op("dve", (lambda bb=bb, h=h, ts=ts, ri=ri: lambda e: e.tensor_tensor(
                        out=kTs[ri], in0=qkT[:, 4 + h, ts], in1=bank(bb)[:, 0:128], op=ALU.mult))(),
                        reads=[f"pb{bb}"], writes=[f"kTs{ri}"])
                    P.op("dve", (lambda bb=bb, h=h, ts=ts, ri=ri: lambda e: e.tensor_tensor(
                        out=qTs[ri], in0=qkT[:, h, ts], in1=bank(bb)[:, 128:256], op=ALU.mult))(),
                        reads=[f"pb{bb}"], writes=[f"qTs{ri}"])
                    P.op("pe", (lambda sb_=sb_, ri=ri: lambda e: e.matmul(bank(sb_)[:, 0:128], lhsT=kTs[ri], rhs=qTs[ri], start=True, stop=True))(),
                         reads=[f"kTs{ri}", f"qTs{ri}"], writes=[f"pb{sb_}"])
                    P.op("dve", (lambda sb_=sb_, ri=ri, d=d: lambda e: e.tensor_tensor(
                        out=Sw[ri], in0=bank(sb_)[:, 0:128], in1=maskT[d], op=ALU.mult))(),
                        reads=[f"pb{sb_}"], writes=[f"Sw{ri}"])
                    P.op("pe", (lambda tb_=tb_, ri=ri: lambda e: e.transpose(out=bank_bf(tb_)[:, 0:128], in_=kTs[ri], identity=identB))(),
                         reads=[f"kTs{ri}"], writes=[f"pb{tb_}"])
                    P.op("act", (lambda tb_=tb_, ri=ri, d=d, h=h, c=c: lambda e: e.activation(
                        out=k2[ri], in_=bank_bf(tb_)[:, 0:128], func=AF.Identity, scale=decbc[d][:, h, c:c + 1]))(),
                        reads=[f"pb{tb_}"], writes=[f"k2{ri}"])
                    if emit_out:
                        P.op("pe", (lambda nb_=nb_, ri=ri, idx=idx: lambda e: e.matmul(
                            bank(nb_)[:, 0:129], lhsT=qTs[ri], rhs=Cbf[:, idx, :], start=True, stop=False))(),
                            reads=[f"qTs{ri}", f"Cbf{idx}"], writes=[f"pb{nb_}"])
                        P.op("pe", (lambda nb_=nb_, ri=ri, c=c, h=h: lambda e: e.matmul(
                            bank(nb_)[:, 0:129], lhsT=Sw[ri], rhs=vaug[:, c, h, :], start=False, stop=True))(),
                            reads=[f"Sw{ri}"], writes=[f"pb{nb_}"])
                        P.op("act", (lambda nb_=nb_, ri=ri: lambda e: e.activation(
                            out=dmx[ri][:, 0:1], in_=bank(nb_)[:, 128:129], func=AF.Abs))(), reads=[f"pb{nb_}"], writes=[f"dmx{ri}"])
                        P.op("dve", (lambda ri=ri, d=d, c=c, h=h: lambda e: e.tensor_tensor(
                            out=dmx[ri][:, 0:1], in0=dmx[ri][:, 0:1], in1=emtT[d][:, c, h:h + 1], op=ALU.max))(),
                            reads=[f"dmx{ri}"], writes=[f"dmx{ri}"])
                        P.op("dve", (lambda ri=ri: lambda e: e.reciprocal(out=dmx[ri][:, 1:2], in_=dmx[ri][:, 0:1]))(),
                             reads=[f"dmx{ri}"], writes=[f"dmr{ri}"])
                        P.op("act", (lambda nb_=nb_, ri=ri, h=h, hs_=hs_: lambda e: e.activation(
                            out=hs_[:, h, :], in_=bank(nb_)[:, 0:128], func=AF.Identity, scale=dmx[ri][:, 1:2]))(),
                            reads=[f"pb{nb_}", f"dmr{ri}"], writes=[hsn + str(h)])
                    P.op("pe", (lambda db_=db_, ri=ri, c=c, h=h: lambda e: e.matmul(
                        bank(db_)[:, 0:129], lhsT=k2[ri], rhs=vaug[:, c, h, :], start=True, stop=True))(),
                        reads=[f"k2{ri}"], writes=[f"pb{db_}"])
                    P.op("dve", (lambda db_=db_, idx=idx, d=d, h=h, c=c: lambda e: e.scalar_tensor_tensor(
                        out=Cst[:, idx, :], in0=Cst[:, idx, :], scalar=decbc[d][:, h, c:c + 1], in1=bank(db_)[:, 0:129],
                        op0=ALU.mult, op1=ALU.add))(), reads=[f"pb{db_}", f"Cst{idx}"], writes=[f"Cst{idx}"])
                    P.op("pool", (lambda idx=idx: lambda e: e.tensor_copy(out=Cbf[:, idx, :], in_=Cst[:, idx, :]))(),
                         reads=[f"Cst{idx}"], writes=[f"Cbf{idx}"])
                if emit_out:
                    hd = hf_d if d == 0 else hb_d
                    dma("sp", hd[c * 128:(c + 1) * 128, :], hs_.rearrange("p h d -> p (h d)"),
                        reads=[hsn + str(h) for h in range(4)], writes=tres("hfb%d" % d, c * 128, 128))
                    hcnt[d] += 1
        P.barrier()

        A4 = Alloc(PB2)
        mlg = A4.get([512], F32)
        hfa = [A4.get([512], F32), A4.get([512], F32)]
        hba = [A4.get([512], F32), A4.get([512], F32)]
        mo = [A4.get([512], BF16), A4.get([512], BF16)]
        sq4 = A4.get([512], F32)
        s4 = A4.get([4], F32)
        memb = A4.get([512], BF16)
        memst = [A4.get([4, 128], BF16), A4.get([4, 128], BF16)]
        dma("sp", mlg, mlg_d[:, L, :], writes=["mlg"])
        trot = Rot([0, 1, 2, 3])
        gs = list(range(0 if emit_ctx else 2, NCH))
        for gi_, g in enumerate(gs):
            p2 = gi_ % 2
            ha, hbv, mov, mst = hfa[p2], hba[p2], mo[p2], memst[p2]
            rows = slice(g * 128, (g + 1) * 128)
            dma("sp", ha, hf_d[rows, :], writes=[f"hfa{p2}"])
            dma("sp", hbv, hb_d[rows, :], writes=[f"hba{p2}"])
            dma("sp", mov, tmv_d[rows, 640:1152], writes=[f"mo{p2}"])
            P.op("dve", (lambda ha=ha, hbv=hbv: lambda e: e.tensor_tensor(out=ha, in0=ha, in1=hbv, op=ALU.add))(),
                 reads=[f"hfa{p2}", f"hba{p2}"], writes=[f"hfa{p2}"])
            P.op("pool", (lambda ha=ha: lambda e: e.tensor_tensor(out=sq4, in0=ha, in1=ha, op=ALU.mult))(), reads=[f"hfa{p2}"], writes=["sq4"])
            P.op("dve", lambda e: e.reduce_sum(out=s4, in_=sq4.rearrange("p (h d) -> p h d", h=4), axis=AX.X), reads=["sq4"], writes=["s4"])
            P.op("act", lambda e: e.activation(out=s4, in_=s4, func=AF.Sqrt, scale=1.0 / 128, bias=cst[:, 0:1]), reads=["s4"] + CST, writes=["s4"])
            P.op("dve", lambda e: e.reciprocal(out=s4, in_=s4), reads=["s4"], writes=["s4"])
            P.op("dve", (lambda ha=ha: lambda e: e.tensor_tensor(out=ha.rearrange("p (h d) -> p h d", h=4),
                                                                in0=ha.rearrange("p (h d) -> p h d", h=4),
                                                                in1=s4.unsqueeze(2).to_broadcast([128, 4, 128]), op=ALU.mult))(),
                 reads=[f"hfa{p2}", "s4"], writes=[f"hfa{p2}"])
            P.op("pool", (lambda ha=ha: lambda e: e.tensor_tensor(out=ha, in0=ha, in1=mlg, op=ALU.mult))(), reads=[f"hfa{p2}", "mlg"], writes=[f"hfa{p2}"])
            P.op("pool", (lambda ha=ha, mov=mov: lambda e: e.tensor_tensor(out=memb, in0=ha, in1=mov, op=ALU.mult))(),
                 reads=[f"hfa{p2}", f"mo{p2}"], writes=["memb"])
            tb = trot.next()
            for h in range(4):
                P.op("pe", (lambda tb=tb, h=h: lambda e: e.transpose(out=bank_bf(tb)[:, h * 128:(h + 1) * 128],
                                                                    in_=memb[:, h * 128:(h + 1) * 128], identity=identB))(),
                     reads=["memb", "identB"], writes=[f"pb{tb}"])
            P.op("act", (lambda tb=tb, mst=mst: lambda e: e.activation(out=mst, in_=bank_bf(tb)[:, 0:512].rearrange("p (a b) -> p a b", a=4),
                                                                       func=AF.Copy))(), reads=[f"pb{tb}"], writes=[f"memst{p2}"])
            dma("sp", attT_d[4:8, :, g * 128:(g + 1) * 128].rearrange("j p t -> p j t"), mst, reads=[f"memst{p2}"],
                writes=tres("attT_m", g * 128, 128))
        P.barrier()

    def phase_mlp(L, emit_ctx, prefetched):
        w1, w2 = w4_views()
        if not prefetched:
            for k in range(8):
                dma("pool", w1[:, k, :], w1_d[L, k * 128:(k + 1) * 128, :], writes=[f"w1_{k}"])
            for k4 in range(8):
                dma("pool", w2[:, k4 * 4:(k4 + 1) * 4, :],
                    w2_d[L, k4 * 512:(k4 + 1) * 512, :].rearrange("(k p) c -> p k c", p=128), writes=[f"w2_{k4}"])
        W1 = [f"w1_{k}" for k in range(8)]
        W2 = [f"w2_{k}" for k in range(8)]
        A = Alloc(PBASE)
        NB = 256
        w_out = A.get([8, D], BF16)
        dma("pool", w_out, w_out_d[L].rearrange("(k p) c -> p k c", p=128), writes=["w_out"])
        attTb = [A.get([8, NB], BF16)]
        xTb = [A.get([8, NB], F32), A.get([8, NB], F32)]
        h2T = A.get([8, NB], BF16)
        sq = [A.get([NB], BF16), A.get([NB], BF16)]
        rstd = A.get([NB], F32)
        xn = [A.get([NB], F32), A.get([NB], F32)]
        rl = [A.get([NB], BF16), A.get([NB], BF16)]
        uT = A.get([32, NB], BF16)
        assert A.off <= WOFF, A.off
        rot = Rot([1, 2, 3, 4, 5, 6, 7])
        t0s = list(range(0 if emit_ctx else TC, T, NB))
        def do_block(bi, t0):
            n = NB
            r = 1 if t0 < TC else 0
            ab, abn = attTb[0], "attTb0"
            xb, xbn = xTb[bi % 2], f"xTb{bi % 2}"
            dma("sp", ab, attT_d[:, :, t0:t0 + n].rearrange("j p t -> p j t"), writes=[abn])
            dma("sp", xb, xT_d[:, :, t0:t0 + n].rearrange("j p t -> p j t"), writes=[xbn])
            for cj in range(8):
                b = rot.next()
                for k in range(8):
                    P.op("pe", (lambda b=b, k=k, cj=cj, ab=ab: lambda e: e.matmul(
                        bank(b)[:, 0:n], lhsT=w_out[:, k, cj * 128:(cj + 1) * 128], rhs=ab[:, k, :], start=(k == 0), stop=(k == 7)))(),
                        reads=["w_out", abn], writes=[f"pb{b}"])
                P.op("dve", (lambda b=b, cj=cj, xb=xb, r=r: lambda e: e.scalar_tensor_tensor(
                    out=xb[:, cj, :], in0=bank(b)[:, 0:n], scalar=modcol(L, 16 + cj, r), in1=xb[:, cj, :], op0=ALU.mult, op1=ALU.add))(),
                    reads=[f"pb{b}", xbn, "mod"], writes=[xbn])
            norm_mod(xb, xbn, n, sq, rstd, xn, h2T, "h2T",
                     lambda j: s2[:, L * 8 + j, r:r + 1], lambda j: modcol(L, 24 + j, r), 0)
            for fc in range(32):
                b = rot.next()
                for k in range(8):
                    P.op("pe", (lambda b=b, k=k, fc=fc: lambda e: e.matmul(
                        bank(b)[:, 0:n], lhsT=w1[:, k, fc * 128:(fc + 1) * 128], rhs=h2T[:, k, :], start=(k == 0), stop=(k == 7)))(),
                        reads=[f"w1_{k}", "h2T"], writes=[f"pb{b}"])
                r_, rn_ = rl[fc % 2], f"rl{fc % 2}"
                P.op("act", (lambda b=b, r_=r_: lambda e: e.activation(out=r_, in_=bank(b)[:, 0:n], func=AF.Relu))(),
                     reads=[f"pb{b}"], writes=[rn_])
                P.op("pool", (lambda r_=r_, fc=fc: lambda e: e.tensor_tensor(out=uT[:, fc, :], in0=r_, in1=r_, op=ALU.mult))(),
                     reads=[rn_], writes=[f"uT{fc}"])
            for cj in range(8):
                b = rot.next()
                for fc in range(32):
                    P.op("pe", (lambda b=b, fc=fc, cj=cj: lambda e: e.matmul(
                        bank(b)[:, 0:n], lhsT=w2[:, fc, cj * 128:(cj + 1) * 128], rhs=uT[:, fc, :], start=(fc == 0), stop=(fc == 31)))(),
                        reads=[f"w2_{fc // 4}", f"uT{fc}"], writes=[f"pb{b}"])
                P.op("dve", (lambda b=b, cj=cj, xb=xb, r=r: lambda e: e.scalar_tensor_tensor(
                    out=xb[:, cj, :], in0=bank(b)[:, 0:n], scalar=modcol(L, 40 + cj, r), in1=xb[:, cj, :], op0=ALU.mult, op1=ALU.add))(),
                    reads=[f"pb{b}", xbn, "mod"], writes=[xbn])
            dma("sp", xT_d[:, :, t0:t0 + n].rearrange("j p t -> p j t"), xb, reads=[xbn], writes=tres("xTo", t0, n))
        for bi, t0 in enumerate(t0s):
            do_block(bi, t0)
        P.barrier()

    def phase_final():
        A = Alloc(PBASE)
        xTb = [A.get([8, 512], F32), A.get([8, 512], F32)]
        sq = [A.get([512], BF16), A.get([512], BF16)]
        rstd = A.get([512], F32)
        xnf = A.get([8, 512], F32)
        ost = [A.get([1024], F32), A.get([1024], F32)]
        rot = Rot([1, 2, 3, 4, 5, 6])
        oc = 0
        def do_block(bi):
            nonlocal oc
            t0 = TC + bi * 512
            n = 512
            xb, xbn = xTb[bi % 2], f"xTb{bi % 2}"
            dma("sp", xb, xT_d[:, :, t0:t0 + n].rearrange("j p t -> p j t"), writes=[xbn])
            ssps = bank(0)
            for j in range(8):
                s_, sn = sq[j % 2], f"sq{j % 2}"
                P.op("act", (lambda s_=s_, j=j, xb=xb: lambda e: e.activation(out=s_, in_=xb[:, j, :], func=AF.Square))(), reads=[xbn], writes=[sn])
                P.op("pe", (lambda s_=s_, j=j: lambda e: e.matmul(ssps, lhsT=onesB, rhs=s_, start=(j == 0), stop=(j == 7)))(),
                     reads=[sn, "onesB"], writes=["pb0"])
            P.op("act", lambda e: e.activation(out=rstd, in_=ssps, func=AF.Sqrt, scale=1.0 / D, bias=cst[:, 0:1]), reads=["pb0"] + CST, writes=["rstd"])
            P.op("dve", lambda e: e.reciprocal(out=rstd, in_=rstd), reads=["rstd"], writes=["rstd"])
            for j in range(8):
                P.op("dve", (lambda j=j, xb=xb: lambda e: e.scalar_tensor_tensor(
                    out=xnf[:, j, :], in0=xb[:, j, :], scalar=nfin[:, j:j + 1], in1=rstd, op0=ALU.mult, op1=ALU.mult))(),
                    reads=[xbn, "rstd", "nfin"], writes=[f"xnf{j}"])
            for m in range(4):
                o_, on_ = ost[oc % 2], f"ost{oc % 2}"
                oc += 1
                for half in range(2):
                    b = rot.next()
                    for jj in range(4):
                        j = half * 4 + jj
                        P.op("pe", (lambda b=b, jj=jj, j=j, m=m: lambda e: e.transpose(
                            out=bank(b)[:, jj * 128:(jj + 1) * 128], in_=xnf[:, j, m * 128:(m + 1) * 128], identity=identF))(),
                            reads=[f"xnf{j}", "identF"], writes=[f"pb{b}"])
                    if half == 0:
                        P.op("act", (lambda b=b, o_=o_: lambda e: e.activation(out=o_[:, 0:512], in_=bank(b), func=AF.Copy))(),
                             reads=[f"pb{b}"], writes=[on_ + "a"])
                    else:
                        P.op("dve", (lambda b=b, o_=o_: lambda e: e.tensor_copy(out=o_[:, 512:1024], in_=bank(b)))(),
                             reads=[f"pb{b}"], writes=[on_ + "b"])
                r0 = bi * 512 + m * 128
                dma("sp", out_d[r0:r0 + 128, :], o_, reads=[on_ + "a", on_ + "b"], writes=[f"out{r0}"])

        for bi in range(8):
            do_block(bi)

    steps = [("mod", phase_mod), ("xT", phase_xT)]
    for L in range(n_layers):
        emit_ctx = L < DEPTH - 1
        steps.append((f"inproj{L}", (lambda L=L: phase_inproj(L))))
        steps.append((f"mlstm{L}", (lambda L=L, ec=emit_ctx: phase_mlstm(L, ec))))
        steps.append((f"attn{L}", (lambda L=L, ec=emit_ctx: phase_attn(L, ec, prefetch=True))))
        steps.append((f"mlp{L}", (lambda L=L, ec=emit_ctx: phase_mlp(L, ec, prefetched=True))))
    steps.append(("final", phase_final))
    for name, fn in steps:
        fn()
        if stop is not None and name == stop:
            break
    P.emit(es)
    es.close()
    return nc, P


def _perm_att():
    idx = np.zeros(512, dtype=np.int64)
    for j in range(4):
        for p in range(128):
            head = j if p < 64 else j + 4
            idx[j * 128 + p] = head * 64 + (p % 64)
    return idx


def _rope_tables():
    rows = TL // 64
    row_idx = np.repeat(np.arange(rows, dtype=np.float32), 64)
    col_idx = np.tile(np.arange(64, dtype=np.float32), rows)
    inv_freq = np.power(np.float32(10000.0), -np.arange(0, 32, 2, dtype=np.float32) / np.float32(32)).astype(np.float32)
    ang = np.concatenate([row_idx[:, None] * inv_freq, col_idx[:, None] * inv_freq], axis=-1).astype(np.float32)
    cos = np.cos(ang).astype(np.float32).reshape(32, 128, 32).transpose(1, 0, 2)
    sin = np.sin(ang).astype(np.float32).reshape(32, 128, 32).transpose(1, 0, 2)
    return np.ascontiguousarray(cos), np.ascontiguousarray(sin)


def host_inputs(inputs, cores):
    f = lambda a: np.ascontiguousarray(np.asarray(a, dtype=np.float32))
    perm = _perm_att()
    w_in = f(inputs["w_in"])
    w_in_p = np.zeros((DEPTH, D, INCP), np.float32)
    w_in_p[:, :, 0:INC] = w_in
    w_in_p[:, :, 0:512] = w_in[:, :, perm]
    w_out = f(inputs["w_out"])
    w_out_p = w_out.copy()
    w_out_p[:, 0:512, :] = w_out[:, perm, :]
    cos, sin = _rope_tables()
    sel = np.zeros((128, 4, 128), np.float32)
    for h in range(4):
        sel[h, h, :] = 1.0
    s_ = np.arange(128)
    maskf = (s_[:, None] <= s_[None, :]).astype(np.float32)
    maskb = (s_[:, None] >= s_[None, :]).astype(np.float32)
    colL = lambda a, nj: np.ascontiguousarray(f(a).reshape(DEPTH, nj, 128).transpose(2, 0, 1))
    qk_gain = np.concatenate([np.tile(f(inputs["q_norm"]), (1, 8)), np.tile(f(inputs["k_norm"]), (1, 2))], axis=1)
    shared = {
        "w_ada": f(inputs["w_ada"]),
        "b_ada_c": colL(inputs["b_ada"], 48),
        "nmix_c": colL(inputs["norm_mix"], 8),
        "nmlp_c": colL(inputs["norm_mlp"], 8),
        "w_in_p": w_in_p,
        "bg_c": np.ascontiguousarray(f(inputs["b_gates"]).T),
        "conv_c": np.ascontiguousarray(f(inputs["conv_qk"]).reshape(DEPTH, 3, 8, 128).transpose(3, 0, 1, 2)),
        "qk_gain": np.ascontiguousarray(np.broadcast_to(qk_gain[None], (128, DEPTH, 640))),
        "ml_gain": np.ascontiguousarray(np.broadcast_to(f(inputs["mlstm_norm"])[None], (128, DEPTH, 512))),
        "w_out_p": w_out_p,
        "w_mlp_in": f(inputs["w_mlp_in"]),
        "w_mlp_out": f(inputs["w_mlp_out"]),
        "nfin_c": np.ascontiguousarray(f(inputs["norm_final"]).reshape(8, 128).T),
        "ident": np.eye(128, dtype=np.float32),
        "maskf": maskf,
        "maskb": maskb,
        "sel4": sel,
        "rope_cos": cos,
        "rope_sin": sin,
    }
    x = f(inputs["x"])
    ctx = f(inputs["ctx"])
    c = f(inputs["c"])
    c_ctx = f(inputs["c_ctx"])
    maps = []
    for b in cores:
        cc = np.stack([c[b].reshape(8, 128).T, c_ctx.reshape(8, 128).T], axis=-1)
        m = dict(shared)
        m["x"] = x[b]
        m["ctx"] = ctx[b]
        m["cc"] = np.ascontiguousarray(cc)
        maps.append(m)
    return maps


_NC_CACHE = {}


def kernel(**inputs):
    if "nc" not in _NC_CACHE:
        _NC_CACHE["nc"] = build()[0]
    nc = _NC_CACHE["nc"]
    maps = host_inputs(inputs, list(range(8)))
    res = run_bass_kernel_spmd(nc, maps, core_ids=list(range(8)))
    out = np.stack([np.asarray(r["out"], dtype=np.float32) for r in res.results], axis=0)
    return out
```

```python
import math
import os
KSKIP = os.environ.get('KSKIP', '')
from contextlib import ExitStack
import numpy as np
import concourse.bass as bass
import concourse.mybir as mybir
from concourse.bass_utils import run_bass_kernel_spmd

F32 = mybir.dt.float32
BF16 = mybir.dt.bfloat16
ALU = mybir.AluOpType
AF = mybir.ActivationFunctionType
AX = mybir.AxisListType

DEPTH = 4
D = 1024
TC = 256
TL = 4096
T = TC + TL
NCH = T // 128
INC = 2832
INCP = 2944
FFN = 4096
EPS = 1e-6
ARENA = 210944

CH = 2048
RROT = 8
NDMA = {"sp": 28, "pool": 20}
CENG = ("pe", "act", "dve", "pool")
ENGS = ("pe", "act", "dve", "pool", "sp")


class Res:
    __slots__ = ("writers", "readers")

    def __init__(self):
        self.writers = []
        self.readers = []


class Rec:
    __slots__ = ("eng", "fn", "idx", "cwaits", "dwaits", "signal", "is_dma", "dma_id", "sig_k")

    def __init__(self, eng, fn, is_dma):
        self.eng = eng
        self.fn = fn
        self.is_dma = is_dma
        self.cwaits = {}
        self.dwaits = {}
        self.signal = False
        self.dma_id = None
        self.sig_k = None


class Prog:
    def __init__(self, nc):
        self.nc = nc
        self.streams = {e: [] for e in ENGS}
        self.maxw = {e: {} for e in ENGS}
        self.dw = {e: set() for e in ENGS}
        self.ndma = {q: 0 for q in NDMA}
        self.res = {}
        self.last_real = {e: None for e in CENG}
        self.dma_live = {q: {} for q in NDMA}

    def R(self, name):
        r = self.res.get(name)
        if r is None:
            r = Res()
            self.res[name] = r
        return r

    def _need(self, rec, tok, same_ok):
        kind, e, i, prod = tok
        if kind == "c":
            if e == rec.eng and (same_ok or e == "pe"):
                return
            if self.maxw[rec.eng].get(e, -1) >= i:
                return
            if rec.cwaits.get(e, (-1, None))[0] < i:
                rec.cwaits[e] = (i, prod)
        else:
            key = (e, i)
            if key in self.dw[rec.eng]:
                return
            rec.dwaits[key] = prod

    def _commit(self, rec):
        for e, (i, prod) in rec.cwaits.items():
            self.maxw[rec.eng][e] = i
            prod.signal = True
        for key, prod in rec.dwaits.items():
            self.dw[rec.eng].add(key)

    def op(self, eng, fn, reads=(), writes=(), is_dma=False):
        rec = Rec(eng, fn, is_dma)
        st = self.streams[eng]
        rec.idx = len(st)
        psn = sorted({n for n in list(reads) + list(writes) if n.startswith("pb") or n == "modps"})
        reads = [self.R(r) for r in reads if r not in psn]
        writes = [self.R(r) for r in writes if r not in psn]
        psr = [self.R(n) for n in psn]
        for r in psr:
            for tok in r.writers:
                self._need(rec, tok, True)
        for r in reads:
            for tok in r.writers:
                self._need(rec, tok, False)
        for w in writes:
            for tok in w.writers:
                self._need(rec, tok, True)
            for tok in w.readers:
                self._need(rec, tok, True)
        self._commit(rec)
        if is_dma:
            rec.dma_id = self.ndma[eng]
            self.ndma[eng] += 1
            self.dma_live[eng][rec.dma_id % NDMA[eng]] = rec
            tok = ("d", eng, rec.dma_id, rec)
        else:
            tok = ("c", eng, rec.idx, rec)
            self.last_real[eng] = rec
        for r in reads:
            r.readers.append(tok)
        for w in writes:
            w.writers = [tok]
            w.readers = []
        for r in psr:
            r.writers = [tok]
            r.readers = []
        st.append(rec)
        return rec

    def barrier(self):
        for eng in ENGS:
            rec = Rec(eng, None, False)
            rec.idx = len(self.streams[eng])
            for e in CENG:
                lr = self.last_real[e]
                if lr is not None:
                    self._need(rec, ("c", e, lr.idx, lr), True)
            for q in NDMA:
                for slot, d in self.dma_live[q].items():
                    self._need(rec, ("d", q, d.dma_id, d), True)
            self._commit(rec)
            self.streams[eng].append(rec)
        self.res = {}

    def emit(self, es):
        nc = self.nc
        sems = {e: [es.enter_context(nc.semaphore(f"c_{e}_{i}")) for i in range(RROT)] for e in CENG}
        dsems = {q: [es.enter_context(nc.semaphore(f"d_{q}_{i}")) for i in range(n)] for q, n in NDMA.items()}
        for e in ENGS:
            k = 0
            for rec in self.streams[e]:
                if rec.is_dma or rec.fn is None:
                    continue
                if rec.signal:
                    rec.sig_k = k
                    k += 1

        def csem(e, k):
            return sems[e][(k // CH) % RROT], (k // (CH * RROT)) * CH + (k % CH) + 1

        def dsem(q, i):
            n = NDMA[q]
            return dsems[q][i % n], 16 * (i // n + 1)

        block = es.enter_context(nc.Block())

        def run(ename):
            def body(eng):
                for rec in self.streams[ename]:
                    for e, (i, prod) in rec.cwaits.items():
                        s, v = csem(e, prod.sig_k)
                        eng.wait_ge(s, v)
                    for (q, i) in rec.dwaits:
                        s, v = dsem(q, i)
                        eng.wait_ge(s, v)
                    if rec.fn is None:
                        continue
                    if rec.is_dma:
                        n = NDMA[ename]
                        if rec.dma_id >= n:
                            s, v = dsem(ename, rec.dma_id - n)
                            eng.wait_ge(s, v)
                    ins = rec.fn(eng)
                    if rec.is_dma:
                        s, v = dsem(ename, rec.dma_id)
                        ins.then_inc(s, 16)
                    elif rec.signal:
                        s, v = csem(ename, rec.sig_k)
                        ins.then_inc(s, 1)
                if ename in NDMA and self.ndma[ename] > 0:
                    n = NDMA[ename]
                    tot = self.ndma[ename]
                    for slot in range(min(n, tot)):
                        last = ((tot - 1 - slot) // n) * n + slot
                        s, v = dsem(ename, last)
                        eng.wait_ge(s, v)
            return body

        block.sync(run("sp"))
        block.tensor(run("pe"))
        block.scalar(run("act"))
        block.vector(run("dve"))
        block.gpsimd(run("pool"))


def tres(name, t0, n):
    return [f"{name}:{i}" for i in range(t0 // 128, (t0 + n + 127) // 128)]


def build(n_layers=DEPTH, debug=False, stop=None):
    nc = bass.Bass("TRN2", target_bir_lowering=False)

    def din(name, shape, dt=F32):
        return nc.dram_tensor(name, list(shape), dt, kind="ExternalInput").ap()

    def dscr(name, shape, dt):
        return nc.dram_tensor(name, list(shape), dt, kind="ExternalOutput" if debug else "Internal").ap()

    x_d = din("x", [TL, D])
    ctx_d = din("ctx", [TC, D])
    cc_d = din("cc", [128, 8, 2])
    w_ada_d = din("w_ada", [DEPTH, D, 6 * D])
    b_ada_d = din("b_ada_c", [128, DEPTH, 48])
    nmix_d = din("nmix_c", [128, DEPTH, 8])
    nmlp_d = din("nmlp_c", [128, DEPTH, 8])
    w_in_d = din("w_in_p", [DEPTH, D, INCP])
    bg_d = din("bg_c", [16, DEPTH])
    conv_d = din("conv_c", [128, DEPTH, 3, 8])
    qkg_d = din("qk_gain", [128, DEPTH, 640])
    mlg_d = din("ml_gain", [128, DEPTH, 512])
    w_out_d = din("w_out_p", [DEPTH, D, D])
    w1_d = din("w_mlp_in", [DEPTH, D, FFN])
    w2_d = din("w_mlp_out", [DEPTH, FFN, D])
    nfin_d = din("nfin_c", [128, 8])
    ident_d = din("ident", [128, 128])
    maskf_d = din("maskf", [128, 128])
    maskb_d = din("maskb", [128, 128])
    sel_d = din("sel4", [128, 4, 128])
    cos_d = din("rope_cos", [128, 32, 32])
    sin_d = din("rope_sin", [128, 32, 32])
    out_d = nc.dram_tensor("out", [TL, D], F32, kind="ExternalOutput").ap()

    xT_d = dscr("xT", [8, 128, T], F32)
    mqkT_d = dscr("mqkT", [8, 128, T], F32)
    gT_d = dscr("gT", [16, T], F32)
    QT_d = dscr("QT", [4, 128, T], BF16)
    KT_d = dscr("KT", [128, T], BF16)
    tmv_d = dscr("tmv", [T, 1152], BF16)
    attT_d = dscr("attT", [8, 128, T], BF16)
    hf_d = dscr("hf", [T, 512], F32)
    hb_d = dscr("hb", [T, 512], F32)

    P = Prog(nc)
    es = ExitStack()
    arena = es.enter_context(nc.sbuf_tensor("arena", [128, ARENA // 4], F32))
    psum = es.enter_context(nc.psum_tensor("psum", [128, 4096], F32))

    class Alloc:
        def __init__(self, base=0):
            self.off = base

        def get(self, shape, dt, parts=128):
            n = 1
            for s in shape:
                n *= s
            es_ = 4 if dt == F32 else 2
            nb = (n * es_ + 3) // 4 * 4
            w0 = self.off // 4
            ap = arena[0:parts, w0:w0 + nb // 4]
            self.off += nb
            assert self.off <= ARENA, f"arena overflow {self.off}"
            if dt == BF16:
                ap = ap.bitcast(BF16)
            if len(shape) == 2:
                ap = ap.rearrange("p (a b) -> p a b", a=shape[0])
            elif len(shape) == 3:
                ap = ap.rearrange("p (a b c) -> p a b c", a=shape[0], b=shape[1])
            return ap

    def bank(i):
        return psum[:, i * 512:(i + 1) * 512]

    def bank_bf(i):
        return psum[:, i * 512:(i + 1) * 512].bitcast(BF16)

    class Rot:
        def __init__(self, items):
            self.items = items
            self.i = 0

        def next(self):
            it = self.items[self.i % len(self.items)]
            self.i += 1
            return it

    def dma(q, out, in_, reads=(), writes=()):
        P.op(q, lambda e: e.dma_start(out=out, in_=in_), reads=reads, writes=writes, is_dma=True)

    A0 = Alloc(0)
    identF = A0.get([128], F32)
    identB = A0.get([128], BF16)
    onesB = A0.get([128], BF16)
    maskT = [A0.get([128], F32), A0.get([128], F32)]
    selF = A0.get([4, 128], F32)
    selB = A0.get([4, 128], BF16)
    cst = A0.get([4], F32)
    cc = A0.get([8, 2], F32)
    sc = A0.get([8, 2], F32)
    mod = A0.get([DEPTH * 48, 2], F32)
    s1 = A0.get([DEPTH * 8, 2], F32)
    s2 = A0.get([DEPTH * 8, 2], F32)
    nmix = A0.get([DEPTH * 8], F32)
    nmlp = A0.get([DEPTH * 8], F32)
    bada = A0.get([DEPTH * 48], F32)
    bgc = A0.get([DEPTH], F32, parts=16)
    convc = A0.get([DEPTH * 3 * 8], F32)
    nfin = A0.get([8], F32)
    PBASE = A0.off

    def modcol(L, j, r):
        return mod[:, L * 48 + j, r:r + 1]

    dma("sp", identF, ident_d, writes=["identF"])
    dma("sp", maskT[0], maskf_d, writes=["maskf"])
    dma("sp", maskT[1], maskb_d, writes=["maskb"])
    dma("sp", selF, sel_d, writes=["selF"])
    dma("sp", cc, cc_d, writes=["cc"])
    dma("sp", nmix, nmix_d.rearrange("p l j -> p (l j)"), writes=["nmix"])
    dma("sp", nmlp, nmlp_d.rearrange("p l j -> p (l j)"), writes=["nmlp"])
    dma("sp", bada, b_ada_d.rearrange("p l j -> p (l j)"), writes=["bada"])
    dma("sp", bgc, bg_d, writes=["bgc"])
    dma("sp", convc, conv_d.rearrange("p l a c -> p (l a c)"), writes=["convc"])
    dma("sp", nfin, nfin_d, writes=["nfin"])
    P.op("act", lambda e: e.activation(out=identB, in_=identF, func=AF.Copy), reads=["identF"], writes=["identB"])
    P.op("pool", lambda e: e.memset(onesB, 1.0), writes=["onesB"])
    P.op("act", lambda e: e.activation(out=selB, in_=selF, func=AF.Copy), reads=["selF"], writes=["selB"])
    P.op("pool", lambda e: e.memset(cst[:, 0:1], EPS), writes=["cst0"])
    P.op("pool", lambda e: e.memset(cst[:, 1:2], 1.0), writes=["cst1"])
    P.op("pool", lambda e: e.memset(cst[:, 2:3], -0.5 * math.log(128.0)), writes=["cst2"])
    P.op("pool", lambda e: e.memset(cst[:, 3:4], 0.0), writes=["cst3"])
    CST = ["cst0", "cst1", "cst2", "cst3"]
    P.op("act", lambda e: e.activation(out=sc, in_=cc, func=AF.Silu), reads=["cc"], writes=["sc"])

    def phase_mod():
        A = Alloc(PBASE)
        wa = [A.get([8, 512], F32), A.get([8, 512], F32)]
        modps = bank(0)
        it = 0
        for L in range(n_layers):
            wsrc = w_ada_d[L].rearrange("(k p) c -> p k c", p=128)
            for pc in range(12):
                buf = wa[it % 2]
                bn = f"wa{it % 2}"
                it += 1
                dma("sp", buf, wsrc[:, :, pc * 512:(pc + 1) * 512], writes=[bn])
                for cjj in range(4):
                    j = pc * 4 + cjj
                    col = (L * 48 + j) * 2
                    for k in range(8):
                        P.op("pe", (lambda buf=buf, k=k, cjj=cjj, col=col: lambda e: e.matmul(
                            modps[:, col:col + 2], lhsT=buf[:, k, cjj * 128:(cjj + 1) * 128], rhs=sc[:, k, :],
                            start=(k == 0), stop=(k == 7)))(),
                            reads=[bn, "sc"], writes=["modps"])
        nl = n_layers * 48
        P.op("dve", lambda e: e.tensor_tensor(
            out=mod[:, 0:nl, :], in0=modps[:, 0:nl * 2].rearrange("p (a b) -> p a b", b=2),
            in1=bada[:, 0:nl].unsqueeze(2).to_broadcast([128, nl, 2]), op=ALU.add),
            reads=["modps", "bada"], writes=["mod"])
        for L in range(n_layers):
            for (dst, nm, nmn, jo) in ((s1, nmix, "nmix", 8), (s2, nmlp, "nmlp", 32)):
                P.op("dve", (lambda dst=dst, L=L, jo=jo: lambda e: e.tensor_scalar_add(
                    out=dst[:, L * 8:(L + 1) * 8, :], in0=mod[:, L * 48 + jo:L * 48 + jo + 8, :], scalar1=1.0))(),
                    reads=["mod"], writes=["s12"])
                P.op("dve", (lambda dst=dst, L=L, nm=nm: lambda e: e.tensor_tensor(
                    out=dst[:, L * 8:(L + 1) * 8, :], in0=dst[:, L * 8:(L + 1) * 8, :],
                    in1=nm[:, L * 8:(L + 1) * 8].unsqueeze(2).to_broadcast([128, 8, 2]), op=ALU.mult))(),
                    reads=["s12", nmn], writes=["s12"])
        P.barrier()

    def phase_xT():
        A = Alloc(PBASE)
        xin = [A.get([1024], F32), A.get([1024], F32)]
        xst = [A.get([8, 128], F32), A.get([8, 128], F32)]
        rot = Rot([1, 2, 3, 4])
        for g in range(NCH):
            src = ctx_d[g * 128:(g + 1) * 128, :] if g < 2 else x_d[(g - 2) * 128:(g - 1) * 128, :]
            xb, xn_ = xin[g % 2], f"xin{g % 2}"
            st, stn = xst[g % 2], f"xst{g % 2}"
            dma("sp", xb, src, writes=[xn_])
            for half in range(2):
                b = rot.next()
                bk = bank(b)
                for jj in range(4):
                    j = half * 4 + jj
                    P.op("pe", (lambda bk=bk, jj=jj, j=j, xb=xb: lambda e: e.transpose(
                        out=bk[:, jj * 128:(jj + 1) * 128], in_=xb[:, j * 128:(j + 1) * 128], identity=identF))(),
                        reads=[xn_, "identF"], writes=[f"pb{b}"])
                eng = "act" if half == 0 else "dve"
                if eng == "act":
                    P.op("act", (lambda bk=bk, st=st, half=half: lambda e: e.activation(
                        out=st[:, half * 4:half * 4 + 4, :], in_=bk.rearrange("p (a b) -> p a b", a=4), func=AF.Copy))(),
                        reads=[f"pb{b}"], writes=[stn])
                else:
                    P.op("dve", (lambda bk=bk, st=st, half=half: lambda e: e.tensor_copy(
                        out=st[:, half * 4:half * 4 + 4, :], in_=bk.rearrange("p (a b) -> p a b", a=4)))(),
                        reads=[f"pb{b}"], writes=[stn])
            dma("sp", xT_d[:, :, g * 128:(g + 1) * 128].rearrange("j p t -> p j t"), st, reads=[stn],
                writes=tres("xT", g * 128, 128))
        P.barrier()

    def norm_mod(xTb, xname, n, sq, rstd, xn, hT, hname, scol, bcol, ssb):
        ssps = bank(ssb)
        for j in range(8):
            s_, sn = sq[j % 2], f"sq{j % 2}"
            P.op("act", (lambda s_=s_, j=j: lambda e: e.activation(out=s_[:, 0:n], in_=xTb[:, j, 0:n], func=AF.Square))(),
                 reads=[xname], writes=[sn])
            P.op("pe", (lambda s_=s_, j=j: lambda e: e.matmul(ssps[:, 0:n], lhsT=onesB, rhs=s_[:, 0:n],
                                                              start=(j == 0), stop=(j == 7)))(),
                 reads=[sn, "onesB"], writes=[f"pb{ssb}"])
        P.op("act", lambda e: e.activation(out=rstd[:, 0:n], in_=ssps[:, 0:n], func=AF.Sqrt, scale=1.0 / D, bias=cst[:, 0:1]),
             reads=[f"pb{ssb}"] + CST, writes=["rstd"])
        P.op("dve", lambda e: e.reciprocal(out=rstd[:, 0:n], in_=rstd[:, 0:n]), reads=["rstd"], writes=["rstd"])
        for j in range(8):
            x_, xn_ = xn[j % 2], f"xn{j % 2}"
            P.op("dve", (lambda x_=x_, j=j: lambda e: e.tensor_tensor(out=x_[:, 0:n], in0=xTb[:, j, 0:n], in1=rstd[:, 0:n],
                                                                      op=ALU.mult))(),
                 reads=[xname, "rstd"], writes=[xn_])
            P.op("act", (lambda x_=x_, j=j: lambda e: e.activation(out=hT[:, j, 0:n], in_=x_[:, 0:n], func=AF.Identity,
                                                                   scale=scol(j), bias=bcol(j)))(),
                 reads=[xn_, "s12", "mod"], writes=[hname])

    def phase_inproj(L):
        A = Alloc(PBASE)
        w_in = A.get([8, INCP], BF16)
        qkg = A.get([640], F32)
        cosT = A.get([32, 32], F32)
        sinT = A.get([32, 32], F32)
        dma("sp", cosT, cos_d, writes=["cosT"])
        dma("sp", sinT, sin_d, writes=["sinT"])
        xTb = [A.get([8, 512], F32), A.get([8, 512], F32)]
        sq = [A.get([512], BF16), A.get([512], BF16)]
        rstd = A.get([512], F32)
        xn = [A.get([512], F32), A.get([512], F32)]
        hT = A.get([8, 512], BF16)
        qk_sb = A.get([640], F32)
        sqq = A.get([640], F32)
        ss = A.get([10], F32)
        qn = A.get([640], F32)
        ra = A.get([320], F32)
        rb = A.get([320], F32)
        qr = A.get([640], BF16)
        QTst = A.get([4, 512], BF16)
        KTst = A.get([512], BF16)
        tmvst = [A.get([1152], BF16), A.get([1152], BF16)]
        fmst = A.get([8, 512], F32)
        gst = A.get([512], F32, parts=16)

        wsrc = w_in_d[L].rearrange("(k p) c -> p k c", p=128)
        for (c0, c1) in ((0, 512), (512, 768), (768, 1792), (1792, 2304), (2304, 2816), (2816, 2944)):
            if 'w' not in KSKIP:
                dma("pool", w_in[:, :, c0:c1], wsrc[:, :, c0:c1], writes=[f"w_in{c0}"])
        dma("sp", qkg, qkg_d[:, L, :], writes=["qkg"])
        rot = Rot([2, 3, 4] if 'b' in KSKIP else [2, 3, 4, 5, 6, 7])
        blocks = [(0, 256)] + [(256 + 512 * i, 512) for i in range(8)]
        def do_block(bi, t0, n):
            r = 1 if t0 < TC else 0
            xb, xname = xTb[bi % 2], f"xTb{bi % 2}"
            dma("sp", xb[:, :, 0:n], xT_d[:, :, t0:t0 + n].rearrange("j p t -> p j t"),
                reads=tres("xT", t0, n), writes=[xname])
            if 'n' not in KSKIP:
                norm_mod(xb, xname, n, sq, rstd, xn, hT, "hT",
                         lambda j: s1[:, L * 8 + j, r:r + 1], lambda j: modcol(L, j, r), 0)
            for m in range(0 if 'm' in KSKIP else n // 128):
                tok0 = t0 + m * 128
                ms = slice(m * 128, (m + 1) * 128)
                tv, tvn = tmvst[m % 2], f"tmvst{m % 2}"
                groups = ((0, 512, "w_in0"), (512, 768, "w_in512"), (1792, 2304, "w_in1792"), (2304, 2816, "w_in2304"))
                pbs = []
                for (c0, c1, wn) in groups:
                    b = rot.next()
                    pbs.append(b)
                    for k in range(8):
                        P.op("pe", (lambda b=b, k=k, c0=c0, c1=c1, ms=ms: lambda e: e.matmul(
                            bank(b)[:, 0:c1 - c0], lhsT=hT[:, k, ms], rhs=w_in[:, k, c0:c1], start=(k == 0), stop=(k == 7)))(),
                            reads=["hT", wn], writes=[f"pb{b}"])
                b0, b1, b2, b3 = pbs
                P.op("act", (lambda b0=b0: lambda e: e.activation(out=qk_sb[:, 0:512], in_=bank(b0), func=AF.Copy))(),
                     reads=[f"pb{b0}"], writes=["qk_sb_a"])
                P.op("act", (lambda b1=b1: lambda e: e.activation(out=qk_sb[:, 512:640], in_=bank(b1)[:, 0:128], func=AF.Copy))(),
                     reads=[f"pb{b1}"], writes=["qk_sb_b"])
                P.op("dve", (lambda b1=b1, tv=tv: lambda e: e.tensor_copy(out=tv[:, 0:128], in_=bank(b1)[:, 128:256]))(),
                     reads=[f"pb{b1}"], writes=[tvn + "a"])
                P.op("dve", (lambda b2=b2, tv=tv: lambda e: e.tensor_copy(out=tv[:, 128:640], in_=bank(b2)))(),
                     reads=[f"pb{b2}"], writes=[tvn + "b"])
                P.op("act", (lambda b3=b3, tv=tv: lambda e: e.activation(out=tv[:, 640:1152], in_=bank(b3), func=(AF.Copy if 's' in KSKIP else AF.Sigmoid)))(),
                     reads=[f"pb{b3}"], writes=[tvn + "c"])
                dma("sp", tmv_d[tok0:tok0 + 128, :], tv, reads=[tvn + "a", tvn + "b", tvn + "c"], writes=tres("tmv", tok0, 128))
                if 'q' not in KSKIP:
                    QKS = ["qk_sb_a", "qk_sb_b"]
                    P.op("dve", lambda e: e.tensor_tensor(out=sqq, in0=qk_sb, in1=qk_sb, op=ALU.mult), reads=QKS, writes=["sqq"])
                    P.op("dve", lambda e: e.reduce_sum(out=ss, in_=sqq.rearrange("p (h d) -> p h d", h=10), axis=AX.X),
                         reads=["sqq"], writes=["ss"])
                    P.op("act", lambda e: e.activation(out=ss, in_=ss, func=AF.Sqrt, scale=1.0 / 64, bias=cst[:, 0:1]),
                         reads=["ss"] + CST, writes=["ss"])
                    P.op("dve", lambda e: e.reciprocal(out=ss, in_=ss), reads=["ss"], writes=["ss"])
                    P.op("pool", lambda e: e.tensor_tensor(out=qn.rearrange("p (h d) -> p h d", h=10),
                                                           in0=qk_sb.rearrange("p (h d) -> p h d", h=10),
                                                           in1=ss.unsqueeze(2).to_broadcast([128, 10, 64]), op=ALU.mult),
                         reads=QKS + ["ss"], writes=["qn"])
                    P.op("pool", lambda e: e.tensor_tensor(out=qn, in0=qn, in1=qkg, op=ALU.mult), reads=["qn", "qkg"], writes=["qn"])
                    if r == 0:
                        gi = (tok0 - TC) // 128
                        q4 = qn.rearrange("p (h i two) -> p h i two", h=10, two=2)
                        o4 = qr.rearrange("p (h i two) -> p h i two", h=10, two=2)
                        x0, x1 = q4[:, :, :, 0], q4[:, :, :, 1]
                        cb = cosT[:, gi, :].unsqueeze(1).to_broadcast([128, 10, 32])
                        sb_ = sinT[:, gi, :].unsqueeze(1).to_broadcast([128, 10, 32])
                        ra3 = ra.rearrange("p (h i) -> p h i", h=10)
                        rb3 = rb.rearrange("p (h i) -> p h i", h=10)
                        P.op("pool", (lambda x0=x0, cb=cb: lambda e: e.tensor_tensor(out=ra3, in0=x0, in1=cb, op=ALU.mult))(),
                             reads=["qn", "cosT"], writes=["ra"])
                        P.op("pool", (lambda x1=x1, sb_=sb_: lambda e: e.tensor_tensor(out=rb3, in0=x1, in1=sb_, op=ALU.mult))(),
                             reads=["qn", "sinT"], writes=["rb"])
                        P.op("pool", (lambda o4=o4: lambda e: e.tensor_tensor(out=o4[:, :, :, 0], in0=ra3, in1=rb3, op=ALU.subtract))(),
                             reads=["ra", "rb"], writes=["qr0"])
                        P.op("pool", (lambda x0=x0, sb_=sb_: lambda e: e.tensor_tensor(out=ra3, in0=x0, in1=sb_, op=ALU.mult))(),
                             reads=["qn", "sinT", "qr0"], writes=["ra"])
                        P.op("pool", (lambda x1=x1, cb=cb: lambda e: e.tensor_tensor(out=rb3, in0=x1, in1=cb, op=ALU.mult))(),
                             reads=["qn", "cosT", "qr0"], writes=["rb"])
                        P.op("pool", (lambda o4=o4: lambda e: e.tensor_tensor(out=o4[:, :, :, 1], in0=ra3, in1=rb3, op=ALU.add))(),
                             reads=["ra", "rb"], writes=["qr1"])
                        QR = ["qr0", "qr1"]
                    else:
                        P.op("pool", lambda e: e.tensor_copy(out=qr, in_=qn), reads=["qn"], writes=["qr0"])
                        QR = ["qr0"]
                    tb = bank_bf(1)
                    for jq in range(5):
                        P.op("pe", (lambda jq=jq: lambda e: e.transpose(out=tb[:, jq * 128:(jq + 1) * 128],
                                                                        in_=qr[:, jq * 128:(jq + 1) * 128], identity=identB))(),
                             reads=QR + ["identB"], writes=["pb1"])
                    P.op("dve", (lambda ms=ms: lambda e: e.tensor_copy(out=QTst[:, :, ms],
                                                                       in_=tb[:, 0:512].rearrange("p (a b) -> p a b", a=4)))(),
                         reads=["pb1"], writes=["QTst"])
                    P.op("act", (lambda ms=ms: lambda e: e.activation(out=KTst[:, ms], in_=tb[:, 512:640], func=AF.Copy))(),
                         reads=["pb1"], writes=["KTst"])
            dma("sp", QT_d[:, :, t0:t0 + n].rearrange("j p t -> p j t"), QTst[:, :, 0:n], reads=["QTst"], writes=tres("QT", t0, n))
            dma("sp", KT_d[:, t0:t0 + n], KTst[:, 0:n], reads=["KTst"], writes=tres("KT", t0, n))
            if 'f' not in KSKIP:
                for cj in range(8):
                    b = rot.next()
                    for k in range(8):
                        P.op("pe", (lambda b=b, k=k, cj=cj: lambda e: e.matmul(
                            bank(b)[:, 0:n], lhsT=w_in[:, k, 768 + cj * 128:768 + (cj + 1) * 128], rhs=hT[:, k, 0:n],
                            start=(k == 0), stop=(k == 7)))(), reads=["hT", "w_in768"], writes=[f"pb{b}"])
                    if cj % 2 == 0:
                        P.op("act", (lambda b=b, cj=cj: lambda e: e.activation(out=fmst[:, cj, 0:n], in_=bank(b)[:, 0:n], func=AF.Copy))(),
                             reads=[f"pb{b}"], writes=[f"fmst{cj}"])
                    else:
                        P.op("dve", (lambda b=b, cj=cj: lambda e: e.tensor_copy(out=fmst[:, cj, 0:n], in_=bank(b)[:, 0:n]))(),
                             reads=[f"pb{b}"], writes=[f"fmst{cj}"])
                dma("sp", mqkT_d[:, :, t0:t0 + n].rearrange("j p t -> p j t"), fmst[:, :, 0:n],
                    reads=[f"fmst{cj}" for cj in range(8)], writes=tres("mqkT", t0, n))
                b = rot.next()
                for k in range(8):
                    P.op("pe", (lambda b=b, k=k: lambda e: e.matmul(bank(b)[:, 0:n], lhsT=w_in[:, k, 2816:2944], rhs=hT[:, k, 0:n],
                                                                  start=(k == 0), stop=(k == 7)))(),
                         reads=["hT", "w_in2816"], writes=[f"pb{b}"])
                P.op("act", (lambda b=b: lambda e: e.activation(out=gst[:, 0:n], in_=bank(b)[0:16, 0:n], func=AF.Identity,
                                                                bias=bgc[:, L:L + 1], scale=1.0))(),
                     reads=[f"pb{b}", "bgc"], writes=["gst"])
                dma("sp", gT_d[:, t0:t0 + n], gst[:, 0:n], reads=["gst"], writes=tres("gT", t0, n))
        for bi, (t0, n) in enumerate(blocks):
            do_block(bi, t0, n)
        P.barrier()

    WOFF = ARENA - 131072

    def w4_views():
        A = Alloc(WOFF)
        return A.get([8, FFN], BF16), A.get([32, D], BF16)

    def phase_attn(L, emit_ctx, prefetch):
        A = Alloc(PBASE)
        KTs = A.get([T], BF16)
        Vaug = A.get([NCH, 2, 128], BF16)
        QA = [[A.get([4, 512], BF16), A.get([4, 512], BF16)] for _ in range(2)]
        PT = [A.get([512], BF16) for _ in range(4)]
        rec = A.get([512], F32)
        attst = [A.get([4, 512], BF16), A.get([4, 512], BF16)]
        assert A.off <= WOFF
        if prefetch:
            w1, w2 = w4_views()
            for k in range(8):
                dma("pool", w1[:, k, :], w1_d[L, k * 128:(k + 1) * 128, :], writes=[f"w1_{k}"])
            for k4 in range(8):
                dma("pool", w2[:, k4 * 4:(k4 + 1) * 4, :],
                    w2_d[L, k4 * 512:(k4 + 1) * 512, :].rearrange("(k p) c -> p k c", p=128), writes=[f"w2_{k4}"])
        dma("sp", KTs, KT_d, writes=["KTs"])
        for bq in range(2):
            P.op("pool", (lambda bq=bq: lambda e: e.memset(QA[bq][0][64:128], 0.0))(), writes=[f"QAz{bq}0"])
            P.op("pool", (lambda bq=bq: lambda e: e.memset(QA[bq][1][0:64], 0.0))(), writes=[f"QAz{bq}1"])
        P.op("pool", lambda e: e.memset(Vaug, 1.0), writes=["Vaug"])
        vsrc = tmv_d.rearrange("(c p) f -> p c f", p=128)
        dma("sp", Vaug[:, :, 0, 0:64], vsrc[:, :, 0:64], reads=["Vaug"], writes=["Vaug0"])
        dma("sp", Vaug[:, :, 1, 64:128], vsrc[:, :, 64:128], reads=["Vaug"], writes=["Vaug1"])
        srot = Rot([0, 1, 2, 3])
        orot = Rot([4, 5])
        prot = Rot([0, 1, 2, 3])
        qblocks = [(256 + 512 * i, 512, list(range(NCH))) for i in range(8)]
        if emit_ctx:
            qblocks = [(0, 256, [0, 1])] + qblocks
        def do_q(qi, t0, n, kbs):
            qa, qbn = QA[qi % 2], f"QA{qi % 2}"
            ast, astn = attst[qi % 2], f"attst{qi % 2}"
            dma("sp", qa[0][0:64, :, 0:n], QT_d[:, 0:64, t0:t0 + n].rearrange("j p t -> p j t"), reads=[f"QAz{qi % 2}0"], writes=[qbn + "h0"])
            dma("sp", qa[1][64:128, :, 0:n], QT_d[:, 64:128, t0:t0 + n].rearrange("j p t -> p j t"), reads=[f"QAz{qi % 2}1"], writes=[qbn + "h1"])
            SK = 2
            its = []
            for j in range(4):
                for hh in range(2):
                    ob = orot.next()
                    for ki, kb in enumerate(kbs):
                        its.append((j, hh, ob, ki, kb))
            nk = len(kbs)
            slots = {}

            def emit_S(i):
                j, hh, ob, ki, kb = its[i]
                sbk = srot.next()
                pi = prot.next()
                pt = PT[pi]
                slots[i] = (pi, pt)
                P.op("pe", (lambda sbk=sbk, kb=kb, hh=hh, j=j: lambda e: e.matmul(
                    bank(sbk)[:, 0:n], lhsT=KTs[:, kb * 128:(kb + 1) * 128], rhs=qa[hh][:, j, 0:n], start=True, stop=True))(),
                    reads=["KTs", qbn + f"h{hh}"], writes=[f"pb{sbk}"])
                P.op("act", (lambda sbk=sbk, pt=pt: lambda e: e.activation(out=pt[:, 0:n], in_=bank(sbk)[:, 0:n], func=AF.Exp,
                                                                           scale=0.125))(),
                     reads=[f"pb{sbk}"], writes=[f"PT{pi}"])

            def emit_PV(i):
                j, hh, ob, ki, kb = its[i]
                pi, pt = slots.pop(i)
                hs = slice(hh * 64, hh * 64 + 64)
                os_ = slice((1 - hh) * 64, (1 - hh) * 64 + 64)
                P.op("pe", (lambda ob=ob, kb=kb, hh=hh, pt=pt, ki=ki: lambda e: e.matmul(
                    bank(ob)[:, 0:n], lhsT=Vaug[:, kb, hh, :], rhs=pt[:, 0:n], start=(ki == 0), stop=(ki == nk - 1)))(),
                    reads=["Vaug0", "Vaug1", f"PT{pi}"], writes=[f"pb{ob}"])
                if ki == nk - 1:
                    P.op("dve", (lambda ob=ob, hs=hs, os_=os_: lambda e: e.reciprocal(out=rec[hs, 0:n], in_=bank(ob)[os_, 0:n]))(),
                         reads=[f"pb{ob}"], writes=["rec"])
                    P.op("dve", (lambda ob=ob, hs=hs, j=j: lambda e: e.tensor_tensor(
                        out=ast[hs, j, 0:n], in0=bank(ob)[hs, 0:n], in1=rec[hs, 0:n], op=ALU.mult))(),
                        reads=[f"pb{ob}", "rec"], writes=[astn])

            for i in range(len(its) + SK):
                if i < len(its):
                    emit_S(i)
                if i - SK >= 0:
                    emit_PV(i - SK)
            dma("sp", attT_d[0:4, :, t0:t0 + n].rearrange("j p t -> p j t"), ast[:, :, 0:n], reads=[astn],
                writes=tres("attT_a", t0, n))
        for qi, (t0, n, kbs) in enumerate(qblocks):
            do_q(qi, t0, n, kbs)
        P.barrier()

    def phase_mlstm(L, emit_ctx):
        ATOP = Alloc(PBASE)
        Rr128 = [ATOP.get([NCH, 2, 128], BF16), ATOP.get([NCH, 2, 128], BF16)]
        Rr = [Rr128[0][0:4], Rr128[1][0:4]]
        emtT = [ATOP.get([NCH, 4], F32), ATOP.get([NCH, 4], F32)]
        decbc = [ATOP.get([4, NCH], F32), ATOP.get([4, NCH], F32)]
        PB2 = ATOP.off
        A = Alloc(PB2)
        Gi = [A.get([T], F32, parts=4), A.get([T], F32, parts=4)]
        Gf = [A.get([T], F32, parts=4), A.get([T], F32, parts=4)]
        zer = A.get([T], F32, parts=4)
        Nb = A.get([T], F32, parts=4)
        Ut = A.get([T], F32, parts=4)
        tmp128 = A.get([T], F32)
        tmp = tmp128[0:4]
        Uend = A.get([NCH], F32, parts=4)
        Uprev = A.get([NCH], F32, parts=4)
        dec128 = A.get([NCH], F32)
        dec = dec128[0:4]
        dma("sp", Gi[0], gT_d[0:4, :], writes=["Gi0"])
        dma("sp", Gf[0], gT_d[4:8, :], writes=["Gf0"])
        dma("sp", Gi[1], gT_d[8:12, :], writes=["Gi1"])
        dma("sp", Gf[1], gT_d[12:16, :], writes=["Gf1"])
        P.op("pool", lambda e: e.memset(zer, 0.0), writes=["zer"])
        P.op("pool", lambda e: e.memset(tmp128, 0.0), writes=["tmp"])
        P.op("pool", lambda e: e.memset(dec128, 0.0), writes=["dec"])
        P.op("pool", lambda e: e.memset(Rr128[0], 0.0), writes=["Rr0a", "Rr0b"])
        P.op("pool", lambda e: e.memset(Rr128[1], 0.0), writes=["Rr1a", "Rr1b"])
        c1 = cst[0:4, 1:2]
        for d in range(2):
            gi, gf = Gi[d], Gf[d]
            gin, gfn = f"Gi{d}", f"Gf{d}"
            P.op("act", (lambda gf=gf: lambda e: e.activation(out=gf, in_=gf, func=AF.Exp, scale=-1.0))(), reads=[gfn], writes=[gfn])
            P.op("act", (lambda gf=gf: lambda e: e.activation(out=gf, in_=gf, func=AF.Ln, bias=c1, scale=1.0))(),
                 reads=[gfn] + CST, writes=[gfn])

            def seg(ap, lo, hi, d=d):
                v = ap[:, lo:hi]
                return v[:, ::-1] if d == 1 else v
            if d == 0:
                P.op("dve", (lambda gf=gf: lambda e: e.tensor_tensor_scan(out=Nb, data0=gf, data1=zer, initial=0.0,
                                                                          op0=ALU.add, op1=ALU.add))(),
                     reads=[gfn, "zer"], writes=["Nb"])
            else:
                P.op("dve", (lambda gf=gf: lambda e: e.tensor_tensor_scan(out=seg(Nb, 0, TC), data0=seg(gf, 0, TC), data1=zer[:, 0:TC],
                                                                          initial=0.0, op0=ALU.add, op1=ALU.add))(),
                     reads=[gfn, "zer"], writes=["Nb"])
                P.op("dve", (lambda gf=gf: lambda e: e.tensor_tensor_scan(out=seg(Nb, TC, T), data0=seg(gf, TC, T), data1=zer[:, TC:T],
                                                                          initial=Nb[:, 0:1], op0=ALU.add, op1=ALU.add))(),
                     reads=[gfn, "zer", "Nb"], writes=["Nb"])
            P.op("dve", (lambda gi=gi: lambda e: e.tensor_tensor(out=gi, in0=gi, in1=Nb, op=ALU.add))(), reads=[gin, "Nb"], writes=[gin])
            if d == 0:
                P.op("dve", (lambda gi=gi: lambda e: e.tensor_tensor_scan(out=Ut, data0=gi, data1=zer, initial=0.0,
                                                                          op0=ALU.max, op1=ALU.max))(),
                     reads=[gin, "zer"], writes=["Ut"])
            else:
                P.op("dve", (lambda gi=gi: lambda e: e.tensor_tensor_scan(out=seg(Ut, 0, TC), data0=seg(gi, 0, TC), data1=zer[:, 0:TC],
                                                                          initial=0.0, op0=ALU.max, op1=ALU.max))(),
                     reads=[gin, "zer"], writes=["Ut"])
                P.op("dve", (lambda gi=gi: lambda e: e.tensor_tensor_scan(out=seg(Ut, TC, T), data0=seg(gi, TC, T), data1=zer[:, TC:T],
                                                                          initial=Ut[:, 0:1], op0=ALU.max, op1=ALU.max))(),
                     reads=[gin, "zer", "Ut"], writes=["Ut"])
            U3 = Ut.rearrange("p (c t) -> p c t", t=128)
            endcol = 127 if d == 0 else 0
            P.op("dve", (lambda endcol=endcol: lambda e: e.tensor_copy(out=Uend, in_=U3[:, :, endcol]))(), reads=["Ut"], writes=["Uend"])
            P.op("pool", lambda e: e.memset(Uprev, 0.0), reads=[], writes=["Uprev"])
            if d == 0:
                P.op("dve", lambda e: e.tensor_copy(out=Uprev[:, 1:NCH], in_=Uend[:, 0:NCH - 1]), reads=["Uend", "Uprev"], writes=["Uprev"])
            else:
                P.op("dve", lambda e: e.tensor_copy(out=Uprev[:, 2:NCH - 1], in_=Uend[:, 3:NCH]), reads=["Uend", "Uprev"], writes=["Uprev"])
                P.op("dve", lambda e: e.tensor_copy(out=Uprev[:, NCH - 1:NCH], in_=Uend[:, 0:1]), reads=["Uend", "Uprev"], writes=["Uprev"])
                P.op("dve", lambda e: e.tensor_copy(out=Uprev[:, 0:1], in_=Uend[:, 1:2]), reads=["Uend", "Uprev"], writes=["Uprev"])
            upb = Uprev.unsqueeze(2).to_broadcast([4, NCH, 128])
            t3 = tmp.rearrange("p (c t) -> p c t", t=128)
            g3 = gi.rearrange("p (c t) -> p c t", t=128)
            Rd = Rr[d]
            P.op("dve", (lambda g3=g3: lambda e: e.tensor_tensor(out=t3, in0=g3, in1=upb, op=ALU.subtract))(),
                 reads=[gin, "Uprev"], writes=["tmp"])
            P.op("act", (lambda Rd=Rd: lambda e: e.activation(out=Rd[:, :, 0, :], in_=t3, func=AF.Exp, bias=cst[0:4, 2:3], scale=1.0))(),
                 reads=["tmp"] + CST, writes=[f"Rr{d}a"])
            P.op("dve", lambda e: e.tensor_tensor(out=t3, in0=upb, in1=U3, op=ALU.subtract), reads=["Ut", "Uprev", f"Rr{d}a"], writes=["tmp"])
            P.op("act", (lambda Rd=Rd: lambda e: e.activation(out=Rd[:, :, 1, :], in_=t3, func=AF.Exp))(),
                 reads=["tmp"], writes=[f"Rr{d}b"])
            P.op("dve", lambda e: e.tensor_tensor(out=tmp, in0=Nb, in1=Ut, op=ALU.subtract), reads=["Nb", "Ut", f"Rr{d}b"], writes=["tmp"])
            P.op("act", lambda e: e.activation(out=tmp, in_=tmp, func=AF.Exp), reads=["tmp"], writes=["tmp"])
            eb = 6 + d
            for c in range(NCH):
                P.op("pe", (lambda c=c, eb=eb: lambda e: e.matmul(bank(eb)[:, c * 4:c * 4 + 4], lhsT=tmp128[:, c * 128:(c + 1) * 128],
                                                                  rhs=identF[:, 0:4], start=True, stop=True))(),
                     reads=["tmp", "identF"], writes=[f"pb{eb}"])
            P.op("dve", (lambda d=d, eb=eb: lambda e: e.tensor_copy(out=emtT[d], in_=bank(eb)[:, 0:NCH * 4].rearrange("p (c h) -> p c h", h=4)))(),
                 reads=[f"pb{eb}"], writes=[f"emtT{d}"])
            P.op("dve", lambda e: e.tensor_tensor(out=dec, in0=Uprev, in1=Uend, op=ALU.subtract), reads=["Uprev", "Uend"], writes=["dec"])
            P.op("act", lambda e: e.activation(out=dec, in_=dec, func=AF.Exp), reads=["dec"], writes=["dec"])
            db = 4 + d
            for h in range(4):
                P.op("pe", (lambda h=h, db=db: lambda e: e.matmul(bank(db)[:, h * NCH:(h + 1) * NCH], lhsT=selF[:, h, :], rhs=dec128,
                                                                  start=True, stop=True))(),
                     reads=["dec", "selF"], writes=[f"pb{db}"])
            P.op("dve", (lambda d=d, db=db: lambda e: e.tensor_copy(out=decbc[d], in_=bank(db)[:, 0:4 * NCH].rearrange("p (h c) -> p h c", h=4)))(),
                 reads=[f"pb{db}"], writes=[f"decbc{d}"])
        P.barrier()

        A = Alloc(PB2)
        qkT = A.get([8, T], BF16)
        vaug = A.get([NCH, 4, 129], BF16)
        Cst = A.get([8, 129], F32)
        Cbf = A.get([8, 129], BF16)
        PB3 = A.off
        A2 = Alloc(PB3)
        PL = 2048
        xc = [A2.get([PL + 2], F32), A2.get([PL + 2], F32)]
        acc = A2.get([PL], F32)
        P.op("pool", lambda e: e.memset(vaug, 1.0), writes=["vaug"])
        for h in range(4):
            dma("sp", vaug[:, :, h, 0:128], tmv_d[:, 128 + h * 128:256 + h * 128].rearrange("(c p) d -> p c d", p=128),
                reads=["vaug"], writes=[f"vaugv{h}"])
        P.op("pool", lambda e: e.memset(Cst, 0.0), writes=["Cst"])
        P.op("pool", lambda e: e.memset(Cbf, 0.0), writes=["Cbf"])
        pieces = [(0, TC, 0, TC), (TC, TC + PL, TC, T), (TC + PL, T, TC, T)]
        it = 0
        for cr in range(8):
            w = [convc[:, (L * 3 + a) * 8 + cr:(L * 3 + a) * 8 + cr + 1] for a in range(3)]
            for (a_, b_, sa, sb) in pieces:
                ln_ = b_ - a_
                lh = a_ > sa
                rh = b_ < sb
                x_, xn_ = xc[it % 2], f"xc{it % 2}"
                it += 1
                lo = a_ - 1 if lh else a_
                hi = b_ + 1 if rh else b_
                c0 = 0 if lh else 1
                dma("sp", x_[:, c0:c0 + hi - lo], mqkT_d[cr, :, lo:hi], writes=[xn_])
                P.op("pool", (lambda x_=x_, w=w, ln_=ln_: lambda e: e.tensor_scalar_mul(out=acc[:, 0:ln_], in0=x_[:, 1:ln_ + 1], scalar1=w[1]))(),
                     reads=[xn_], writes=["acc"])
                o0 = 0 if lh else 1
                P.op("dve", (lambda x_=x_, w=w, o0=o0, ln_=ln_: lambda e: e.scalar_tensor_tensor(
                    out=acc[:, o0:ln_], in0=x_[:, o0:ln_], scalar=w[0], in1=acc[:, o0:ln_], op0=ALU.mult, op1=ALU.add))(),
                    reads=[xn_, "acc"], writes=["acc"])
                o1 = ln_ if rh else ln_ - 1
                P.op("dve", (lambda x_=x_, w=w, o1=o1: lambda e: e.scalar_tensor_tensor(
                    out=acc[:, 0:o1], in0=x_[:, 2:o1 + 2], scalar=w[2], in1=acc[:, 0:o1], op0=ALU.mult, op1=ALU.add))(),
                    reads=[xn_, "acc"], writes=["acc"])
                P.op("act", (lambda cr=cr, a_=a_, b_=b_, ln_=ln_: lambda e: e.activation(out=qkT[:, cr, a_:b_], in_=acc[:, 0:ln_], func=AF.Silu))(),
                     reads=["acc"], writes=[f"qkT{cr}"])
        P.barrier()

        A3 = Alloc(PB3)
        NR = 8
        kTs = [A3.get([128], BF16) for _ in range(NR)]
        qTs = [A3.get([128], BF16) for _ in range(NR)]
        Sw = [A3.get([128], BF16) for _ in range(NR)]
        k2 = [A3.get([128], BF16) for _ in range(NR)]
        dmx = [A3.get([2], F32) for _ in range(NR)]
        hst = [[A3.get([4, 128], F32) for _ in range(2)] for _ in range(2)]
        prot = Rot(list(range(8)))
        order = [list(range(NCH)), [1, 0] + list(range(NCH - 1, 1, -1))]
        step = 0
        hcnt = [0, 0]
        for i in range(NCH):
            for d in range(2):
                c = order[d][i]
                emit_out = emit_ctx or c >= 2
                hs_, hsn = hst[d][hcnt[d] % 2], f"hst{d}{hcnt[d] % 2}"
                for h in range(4):
                    ri = step % NR
                    step += 1
                    idx = d * 4 + h
                    ts = slice(c * 128, (c + 1) * 128)
                    bb, sb_, tb_, nb_, db_ = prot.next(), prot.next(), prot.next(), prot.next(), prot.next()
                    Rd = Rr128[d]
                    P.op("pe", (lambda bb=bb, h=h, c=c, Rd=Rd: lambda e: e.matmul(
                        bank(bb)[:, 0:256], lhsT=selB[:, h, :], rhs=Rd[:, c, :, :].rearrange("p a b -> p (a b)"),
                        start=True, stop=True))(), reads=[], writes=[f"pb{bb}"])
                    P.op("dve", (lambda bb=bb, h=h, ts=ts, ri=ri: lambda e: e.tensor_tensor(
                        out=kTs[ri], in0=qkT[:, 4 + h, ts], in1=bank(bb)[:, 0:128], op=ALU.mult))(),
                        reads=[f"pb{bb}"], writes=[f"kTs{ri}"])
                    P.op("dve", (lambda bb=bb, h=h, ts=ts, ri=ri: lambda e: e.tensor_tensor(
                        out=qTs[ri], in0=qkT[:, h, ts], in1=bank(bb)[:, 128:256], op=ALU.mult))(),
                        reads=[f"pb{bb}"], writes=[f"qTs{ri}"])
                    P.op("pe", (lambda sb_=sb_, ri=ri: lambda e: e.matmul(bank(sb_)[:, 0:128], lhsT=kTs[ri], rhs=qTs[ri], start=True, stop=True))(),
                         reads=[f"kTs{ri}", f"qTs{ri}"], writes=[f"pb{sb_}"])
                    P.op("dve", (lambda sb_=sb_, ri=ri, d=d: lambda e: e.tensor_tensor(
                        out=Sw[ri], in0=bank(sb_)[:, 0:128], in1=maskT[d], op=ALU.mult))(),
                        reads=[f"pb{sb_}"], writes=[f"Sw{ri}"])
                    P.op("pe", (lambda tb_=tb_, ri=ri: lambda e: e.transpose(out=bank_bf(tb_)[:, 0:128], in_=kTs[ri], identity=identB))(),
                         reads=[f"kTs{ri}"], writes=[f"pb{tb_}"])
                    P.op("act", (lambda tb_=tb_, ri=ri, d=d, h=h, c=c: lambda e: e.activation(
                        out=k2[ri], in_=bank_bf(tb_)[:, 0:128], func=AF.Identity, scale=decbc[d][:, h, c:c + 1]))(),
                        reads=[f"pb{tb_}"], writes=[f"k2{ri}"])
                    if emit_out:
                        P.op("pe", (lambda nb_=nb_, ri=ri, idx=idx: lambda e: e.matmul(
                            bank(nb_)[:, 0:129], lhsT=qTs[ri], rhs=Cbf[:, idx, :], start=True, stop=False))(),
                            reads=[f"qTs{ri}", f"Cbf{idx}"], writes=[f"pb{nb_}"])
                        P.op("pe", (lambda nb_=nb_, ri=ri, c=c, h=h: lambda e: e.matmul(
                            bank(nb_)[:, 0:129], lhsT=Sw[ri], rhs=vaug[:, c, h, :], start=False, stop=True))(),
                            reads=[f"Sw{ri}"], writes=[f"pb{nb_}"])
                        P.op("act", (lambda nb_=nb_, ri=ri: lambda e: e.activation(
                            out=dmx[ri][:, 0:1], in_=bank(nb_)[:, 128:129], func=AF.Abs))(), reads=[f"pb{nb_}"], writes=[f"dmx{ri}"])
                        P.op("dve", (lambda ri=ri, d=d, c=c, h=h: lambda e: e.tensor_tensor(
                            out=dmx[ri][:, 0:1], in0=dmx[ri][:, 0:1], in1=emtT[d][:, c, h:h + 1], op=ALU.max))(),
                            reads=[f"dmx{ri}"], writes=[f"dmx{ri}"])
                        P.op("dve", (lambda ri=ri: lambda e: e.reciprocal(out=dmx[ri][:, 1:2], in_=dmx[ri][:, 0:1]))(),
                             reads=[f"dmx{ri}"], writes=[f"dmr{ri}"])
                        P.op("act", (lambda nb_=nb_, ri=ri, h=h, hs_=hs_: lambda e: e.activation(
                            out=hs_[:, h, :], in_=bank(nb_)[:, 0:128], func=AF.Identity, scale=dmx[ri][:, 1:2]))(),
                            reads=[f"pb{nb_}", f"dmr{ri}"], writes=[hsn + str(h)])
                    P.op("pe", (lambda db_=db_, ri=ri, c=c, h=h: lambda e: e.matmul(
                        bank(db_)[:, 0:129], lhsT=k2[ri], rhs=vaug[:, c, h, :], start=True, stop=True))(),
                        reads=[f"k2{ri}"], writes=[f"pb{db_}"])
                    P.op("dve", (lambda db_=db_, idx=idx, d=d, h=h, c=c: lambda e: e.scalar_tensor_tensor(
                        out=Cst[:, idx, :], in0=Cst[:, idx, :], scalar=decbc[d][:, h, c:c + 1], in1=bank(db_)[:, 0:129],
                        op0=ALU.mult, op1=ALU.add))(), reads=[f"pb{db_}", f"Cst{idx}"], writes=[f"Cst{idx}"])
                    P.op("pool", (lambda idx=idx: lambda e: e.tensor_copy(out=Cbf[:, idx, :], in_=Cst[:, idx, :]))(),
                         reads=[f"Cst{idx}"], writes=[f"Cbf{idx}"])
                if emit_out:
                    hd = hf_d if d == 0 else hb_d
                    dma("sp", hd[c * 128:(c + 1) * 128, :], hs_.rearrange("p h d -> p (h d)"),
                        reads=[hsn + str(h) for h in range(4)], writes=tres("hfb%d" % d, c * 128, 128))
                    hcnt[d] += 1
        P.barrier()

        A4 = Alloc(PB2)
        mlg = A4.get([512], F32)
        hfa = [A4.get([512], F32), A4.get([512], F32)]
        hba = [A4.get([512], F32), A4.get([512], F32)]
        mo = [A4.get([512], BF16), A4.get([512], BF16)]
        sq4 = A4.get([512], F32)
        s4 = A4.get([4], F32)
        memb = A4.get([512], BF16)
        memst = [A4.get([4, 128], BF16), A4.get([4, 128], BF16)]
        dma("sp", mlg, mlg_d[:, L, :], writes=["mlg"])
        trot = Rot([0, 1, 2, 3])
        gs = list(range(0 if emit_ctx else 2, NCH))
        for gi_, g in enumerate(gs):
            p2 = gi_ % 2
            ha, hbv, mov, mst = hfa[p2], hba[p2], mo[p2], memst[p2]
            rows = slice(g * 128, (g + 1) * 128)
            dma("sp", ha, hf_d[rows, :], writes=[f"hfa{p2}"])
            dma("sp", hbv, hb_d[rows, :], writes=[f"hba{p2}"])
            dma("sp", mov, tmv_d[rows, 640:1152], writes=[f"mo{p2}"])
            P.op("dve", (lambda ha=ha, hbv=hbv: lambda e: e.tensor_tensor(out=ha, in0=ha, in1=hbv, op=ALU.add))(),
                 reads=[f"hfa{p2}", f"hba{p2}"], writes=[f"hfa{p2}"])
            P.op("pool", (lambda ha=ha: lambda e: e.tensor_tensor(out=sq4, in0=ha, in1=ha, op=ALU.mult))(), reads=[f"hfa{p2}"], writes=["sq4"])
            P.op("dve", lambda e: e.reduce_sum(out=s4, in_=sq4.rearrange("p (h d) -> p h d", h=4), axis=AX.X), reads=["sq4"], writes=["s4"])
            P.op("act", lambda e: e.activation(out=s4, in_=s4, func=AF.Sqrt, scale=1.0 / 128, bias=cst[:, 0:1]), reads=["s4"] + CST, writes=["s4"])
            P.op("dve", lambda e: e.reciprocal(out=s4, in_=s4), reads=["s4"], writes=["s4"])
            P.op("dve", (lambda ha=ha: lambda e: e.tensor_tensor(out=ha.rearrange("p (h d) -> p h d", h=4),
                                                                in0=ha.rearrange("p (h d) -> p h d", h=4),
                                                                in1=s4.unsqueeze(2).to_broadcast([128, 4, 128]), op=ALU.mult))(),
                 reads=[f"hfa{p2}", "s4"], writes=[f"hfa{p2}"])
            P.op("pool", (lambda ha=ha: lambda e: e.tensor_tensor(out=ha, in0=ha, in1=mlg, op=ALU.mult))(), reads=[f"hfa{p2}", "mlg"], writes=[f"hfa{p2}"])
            P.op("pool", (lambda ha=ha, mov=mov: lambda e: e.tensor_tensor(out=memb, in0=ha, in1=mov, op=ALU.mult))(),
                 reads=[f"hfa{p2}", f"mo{p2}"], writes=["memb"])
            tb = trot.next()
            for h in range(4):
                P.op("pe", (lambda tb=tb, h=h: lambda e: e.transpose(out=bank_bf(tb)[:, h * 128:(h + 1) * 128],
                                                                    in_=memb[:, h * 128:(h + 1) * 128], identity=identB))(),
                     reads=["memb", "identB"], writes=[f"pb{tb}"])
            P.op("act", (lambda tb=tb, mst=mst: lambda e: e.activation(out=mst, in_=bank_bf(tb)[:, 0:512].rearrange("p (a b) -> p a b", a=4),
                                                                       func=AF.Copy))(), reads=[f"pb{tb}"], writes=[f"memst{p2}"])
            dma("sp", attT_d[4:8, :, g * 128:(g + 1) * 128].rearrange("j p t -> p j t"), mst, reads=[f"memst{p2}"],
                writes=tres("attT_m", g * 128, 128))
        P.barrier()

    def phase_mlp(L, emit_ctx, prefetched):
        w1, w2 = w4_views()
        if not prefetched:
            for k in range(8):
                dma("pool", w1[:, k, :], w1_d[L, k * 128:(k + 1) * 128, :], writes=[f"w1_{k}"])
            for k4 in range(8):
                dma("pool", w2[:, k4 * 4:(k4 + 1) * 4, :],
                    w2_d[L, k4 * 512:(k4 + 1) * 512, :].rearrange("(k p) c -> p k c", p=128), writes=[f"w2_{k4}"])
        W1 = [f"w1_{k}" for k in range(8)]
        W2 = [f"w2_{k}" for k in range(8)]
        A = Alloc(PBASE)
        NB = 256
        w_out = A.get([8, D], BF16)
        dma("pool", w_out, w_out_d[L].rearrange("(k p) c -> p k c", p=128), writes=["w_out"])
        attTb = [A.get([8, NB], BF16)]
        xTb = [A.get([8, NB], F32), A.get([8, NB], F32)]
        h2T = A.get([8, NB], BF16)
        sq = [A.get([NB], BF16), A.get([NB], BF16)]
        rstd = A.get([NB], F32)
        xn = [A.get([NB], F32), A.get([NB], F32)]
        rl = [A.get([NB], BF16), A.get([NB], BF16)]
        uT = A.get([32, NB], BF16)
        assert A.off <= WOFF, A.off
        rot = Rot([1, 2, 3, 4, 5, 6, 7])
        t0s = list(range(0 if emit_ctx else TC, T, NB))
        def do_block(bi, t0):
            n = NB
            r = 1 if t0 < TC else 0
            ab, abn = attTb[0], "attTb0"
            xb, xbn = xTb[bi % 2], f"xTb{bi % 2}"
            dma("sp", ab, attT_d[:, :, t0:t0 + n].rearrange("j p t -> p j t"), writes=[abn])
            dma("sp", xb, xT_d[:, :, t0:t0 + n].rearrange("j p t -> p j t"), writes=[xbn])
            for cj in range(8):
                b = rot.next()
                for k in range(8):
                    P.op("pe", (lambda b=b, k=k, cj=cj, ab=ab: lambda e: e.matmul(
                        bank(b)[:, 0:n], lhsT=w_out[:, k, cj * 128:(cj + 1) * 128], rhs=ab[:, k, :], start=(k == 0), stop=(k == 7)))(),
                        reads=["w_out", abn], writes=[f"pb{b}"])
                P.op("dve", (lambda b=b, cj=cj, xb=xb, r=r: lambda e: e.scalar_tensor_tensor(
                    out=xb[:, cj, :], in0=bank(b)[:, 0:n], scalar=modcol(L, 16 + cj, r), in1=xb[:, cj, :], op0=ALU.mult, op1=ALU.add))(),
                    reads=[f"pb{b}", xbn, "mod"], writes=[xbn])
            norm_mod(xb, xbn, n, sq, rstd, xn, h2T, "h2T",
                     lambda j: s2[:, L * 8 + j, r:r + 1], lambda j: modcol(L, 24 + j, r), 0)
            for fc in range(32):
                b = rot.next()
                for k in range(8):
                    P.op("pe", (lambda b=b, k=k, fc=fc: lambda e: e.matmul(
                        bank(b)[:, 0:n], lhsT=w1[:, k, fc * 128:(fc + 1) * 128], rhs=h2T[:, k, :], start=(k == 0), stop=(k == 7)))(),
                        reads=[f"w1_{k}", "h2T"], writes=[f"pb{b}"])
                r_, rn_ = rl[fc % 2], f"rl{fc % 2}"
                P.op("act", (lambda b=b, r_=r_: lambda e: e.activation(out=r_, in_=bank(b)[:, 0:n], func=AF.Relu))(),
                     reads=[f"pb{b}"], writes=[rn_])
                P.op("pool", (lambda r_=r_, fc=fc: lambda e: e.tensor_tensor(out=uT[:, fc, :], in0=r_, in1=r_, op=ALU.mult))(),
                     reads=[rn_], writes=[f"uT{fc}"])
            for cj in range(8):
                b = rot.next()
                for fc in range(32):
                    P.op("pe", (lambda b=b, fc=fc, cj=cj: lambda e: e.matmul(
                        bank(b)[:, 0:n], lhsT=w2[:, fc, cj * 128:(cj + 1) * 128], rhs=uT[:, fc, :], start=(fc == 0), stop=(fc == 31)))(),
                        reads=[f"w2_{fc // 4}", f"uT{fc}"], writes=[f"pb{b}"])
                P.op("dve", (lambda b=b, cj=cj, xb=xb, r=r: lambda e: e.scalar_tensor_tensor(
                    out=xb[:, cj, :], in0=bank(b)[:, 0:n], scalar=modcol(L, 40 + cj, r), in1=xb[:, cj, :], op0=ALU.mult, op1=ALU.add))(),
                    reads=[f"pb{b}", xbn, "mod"], writes=[xbn])
            dma("sp", xT_d[:, :, t0:t0 + n].rearrange("j p t -> p j t"), xb, reads=[xbn], writes=tres("xTo", t0, n))
        for bi, t0 in enumerate(t0s):
            do_block(bi, t0)
        P.barrier()

    def phase_final():
        A = Alloc(PBASE)
        xTb = [A.get([8, 512], F32), A.get([8, 512], F32)]
        sq = [A.get([512], BF16), A.get([512], BF16)]
        rstd = A.get([512], F32)
        xnf = A.get([8, 512], F32)
        ost = [A.get([1024], F32), A.get([1024], F32)]
        rot = Rot([1, 2, 3, 4, 5, 6])
        oc = 0
        def do_block(bi):
            nonlocal oc
            t0 = TC + bi * 512
            n = 512
            xb, xbn = xTb[bi % 2], f"xTb{bi % 2}"
            dma("sp", xb, xT_d[:, :, t0:t0 + n].rearrange("j p t -> p j t"), writes=[xbn])
            ssps = bank(0)
            for j in range(8):
                s_, sn = sq[j % 2], f"sq{j % 2}"
                P.op("act", (lambda s_=s_, j=j, xb=xb: lambda e: e.activation(out=s_, in_=xb[:, j, :], func=AF.Square))(), reads=[xbn], writes=[sn])
                P.op("pe", (lambda s_=s_, j=j: lambda e: e.matmul(ssps, lhsT=onesB, rhs=s_, start=(j == 0), stop=(j == 7)))(),
                     reads=[sn, "onesB"], writes=["pb0"])
            P.op("act", lambda e: e.activation(out=rstd, in_=ssps, func=AF.Sqrt, scale=1.0 / D, bias=cst[:, 0:1]), reads=["pb0"] + CST, writes=["rstd"])
            P.op("dve", lambda e: e.reciprocal(out=rstd, in_=rstd), reads=["rstd"], writes=["rstd"])
            for j in range(8):
                P.op("dve", (lambda j=j, xb=xb: lambda e: e.scalar_tensor_tensor(
                    out=xnf[:, j, :], in0=xb[:, j, :], scalar=nfin[:, j:j + 1], in1=rstd, op0=ALU.mult, op1=ALU.mult))(),
                    reads=[xbn, "rstd", "nfin"], writes=[f"xnf{j}"])
            for m in range(4):
                o_, on_ = ost[oc % 2], f"ost{oc % 2}"
                oc += 1
                for half in range(2):
                    b = rot.next()
                    for jj in range(4):
                        j = half * 4 + jj
                        P.op("pe", (lambda b=b, jj=jj, j=j, m=m: lambda e: e.transpose(
                            out=bank(b)[:, jj * 128:(jj + 1) * 128], in_=xnf[:, j, m * 128:(m + 1) * 128], identity=identF))(),
                            reads=[f"xnf{j}", "identF"], writes=[f"pb{b}"])
                    if half == 0:
                        P.op("act", (lambda b=b, o_=o_: lambda e: e.activation(out=o_[:, 0:512], in_=bank(b), func=AF.Copy))(),
                             reads=[f"pb{b}"], writes=[on_ + "a"])
                    else:
                        P.op("dve", (lambda b=b, o_=o_: lambda e: e.tensor_copy(out=o_[:, 512:1024], in_=bank(b)))(),
                             reads=[f"pb{b}"], writes=[on_ + "b"])
                r0 = bi * 512 + m * 128
                dma("sp", out_d[r0:r0 + 128, :], o_, reads=[on_ + "a", on_ + "b"], writes=[f"out{r0}"])

        for bi in range(8):
            do_block(bi)

    steps = [("mod", phase_mod), ("xT", phase_xT)]
    for L in range(n_layers):
        emit_ctx = L < DEPTH - 1
        steps.append((f"inproj{L}", (lambda L=L: phase_inproj(L))))
        steps.append((f"mlstm{L}", (lambda L=L, ec=emit_ctx: phase_mlstm(L, ec))))
        steps.append((f"attn{L}", (lambda L=L, ec=emit_ctx: phase_attn(L, ec, prefetch=True))))
        steps.append((f"mlp{L}", (lambda L=L, ec=emit_ctx: phase_mlp(L, ec, prefetched=True))))
    steps.append(("final", phase_final))
    for name, fn in steps:
        fn()
        if stop is not None and name == stop:
            break
    P.emit(es)
    es.close()
    return nc, P


def _perm_att():
    idx = np.zeros(512, dtype=np.int64)
    for j in range(4):
        for p in range(128):
            head = j if p < 64 else j + 4
            idx[j * 128 + p] = head * 64 + (p % 64)
    return idx


def _rope_tables():
    rows = TL // 64
    row_idx = np.repeat(np.arange(rows, dtype=np.float32), 64)
    col_idx = np.tile(np.arange(64, dtype=np.float32), rows)
    inv_freq = np.power(np.float32(10000.0), -np.arange(0, 32, 2, dtype=np.float32) / np.float32(32)).astype(np.float32)
    ang = np.concatenate([row_idx[:, None] * inv_freq, col_idx[:, None] * inv_freq], axis=-1).astype(np.float32)
    cos = np.cos(ang).astype(np.float32).reshape(32, 128, 32).transpose(1, 0, 2)
    sin = np.sin(ang).astype(np.float32).reshape(32, 128, 32).transpose(1, 0, 2)
    return np.ascontiguousarray(cos), np.ascontiguousarray(sin)


def host_inputs(inputs, cores):
    f = lambda a: np.ascontiguousarray(np.asarray(a, dtype=np.float32))
    perm = _perm_att()
    w_in = f(inputs["w_in"])
    w_in_p = np.zeros((DEPTH, D, INCP), np.float32)
    w_in_p[:, :, 0:INC] = w_in
    w_in_p[:, :, 0:512] = w_in[:, :, perm]
    w_out = f(inputs["w_out"])
    w_out_p = w_out.copy()
    w_out_p[:, 0:512, :] = w_out[:, perm, :]
    cos, sin = _rope_tables()
    sel = np.zeros((128, 4, 128), np.float32)
    for h in range(4):
        sel[h, h, :] = 1.0
    s_ = np.arange(128)
    maskf = (s_[:, None] <= s_[None, :]).astype(np.float32)
    maskb = (s_[:, None] >= s_[None, :]).astype(np.float32)
    colL = lambda a, nj: np.ascontiguousarray(f(a).reshape(DEPTH, nj, 128).transpose(2, 0, 1))
    qk_gain = np.concatenate([np.tile(f(inputs["q_norm"]), (1, 8)), np.tile(f(inputs["k_norm"]), (1, 2))], axis=1)
    shared = {
        "w_ada": f(inputs["w_ada"]),
        "b_ada_c": colL(inputs["b_ada"], 48),
        "nmix_c": colL(inputs["norm_mix"], 8),
        "nmlp_c": colL(inputs["norm_mlp"], 8),
        "w_in_p": w_in_p,
        "bg_c": np.ascontiguousarray(f(inputs["b_gates"]).T),
        "conv_c": np.ascontiguousarray(f(inputs["conv_qk"]).reshape(DEPTH, 3, 8, 128).transpose(3, 0, 1, 2)),
        "qk_gain": np.ascontiguousarray(np.broadcast_to(qk_gain[None], (128, DEPTH, 640))),
        "ml_gain": np.ascontiguousarray(np.broadcast_to(f(inputs["mlstm_norm"])[None], (128, DEPTH, 512))),
        "w_out_p": w_out_p,
        "w_mlp_in": f(inputs["w_mlp_in"]),
        "w_mlp_out": f(inputs["w_mlp_out"]),
        "nfin_c": np.ascontiguousarray(f(inputs["norm_final"]).reshape(8, 128).T),
        "ident": np.eye(128, dtype=np.float32),
        "maskf": maskf,
        "maskb": maskb,
        "sel4": sel,
        "rope_cos": cos,
        "rope_sin": sin,
    }
    x = f(inputs["x"])
    ctx = f(inputs["ctx"])
    c = f(inputs["c"])
    c_ctx = f(inputs["c_ctx"])
    maps = []
    for b in cores:
        cc = np.stack([c[b].reshape(8, 128).T, c_ctx.reshape(8, 128).T], axis=-1)
        m = dict(shared)
        m["x"] = x[b]
        m["ctx"] = ctx[b]
        m["cc"] = np.ascontiguousarray(cc)
        maps.append(m)
    return maps


_NC_CACHE = {}


def kernel(**inputs):
    if "nc" not in _NC_CACHE:
        _NC_CACHE["nc"] = build()[0]
    nc = _NC_CACHE["nc"]
    maps = host_inputs(inputs, list(range(8)))
    res = run_bass_kernel_spmd(nc, maps, core_ids=list(range(8)))
    out = np.stack([np.asarray(r["out"], dtype=np.float32) for r in res.results], axis=0)
    return out
```

```python
import math
import os
KSKIP = os.environ.get('KSKIP', '')
from contextlib import ExitStack
import numpy as np
import concourse.bass as bass
import concourse.mybir as mybir
from concourse.bass_utils import run_bass_kernel_spmd

F32 = mybir.dt.float32
BF16 = mybir.dt.bfloat16
ALU = mybir.AluOpType
AF = mybir.ActivationFunctionType
AX = mybir.AxisListType

DEPTH = 4
D = 1024
TC = 256
TL = 4096
T = TC + TL
NCH = T // 128
INC = 2832
INCP = 2944
FFN = 4096
EPS = 1e-6
ARENA = 210944

CH = 2048
RROT = 8
NDMA = {"sp": 28, "pool": 20}
CENG = ("pe", "act", "dve", "pool")
ENGS = ("pe", "act", "dve", "pool", "sp")


class Res:
    __slots__ = ("writers", "readers")

    def __init__(self):
        self.writers = []
        self.readers = []


class Rec:
    __slots__ = ("eng", "fn", "idx", "cwaits", "dwaits", "signal", "is_dma", "dma_id", "sig_k")

    def __init__(self, eng, fn, is_dma):
        self.eng = eng
        self.fn = fn
        self.is_dma = is_dma
        self.cwaits = {}
        self.dwaits = {}
        self.signal = False
        self.dma_id = None
        self.sig_k = None


class Prog:
    def __init__(self, nc):
        self.nc = nc
        self.streams = {e: [] for e in ENGS}
        self.maxw = {e: {} for e in ENGS}
        self.dw = {e: set() for e in ENGS}
        self.ndma = {q: 0 for q in NDMA}
        self.res = {}
        self.last_real = {e: None for e in CENG}
        self.dma_live = {q: {} for q in NDMA}

    def R(self, name):
        r = self.res.get(name)
        if r is None:
            r = Res()
            self.res[name] = r
        return r

    def _need(self, rec, tok, same_ok):
        kind, e, i, prod = tok
        if kind == "c":
            if e == rec.eng and (same_ok or e == "pe"):
                return
            if self.maxw[rec.eng].get(e, -1) >= i:
                return
            if rec.cwaits.get(e, (-1, None))[0] < i:
                rec.cwaits[e] = (i, prod)
        else:
            key = (e, i)
            if key in self.dw[rec.eng]:
                return
            rec.dwaits[key] = prod

    def _commit(self, rec):
        for e, (i, prod) in rec.cwaits.items():
            self.maxw[rec.eng][e] = i
            prod.signal = True
        for key, prod in rec.dwaits.items():
            self.dw[rec.eng].add(key)

    def op(self, eng, fn, reads=(), writes=(), is_dma=False):
        rec = Rec(eng, fn, is_dma)
        st = self.streams[eng]
        rec.idx = len(st)
        psn = sorted({n for n in list(reads) + list(writes) if n.startswith("pb") or n == "modps"})
        reads = [self.R(r) for r in reads if r not in psn]
        writes = [self.R(r) for r in writes if r not in psn]
        psr = [self.R(n) for n in psn]
        for r in psr:
            for tok in r.writers:
                self._need(rec, tok, True)
        for r in reads:
            for tok in r.writers:
                self._need(rec, tok, False)
        for w in writes:
            for tok in w.writers:
                self._need(rec, tok, True)
            for tok in w.readers:
                self._need(rec, tok, True)
        self._commit(rec)
        if is_dma:
            rec.dma_id = self.ndma[eng]
            self.ndma[eng] += 1
            self.dma_live[eng][rec.dma_id % NDMA[eng]] = rec
            tok = ("d", eng, rec.dma_id, rec)
        else:
            tok = ("c", eng, rec.idx, rec)
            self.last_real[eng] = rec
        for r in reads:
            r.readers.append(tok)
        for w in writes:
            w.writers = [tok]
            w.readers = []
        for r in psr:
            r.writers = [tok]
            r.readers = []
        st.append(rec)
        return rec

    def barrier(self):
        for eng in ENGS:
            rec = Rec(eng, None, False)
            rec.idx = len(self.streams[eng])
            for e in CENG:
                lr = self.last_real[e]
                if lr is not None:
                    self._need(rec, ("c", e, lr.idx, lr), True)
            for q in NDMA:
                for slot, d in self.dma_live[q].items():
                    self._need(rec, ("d", q, d.dma_id, d), True)
            self._commit(rec)
            self.streams[eng].append(rec)
        self.res = {}

    def emit(self, es):
        nc = self.nc
        sems = {e: [es.enter_context(nc.semaphore(f"c_{e}_{i}")) for i in range(RROT)] for e in CENG}
        dsems = {q: [es.enter_context(nc.semaphore(f"d_{q}_{i}")) for i in range(n)] for q, n in NDMA.items()}
        for e in ENGS:
            k = 0
            for rec in self.streams[e]:
                if rec.is_dma or rec.fn is None:
                    continue
                if rec.signal:
                    rec.sig_k = k
                    k += 1

        def csem(e, k):
            return sems[e][(k // CH) % RROT], (k // (CH * RROT)) * CH + (k % CH) + 1

        def dsem(q, i):
            n = NDMA[q]
            return dsems[q][i % n], 16 * (i // n + 1)

        block = es.enter_context(nc.Block())

        def run(ename):
            def body(eng):
                for rec in self.streams[ename]:
                    for e, (i, prod) in rec.cwaits.items():
                        s, v = csem(e, prod.sig_k)
                        eng.wait_ge(s, v)
                    for (q, i) in rec.dwaits:
                        s, v = dsem(q, i)
                        eng.wait_ge(s, v)
                    if rec.fn is None:
                        continue
                    if rec.is_dma:
                        n = NDMA[ename]
                        if rec.dma_id >= n:
                            s, v = dsem(ename, rec.dma_id - n)
                            eng.wait_ge(s, v)
                    ins = rec.fn(eng)
                    if rec.is_dma:
                        s, v = dsem(ename, rec.dma_id)
                        ins.then_inc(s, 16)
                    elif rec.signal:
                        s, v = csem(ename, rec.sig_k)
                        ins.then_inc(s, 1)
                if ename in NDMA and self.ndma[ename] > 0:
                    n = NDMA[ename]
                    tot = self.ndma[ename]
                    for slot in range(min(n, tot)):
                        last = ((tot - 1 - slot) // n) * n + slot
                        s, v = dsem(ename, last)
                        eng.wait_ge(s, v)
            return body

        block.sync(run("sp"))
        block.tensor(run("pe"))
        block.scalar(run("act"))
        block.vector(run("dve"))
        block.gpsimd(run("pool"))


def tres(name, t0, n):
    return [f"{name}:{i}" for i in range(t0 // 128, (t0 + n + 127) // 128)]


def build(n_layers=DEPTH, debug=False, stop=None):
    nc = bass.Bass("TRN2", target_bir_lowering=False)

    def din(name, shape, dt=F32):
        return nc.dram_tensor(name, list(shape), dt, kind="ExternalInput").ap()

    def dscr(name, shape, dt):
        return nc.dram_tensor(name, list(shape), dt, kind="ExternalOutput" if debug else "Internal").ap()

    x_d = din("x", [TL, D])
    ctx_d = din("ctx", [TC, D])
    cc_d = din("cc", [128, 8, 2])
    w_ada_d = din("w_ada", [DEPTH, D, 6 * D])
    b_ada_d = din("b_ada_c", [128, DEPTH, 48])
    nmix_d = din("nmix_c", [128, DEPTH, 8])
    nmlp_d = din("nmlp_c", [128, DEPTH, 8])
    w_in_d = din("w_in_p", [DEPTH, D, INCP])
    bg_d = din("bg_c", [16, DEPTH])
    conv_d = din("conv_c", [128, DEPTH, 3, 8])
    qkg_d = din("qk_gain", [128, DEPTH, 640])
    mlg_d = din("ml_gain", [128, DEPTH, 512])
    w_out_d = din("w_out_p", [DEPTH, D, D])
    w1_d = din("w_mlp_in", [DEPTH, D, FFN])
    w2_d = din("w_mlp_out", [DEPTH, FFN, D])
    nfin_d = din("nfin_c", [128, 8])
    ident_d = din("ident", [128, 128])
    maskf_d = din("maskf", [128, 128])
    maskb_d = din("maskb", [128, 128])
    sel_d = din("sel4", [128, 4, 128])
    cos_d = din("rope_cos", [128, 32, 32])
    sin_d = din("rope_sin", [128, 32, 32])
    out_d = nc.dram_tensor("out", [TL, D], F32, kind="ExternalOutput").ap()

    xT_d = dscr("xT", [8, 128, T], F32)
    mqkT_d = dscr("mqkT", [8, 128, T], F32)
    gT_d = dscr("gT", [16, T], F32)
    QT_d = dscr("QT", [4, 128, T], BF16)
    KT_d = dscr("KT", [128, T], BF16)
    tmv_d = dscr("tmv", [T, 1152], BF16)
    attT_d = dscr("attT", [8, 128, T], BF16)
    hf_d = dscr("hf", [T, 512], F32)
    hb_d = dscr("hb", [T, 512], F32)

    P = Prog(nc)
    es = ExitStack()
    arena = es.enter_context(nc.sbuf_tensor("arena", [128, ARENA // 4], F32))
    psum = es.enter_context(nc.psum_tensor("psum", [128, 4096], F32))

    class Alloc:
        def __init__(self, base=0):
            self.off = base

        def get(self, shape, dt, parts=128):
            n = 1
            for s in shape:
                n *= s
            es_ = 4 if dt == F32 else 2
            nb = (n * es_ + 3) // 4 * 4
            w0 = self.off // 4
            ap = arena[0:parts, w0:w0 + nb // 4]
            self.off += nb
            assert self.off <= ARENA, f"arena overflow {self.off}"
            if dt == BF16:
                ap = ap.bitcast(BF16)
            if len(shape) == 2:
                ap = ap.rearrange("p (a b) -> p a b", a=shape[0])
            elif len(shape) == 3:
                ap = ap.rearrange("p (a b c) -> p a b c", a=shape[0], b=shape[1])
            return ap

    def bank(i):
        return psum[:, i * 512:(i + 1) * 512]

    def bank_bf(i):
        return psum[:, i * 512:(i + 1) * 512].bitcast(BF16)

    class Rot:
        def __init__(self, items):
            self.items = items
            self.i = 0

        def next(self):
            it = self.items[self.i % len(self.items)]
            self.i += 1
            return it

    def dma(q, out, in_, reads=(), writes=()):
        P.op(q, lambda e: e.dma_start(out=out, in_=in_), reads=reads, writes=writes, is_dma=True)

    A0 = Alloc(0)
    identF = A0.get([128], F32)
    identB = A0.get([128], BF16)
    onesB = A0.get([128], BF16)
    maskT = [A0.get([128], F32), A0.get([128], F32)]
    selF = A0.get([4, 128], F32)
    selB = A0.get([4, 128], BF16)
    cst = A0.get([4], F32)
    cc = A0.get([8, 2], F32)
    sc = A0.get([8, 2], F32)
    mod = A0.get([DEPTH * 48, 2], F32)
    s1 = A0.get([DEPTH * 8, 2], F32)
    s2 = A0.get([DEPTH * 8, 2], F32)
    nmix = A0.get([DEPTH * 8], F32)
    nmlp = A0.get([DEPTH * 8], F32)
    bada = A0.get([DEPTH * 48], F32)
    bgc = A0.get([DEPTH], F32, parts=16)
    convc = A0.get([DEPTH * 3 * 8], F32)
    nfin = A0.get([8], F32)
    PBASE = A0.off

    def modcol(L, j, r):
        return mod[:, L * 48 + j, r:r + 1]

    dma("sp", identF, ident_d, writes=["identF"])
    dma("sp", maskT[0], maskf_d, writes=["maskf"])
    dma("sp", maskT[1], maskb_d, writes=["maskb"])
    dma("sp", selF, sel_d, writes=["selF"])
    dma("sp", cc, cc_d, writes=["cc"])
    dma("sp", nmix, nmix_d.rearrange("p l j -> p (l j)"), writes=["nmix"])
    dma("sp", nmlp, nmlp_d.rearrange("p l j -> p (l j)"), writes=["nmlp"])
    dma("sp", bada, b_ada_d.rearrange("p l j -> p (l j)"), writes=["bada"])
    dma("sp", bgc, bg_d, writes=["bgc"])
    dma("sp", convc, conv_d.rearrange("p l a c -> p (l a c)"), writes=["convc"])
    dma("sp", nfin, nfin_d, writes=["nfin"])
    P.op("act", lambda e: e.activation(out=identB, in_=identF, func=AF.Copy), reads=["identF"], writes=["identB"])
    P.op("pool", lambda e: e.memset(onesB, 1.0), writes=["onesB"])
    P.op("act", lambda e: e.activation(out=selB, in_=selF, func=AF.Copy), reads=["selF"], writes=["selB"])
    P.op("pool", lambda e: e.memset(cst[:, 0:1], EPS), writes=["cst0"])
    P.op("pool", lambda e: e.memset(cst[:, 1:2], 1.0), writes=["cst1"])
    P.op("pool", lambda e: e.memset(cst[:, 2:3], -0.5 * math.log(128.0)), writes=["cst2"])
    P.op("pool", lambda e: e.memset(cst[:, 3:4], 0.0), writes=["cst3"])
    CST = ["cst0", "cst1", "cst2", "cst3"]
    P.op("act", lambda e: e.activation(out=sc, in_=cc, func=AF.Silu), reads=["cc"], writes=["sc"])

    def phase_mod():
        A = Alloc(PBASE)
        wa = [A.get([8, 512], F32), A.get([8, 512], F32)]
        modps = bank(0)
        it = 0
        for L in range(n_layers):
            wsrc = w_ada_d[L].rearrange("(k p) c -> p k c", p=128)
            for pc in range(12):
                buf = wa[it % 2]
                bn = f"wa{it % 2}"
                it += 1
                dma("sp", buf, wsrc[:, :, pc * 512:(pc + 1) * 512], writes=[bn])
                for cjj in range(4):
                    j = pc * 4 + cjj
                    col = (L * 48 + j) * 2
                    for k in range(8):
                        P.op("pe", (lambda buf=buf, k=k, cjj=cjj, col=col: lambda e: e.matmul(
                            modps[:, col:col + 2], lhsT=buf[:, k, cjj * 128:(cjj + 1) * 128], rhs=sc[:, k, :],
                            start=(k == 0), stop=(k == 7)))(),
                            reads=[bn, "sc"], writes=["modps"])
        nl = n_layers * 48
        P.op("dve", lambda e: e.tensor_tensor(
            out=mod[:, 0:nl, :], in0=modps[:, 0:nl * 2].rearrange("p (a b) -> p a b", b=2),
            in1=bada[:, 0:nl].unsqueeze(2).to_broadcast([128, nl, 2]), op=ALU.add),
            reads=["modps", "bada"], writes=["mod"])
        for L in range(n_layers):
            for (dst, nm, nmn, jo) in ((s1, nmix, "nmix", 8), (s2, nmlp, "nmlp", 32)):
                P.op("dve", (lambda dst=dst, L=L, jo=jo: lambda e: e.tensor_scalar_add(
                    out=dst[:, L * 8:(L + 1) * 8, :], in0=mod[:, L * 48 + jo:L * 48 + jo + 8, :], scalar1=1.0))(),
                    reads=["mod"], writes=["s12"])
                P.op("dve", (lambda dst=dst, L=L, nm=nm: lambda e: e.tensor_tensor(
                    out=dst[:, L * 8:(L + 1) * 8, :], in0=dst[:, L * 8:(L + 1) * 8, :],
                    in1=nm[:, L * 8:(L + 1) * 8].unsqueeze(2).to_broadcast([128, 8, 2]), op=ALU.mult))(),
                    reads=["s12", nmn], writes=["s12"])
        P.barrier()

    def phase_xT():
        A = Alloc(PBASE)
        xin = [A.get([1024], F32), A.get([1024], F32)]
        xst = [A.get([8, 128], F32), A.get([8, 128], F32)]
        rot = Rot([1, 2, 3, 4])
        for g in range(NCH):
            src = ctx_d[g * 128:(g + 1) * 128, :] if g < 2 else x_d[(g - 2) * 128:(g - 1) * 128, :]
            xb, xn_ = xin[g % 2], f"xin{g % 2}"
            st, stn = xst[g % 2], f"xst{g % 2}"
            dma("sp", xb, src, writes=[xn_])
            for half in range(2):
                b = rot.next()
                bk = bank(b)
                for jj in range(4):
                    j = half * 4 + jj
                    P.op("pe", (lambda bk=bk, jj=jj, j=j, xb=xb: lambda e: e.transpose(
                        out=bk[:, jj * 128:(jj + 1) * 128], in_=xb[:, j * 128:(j + 1) * 128], identity=identF))(),
                        reads=[xn_, "identF"], writes=[f"pb{b}"])
                eng = "act" if half == 0 else "dve"
                if eng == "act":
                    P.op("act", (lambda bk=bk, st=st, half=half: lambda e: e.activation(
                        out=st[:, half * 4:half * 4 + 4, :], in_=bk.rearrange("p (a b) -> p a b", a=4), func=AF.Copy))(),
                        reads=[f"pb{b}"], writes=[stn])
                else:
                    P.op("dve", (lambda bk=bk, st=st, half=half: lambda e: e.tensor_copy(
                        out=st[:, half * 4:half * 4 + 4, :], in_=bk.rearrange("p (a b) -> p a b", a=4)))(),
                        reads=[f"pb{b}"], writes=[stn])
            dma("sp", xT_d[:, :, g * 128:(g + 1) * 128].rearrange("j p t -> p j t"), st, reads=[stn],
                writes=tres("xT", g * 128, 128))
        P.barrier()

    def norm_mod(xTb, xname, n, sq, rstd, xn, hT, hname, scol, bcol, ssb):
        ssps = bank(ssb)
        for j in range(8):
            s_, sn = sq[j % 2], f"sq{j % 2}"
            P.op("act", (lambda s_=s_, j=j: lambda e: e.activation(out=s_[:, 0:n], in_=xTb[:, j, 0:n], func=AF.Square))(),
                 reads=[xname], writes=[sn])
            P.op("pe", (lambda s_=s_, j=j: lambda e: e.matmul(ssps[:, 0:n], lhsT=onesB, rhs=s_[:, 0:n],
                                                              start=(j == 0), stop=(j == 7)))(),
                 reads=[sn, "onesB"], writes=[f"pb{ssb}"])
        P.op("act", lambda e: e.activation(out=rstd[:, 0:n], in_=ssps[:, 0:n], func=AF.Sqrt, scale=1.0 / D, bias=cst[:, 0:1]),
             reads=[f"pb{ssb}"] + CST, writes=["rstd"])
        P.op("dve", lambda e: e.reciprocal(out=rstd[:, 0:n], in_=rstd[:, 0:n]), reads=["rstd"], writes=["rstd"])
        for j in range(8):
            x_, xn_ = xn[j % 2], f"xn{j % 2}"
            P.op("dve", (lambda x_=x_, j=j: lambda e: e.tensor_tensor(out=x_[:, 0:n], in0=xTb[:, j, 0:n], in1=rstd[:, 0:n],
                                                                      op=ALU.mult))(),
                 reads=[xname, "rstd"], writes=[xn_])
            P.op("act", (lambda x_=x_, j=j: lambda e: e.activation(out=hT[:, j, 0:n], in_=x_[:, 0:n], func=AF.Identity,
                                                                   scale=scol(j), bias=bcol(j)))(),
                 reads=[xn_, "s12", "mod"], writes=[hname])

    def phase_inproj(L):
        A = Alloc(PBASE)
        w_in = A.get([8, INCP], BF16)
        qkg = A.get([640], F32)
        cosT = A.get([32, 32], F32)
        sinT = A.get([32, 32], F32)
        dma("sp", cosT, cos_d, writes=["cosT"])
        dma("sp", sinT, sin_d, writes=["sinT"])
        xTb = [A.get([8, 512], F32), A.get([8, 512], F32)]
        sq = [A.get([512], BF16), A.get([512], BF16)]
        rstd = A.get([512], F32)
        xn = [A.get([512], F32), A.get([512], F32)]
        hT = A.get([8, 512], BF16)
        qk_sb = A.get([640], F32)
        sqq = A.get([640], F32)
        ss = A.get([10], F32)
        qn = A.get([640], F32)
        ra = A.get([320], F32)
        rb = A.get([320], F32)
        qr = A.get([640], BF16)
        QTst = A.get([4, 512], BF16)
        KTst = A.get([512], BF16)
        tmvst = [A.get([1152], BF16), A.get([1152], BF16)]
        fmst = A.get([8, 512], F32)
        gst = A.get([512], F32, parts=16)

        wsrc = w_in_d[L].rearrange("(k p) c -> p k c", p=128)
        for (c0, c1) in ((0, 512), (512, 768), (768, 1792), (1792, 2304), (2304, 2816), (2816, 2944)):
            if 'w' not in KSKIP:
                dma("pool", w_in[:, :, c0:c1], wsrc[:, :, c0:c1], writes=[f"w_in{c0}"])
        dma("sp", qkg, qkg_d[:, L, :], writes=["qkg"])
        rot = Rot([2, 3, 4] if 'b' in KSKIP else [2, 3, 4, 5, 6, 7])
        blocks = [(0, 256)] + [(256 + 512 * i, 512) for i in range(8)]
        def do_block(bi, t0, n):
            r = 1 if t0 < TC else 0
            xb, xname = xTb[bi % 2], f"xTb{bi % 2}"
            dma("sp", xb[:, :, 0:n], xT_d[:, :, t0:t0 + n].rearrange("j p t -> p j t"),
                reads=tres("xT", t0, n), writes=[xname])
            if 'n' not in KSKIP:
                norm_mod(xb, xname, n, sq, rstd, xn, hT, "hT",
                         lambda j: s1[:, L * 8 + j, r:r + 1], lambda j: modcol(L, j, r), 0)
            for m in range(0 if 'm' in KSKIP else n // 128):
                tok0 = t0 + m * 128
                ms = slice(m * 128, (m + 1) * 128)
                tv, tvn = tmvst[m % 2], f"tmvst{m % 2}"
                groups = ((0, 512, "w_in0"), (512, 768, "w_in512"), (1792, 2304, "w_in1792"), (2304, 2816, "w_in2304"))
                pbs = []
                for (c0, c1, wn) in groups:
                    b = rot.next()
                    pbs.append(b)
                    for k in range(8):
                        P.op("pe", (lambda b=b, k=k, c0=c0, c1=c1, ms=ms: lambda e: e.matmul(
                            bank(b)[:, 0:c1 - c0], lhsT=hT[:, k, ms], rhs=w_in[:, k, c0:c1], start=(k == 0), stop=(k == 7)))(),
                            reads=["hT", wn], writes=[f"pb{b}"])
                b0, b1, b2, b3 = pbs
                P.op("act", (lambda b0=b0: lambda e: e.activation(out=qk_sb[:, 0:512], in_=bank(b0), func=AF.Copy))(),
                     reads=[f"pb{b0}"], writes=["qk_sb_a"])
                P.op("act", (lambda b1=b1: lambda e: e.activation(out=qk_sb[:, 512:640], in_=bank(b1)[:, 0:128], func=AF.Copy))(),
                     reads=[f"pb{b1}"], writes=["qk_sb_b"])
                P.op("dve", (lambda b1=b1, tv=tv: lambda e: e.tensor_copy(out=tv[:, 0:128], in_=bank(b1)[:, 128:256]))(),
                     reads=[f"pb{b1}"], writes=[tvn + "a"])
                P.op("dve", (lambda b2=b2, tv=tv: lambda e: e.tensor_copy(out=tv[:, 128:640], in_=bank(b2)))(),
                     reads=[f"pb{b2}"], writes=[tvn + "b"])
                P.op("act", (lambda b3=b3, tv=tv: lambda e: e.activation(out=tv[:, 640:1152], in_=bank(b3), func=(AF.Copy if 's' in KSKIP else AF.Sigmoid)))(),
                     reads=[f"pb{b3}"], writes=[tvn + "c"])
                dma("sp", tmv_d[tok0:tok0 + 128, :], tv, reads=[tvn + "a", tvn + "b", tvn + "c"], writes=tres("tmv", tok0, 128))
                if 'q' not in KSKIP:
                    QKS = ["qk_sb_a", "qk_sb_b"]
                    P.op("dve", lambda e: e.tensor_tensor(out=sqq, in0=qk_sb, in1=qk_sb, op=ALU.mult), reads=QKS, writes=["sqq"])
                    P.op("dve", lambda e: e.reduce_sum(out=ss, in_=sqq.rearrange("p (h d) -> p h d", h=10), axis=AX.X),
                         reads=["sqq"], writes=["ss"])
                    P.op("act", lambda e: e.activation(out=ss, in_=ss, func=AF.Sqrt, scale=1.0 / 64, bias=cst[:, 0:1]),
                         reads=["ss"] + CST, writes=["ss"])
                    P.op("dve", lambda e: e.reciprocal(out=ss, in_=ss), reads=["ss"], writes=["ss"])
                    P.op("pool", lambda e: e.tensor_tensor(out=qn.rearrange("p (h d) -> p h d", h=10),
                                                           in0=qk_sb.rearrange("p (h d) -> p h d", h=10),
                                                           in1=ss.unsqueeze(2).to_broadcast([128, 10, 64]), op=ALU.mult),
                         reads=QKS + ["ss"], writes=["qn"])
                    P.op("pool", lambda e: e.tensor_tensor(out=qn, in0=qn, in1=qkg, op=ALU.mult), reads=["qn", "qkg"], writes=["qn"])
                    if r == 0:
                        gi = (tok0 - TC) // 128
                        q4 = qn.rearrange("p (h i two) -> p h i two", h=10, two=2)
                        o4 = qr.rearrange("p (h i two) -> p h i two", h=10, two=2)
                        x0, x1 = q4[:, :, :, 0], q4[:, :, :, 1]
                        cb = cosT[:, gi, :].unsqueeze(1).to_broadcast([128, 10, 32])
                        sb_ = sinT[:, gi, :].unsqueeze(1).to_broadcast([128, 10, 32])
                        ra3 = ra.rearrange("p (h i) -> p h i", h=10)
                        rb3 = rb.rearrange("p (h i) -> p h i", h=10)
                        P.op("pool", (lambda x0=x0, cb=cb: lambda e: e.tensor_tensor(out=ra3, in0=x0, in1=cb, op=ALU.mult))(),
                             reads=["qn", "cosT"], writes=["ra"])
                        P.op("pool", (lambda x1=x1, sb_=sb_: lambda e: e.tensor_tensor(out=rb3, in0=x1, in1=sb_, op=ALU.mult))(),
                             reads=["qn", "sinT"], writes=["rb"])
                        P.op("pool", (lambda o4=o4: lambda e: e.tensor_tensor(out=o4[:, :, :, 0], in0=ra3, in1=rb3, op=ALU.subtract))(),
                             reads=["ra", "rb"], writes=["qr0"])
                        P.op("pool", (lambda x0=x0, sb_=sb_: lambda e: e.tensor_tensor(out=ra3, in0=x0, in1=sb_, op=ALU.mult))(),
                             reads=["qn", "sinT", "qr0"], writes=["ra"])
                        P.op("pool", (lambda x1=x1, cb=cb: lambda e: e.tensor_tensor(out=rb3, in0=x1, in1=cb, op=ALU.mult))(),
                             reads=["qn", "cosT", "qr0"], writes=["rb"])
                        P.op("pool", (lambda o4=o4: lambda e: e.tensor_tensor(out=o4[:, :, :, 1], in0=ra3, in1=rb3, op=ALU.add))(),
                             reads=["ra", "rb"], writes=["qr1"])
                        QR = ["qr0", "qr1"]
                    else:
                        P.op("pool", lambda e: e.tensor_copy(out=qr, in_=qn), reads=["qn"], writes=["qr0"])
                        QR = ["qr0"]
                    tb = bank_bf(1)
                    for jq in range(5):
                        P.op("pe", (lambda jq=jq: lambda e: e.transpose(out=tb[:, jq * 128:(jq + 1) * 128],
                                                                        in_=qr[:, jq * 128:(jq + 1) * 128], identity=identB))(),
                             reads=QR + ["identB"], writes=["pb1"])
                    P.op("dve", (lambda ms=ms: lambda e: e.tensor_copy(out=QTst[:, :, ms],
                                                                       in_=tb[:, 0:512].rearrange("p (a b) -> p a b", a=4)))(),
                         reads=["pb1"], writes=["QTst"])
                    P.op("act", (lambda ms=ms: lambda e: e.activation(out=KTst[:, ms], in_=tb[:, 512:640], func=AF.Copy))(),
                         reads=["pb1"], writes=["KTst"])
            dma("sp", QT_d[:, :, t0:t0 + n].rearrange("j p t -> p j t"), QTst[:, :, 0:n], reads=["QTst"], writes=tres("QT", t0, n))
            dma("sp", KT_d[:, t0:t0 + n], KTst[:, 0:n], reads=["KTst"], writes=tres("KT", t0, n))
            if 'f' not in KSKIP:
                for cj in range(8):
                    b = rot.next()
                    for k in range(8):
                        P.op("pe", (lambda b=b, k=k, cj=cj: lambda e: e.matmul(
                            bank(b)[:, 0:n], lhsT=w_in[:, k, 768 + cj * 128:768 + (cj + 1) * 128], rhs=hT[:, k, 0:n],
                            start=(k == 0), stop=(k == 7)))(), reads=["hT", "w_in768"], writes=[f"pb{b}"])
                    if cj % 2 == 0:
                        P.op("act", (lambda b=b, cj=cj: lambda e: e.activation(out=fmst[:, cj, 0:n], in_=bank(b)[:, 0:n], func=AF.Copy))(),
                             reads=[f"pb{b}"], writes=[f"fmst{cj}"])
                    else:
                        P.op("dve", (lambda b=b, cj=cj: lambda e: e.tensor_copy(out=fmst[:, cj, 0:n], in_=bank(b)[:, 0:n]))(),
                             reads=[f"pb{b}"], writes=[f"fmst{cj}"])
                dma("sp", mqkT_d[:, :, t0:t0 + n].rearrange("j p t -> p j t"), fmst[:, :, 0:n],
                    reads=[f"fmst{cj}" for cj in range(8)], writes=tres("mqkT", t0, n))
                b = rot.next()
                for k in range(8):
                    P.op("pe", (lambda b=b, k=k: lambda e: e.matmul(bank(b)[:, 0:n], lhsT=w_in[:, k, 2816:2944], rhs=hT[:, k, 0:n],
                                                                  start=(k == 0), stop=(k == 7)))(),
                         reads=["hT", "w_in2816"], writes=[f"pb{b}"])
                P.op("act", (lambda b=b: lambda e: e.activation(out=gst[:, 0:n], in_=bank(b)[0:16, 0:n], func=AF.Identity,
                                                                bias=bgc[:, L:L + 1], scale=1.0))(),
                     reads=[f"pb{b}", "bgc"], writes=["gst"])
                dma("sp", gT_d[:, t0:t0 + n], gst[:, 0:n], reads=["gst"], writes=tres("gT", t0, n))
        for bi, (t0, n) in enumerate(blocks):
            do_block(bi, t0, n)
        P.barrier()

    WOFF = ARENA - 131072

    def w4_views():
        A = Alloc(WOFF)
        return A.get([8, FFN], BF16), A.get([32, D], BF16)

    def phase_attn(L, emit_ctx, prefetch):
        A = Alloc(PBASE)
        KTs = A.get([T], BF16)
        Vaug = A.get([NCH, 2, 128], BF16)
        QA = [[A.get([4, 512], BF16), A.get([4, 512], BF16)] for _ in range(2)]
        PT = [A.get([512], BF16) for _ in range(4)]
        rec = A.get([512], F32)
        attst = [A.get([4, 512], BF16), A.get([4, 512], BF16)]
        assert A.off <= WOFF
        if prefetch:
            w1, w2 = w4_views()
            for k in range(8):
                dma("pool", w1[:, k, :], w1_d[L, k * 128:(k + 1) * 128, :], writes=[f"w1_{k}"])
            for k4 in range(8):
                dma("pool", w2[:, k4 * 4:(k4 + 1) * 4, :],
                    w2_d[L, k4 * 512:(k4 + 1) * 512, :].rearrange("(k p) c -> p k c", p=128), writes=[f"w2_{k4}"])
        dma("sp", KTs, KT_d, writes=["KTs"])
        for bq in range(2):
            P.op("pool", (lambda bq=bq: lambda e: e.memset(QA[bq][0][64:128], 0.0))(), writes=[f"QAz{bq}0"])
            P.op("pool", (lambda bq=bq: lambda e: e.memset(QA[bq][1][0:64], 0.0))(), writes=[f"QAz{bq}1"])
        P.op("pool", lambda e: e.memset(Vaug, 1.0), writes=["Vaug"])
        vsrc = tmv_d.rearrange("(c p) f -> p c f", p=128)
        dma("sp", Vaug[:, :, 0, 0:64], vsrc[:, :, 0:64], reads=["Vaug"], writes=["Vaug0"])
        dma("sp", Vaug[:, :, 1, 64:128], vsrc[:, :, 64:128], reads=["Vaug"], writes=["Vaug1"])
        srot = Rot([0, 1, 2, 3])
        orot = Rot([4, 5])
        prot = Rot([0, 1, 2, 3])
        qblocks = [(256 + 512 * i, 512, list(range(NCH))) for i in range(8)]
        if emit_ctx:
            qblocks = [(0, 256, [0, 1])] + qblocks
        def do_q(qi, t0, n, kbs):
            qa, qbn = QA[qi % 2], f"QA{qi % 2}"
            ast, astn = attst[qi % 2], f"attst{qi % 2}"
            dma("sp", qa[0][0:64, :, 0:n], QT_d[:, 0:64, t0:t0 + n].rearrange("j p t -> p j t"), reads=[f"QAz{qi % 2}0"], writes=[qbn + "h0"])
            dma("sp", qa[1][64:128, :, 0:n], QT_d[:, 64:128, t0:t0 + n].rearrange("j p t -> p j t"), reads=[f"QAz{qi % 2}1"], writes=[qbn + "h1"])
            SK = 2
            its = []
            for j in range(4):
                for hh in range(2):
                    ob = orot.next()
                    for ki, kb in enumerate(kbs):
                        its.append((j, hh, ob, ki, kb))
            nk = len(kbs)
            slots = {}

            def emit_S(i):
                j, hh, ob, ki, kb = its[i]
                sbk = srot.next()
                pi = prot.next()
                pt = PT[pi]
                slots[i] = (pi, pt)
                P.op("pe", (lambda sbk=sbk, kb=kb, hh=hh, j=j: lambda e: e.matmul(
                    bank(sbk)[:, 0:n], lhsT=KTs[:, kb * 128:(kb + 1) * 128], rhs=qa[hh][:, j, 0:n], start=True, stop=True))(),
                    reads=["KTs", qbn + f"h{hh}"], writes=[f"pb{sbk}"])
                P.op("act", (lambda sbk=sbk, pt=pt: lambda e: e.activation(out=pt[:, 0:n], in_=bank(sbk)[:, 0:n], func=AF.Exp,
                                                                           scale=0.125))(),
                     reads=[f"pb{sbk}"], writes=[f"PT{pi}"])

            def emit_PV(i):
                j, hh, ob, ki, kb = its[i]
                pi, pt = slots.pop(i)
                hs = slice(hh * 64, hh * 64 + 64)
                os_ = slice((1 - hh) * 64, (1 - hh) * 64 + 64)
                P.op("pe", (lambda ob=ob, kb=kb, hh=hh, pt=pt, ki=ki: lambda e: e.matmul(
                    bank(ob)[:, 0:n], lhsT=Vaug[:, kb, hh, :], rhs=pt[:, 0:n], start=(ki == 0), stop=(ki == nk - 1)))(),
                    reads=["Vaug0", "Vaug1", f"PT{pi}"], writes=[f"pb{ob}"])
                if ki == nk - 1:
                    P.op("dve", (lambda ob=ob, hs=hs, os_=os_: lambda e: e.reciprocal(out=rec[hs, 0:n], in_=bank(ob)[os_, 0:n]))(),
                         reads=[f"pb{ob}"], writes=["rec"])
                    P.op("dve", (lambda ob=ob, hs=hs, j=j: lambda e: e.tensor_tensor(
                        out=ast[hs, j, 0:n], in0=bank(ob)[hs, 0:n], in1=rec[hs, 0:n], op=ALU.mult))(),
                        reads=[f"pb{ob}", "rec"], writes=[astn])

            for i in range(len(its) + SK):
                if i < len(its):
                    emit_S(i)
                if i - SK >= 0:
                    emit_PV(i - SK)
            dma("sp", attT_d[0:4, :, t0:t0 + n].rearrange("j p t -> p j t"), ast[:, :, 0:n], reads=[astn],
                writes=tres("attT_a", t0, n))
        for qi, (t0, n, kbs) in enumerate(qblocks):
            do_q(qi, t0, n, kbs)
        P.barrier()

    def phase_mlstm(L, emit_ctx):
        ATOP = Alloc(PBASE)
        Rr128 = [ATOP.get([NCH, 2, 128], BF16), ATOP.get([NCH, 2, 128], BF16)]
        Rr = [Rr128[0][0:4], Rr128[1][0:4]]
        emtT = [ATOP.get([NCH, 4], F32), ATOP.get([NCH, 4], F32)]
        decbc = [ATOP.get([4, NCH], F32), ATOP.get([4, NCH], F32)]
        PB2 = ATOP.off
        A = Alloc(PB2)
        Gi = [A.get([T], F32, parts=4), A.get([T], F32, parts=4)]
        Gf = [A.get([T], F32, parts=4), A.get([T], F32, parts=4)]
        zer = A.get([T], F32, parts=4)
        Nb = A.get([T], F32, parts=4)
        Ut = A.get([T], F32, parts=4)
        tmp128 = A.get([T], F32)
        tmp = tmp128[0:4]
        Uend = A.get([NCH], F32, parts=4)
        Uprev = A.get([NCH], F32, parts=4)
        dec128 = A.get([NCH], F32)
        dec = dec128[0:4]
        dma("sp", Gi[0], gT_d[0:4, :], writes=["Gi0"])
        dma("sp", Gf[0], gT_d[4:8, :], writes=["Gf0"])
        dma("sp", Gi[1], gT_d[8:12, :], writes=["Gi1"])
        dma("sp", Gf[1], gT_d[12:16, :], writes=["Gf1"])
        P.op("pool", lambda e: e.memset(zer, 0.0), writes=["zer"])
        P.op("pool", lambda e: e.memset(tmp128, 0.0), writes=["tmp"])
        P.op("pool", lambda e: e.memset(dec128, 0.0), writes=["dec"])
        P.op("pool", lambda e: e.memset(Rr128[0], 0.0), writes=["Rr0a", "Rr0b"])
        P.op("pool", lambda e: e.memset(Rr128[1], 0.0), writes=["Rr1a", "Rr1b"])
        c1 = cst[0:4, 1:2]
        for d in range(2):
            gi, gf = Gi[d], Gf[d]
            gin, gfn = f"Gi{d}", f"Gf{d}"
            P.op("act", (lambda gf=gf: lambda e: e.activation(out=gf, in_=gf, func=AF.Exp, scale=-1.0))(), reads=[gfn], writes=[gfn])
            P.op("act", (lambda gf=gf: lambda e: e.activation(out=gf, in_=gf, func=AF.Ln, bias=c1, scale=1.0))(),
                 reads=[gfn] + CST, writes=[gfn])

            def seg(ap, lo, hi, d=d):
                v = ap[:, lo:hi]
                return v[:, ::-1] if d == 1 else v
            if d == 0:
                P.op("dve", (lambda gf=gf: lambda e: e.tensor_tensor_scan(out=Nb, data0=gf, data1=zer, initial=0.0,
                                                                          op0=ALU.add, op1=ALU.add))(),
                     reads=[gfn, "zer"], writes=["Nb"])
            else:
                P.op("dve", (lambda gf=gf: lambda e: e.tensor_tensor_scan(out=seg(Nb, 0, TC), data0=seg(gf, 0, TC), data1=zer[:, 0:TC],
                                                                          initial=0.0, op0=ALU.add, op1=ALU.add))(),
                     reads=[gfn, "zer"], writes=["Nb"])
                P.op("dve", (lambda gf=gf: lambda e: e.tensor_tensor_scan(out=seg(Nb, TC, T), data0=seg(gf, TC, T), data1=zer[:, TC:T],
                                                                          initial=Nb[:, 0:1], op0=ALU.add, op1=ALU.add))(),
                     reads=[gfn, "zer", "Nb"], writes=["Nb"])
            P.op("dve", (lambda gi=gi: lambda e: e.tensor_tensor(out=gi, in0=gi, in1=Nb, op=ALU.add))(), reads=[gin, "Nb"], writes=[gin])
            if d == 0:
                P.op("dve", (lambda gi=gi: lambda e: e.tensor_tensor_scan(out=Ut, data0=gi, data1=zer, initial=0.0,
                                                                          op0=ALU.max, op1=ALU.max))(),
                     reads=[gin, "zer"], writes=["Ut"])
            else:
                P.op("dve", (lambda gi=gi: lambda e: e.tensor_tensor_scan(out=seg(Ut, 0, TC), data0=seg(gi, 0, TC), data1=zer[:, 0:TC],
                                                                          initial=0.0, op0=ALU.max, op1=ALU.max))(),
                     reads=[gin, "zer"], writes=["Ut"])
                P.op("dve", (lambda gi=gi: lambda e: e.tensor_tensor_scan(out=seg(Ut, TC, T), data0=seg(gi, TC, T), data1=zer[:, TC:T],
                                                                          initial=Ut[:, 0:1], op0=ALU.max, op1=ALU.max))(),
                     reads=[gin, "zer", "Ut"], writes=["Ut"])
            U3 = Ut.rearrange("p (c t) -> p c t", t=128)
            endcol = 127 if d == 0 else 0
            P.op("dve", (lambda endcol=endcol: lambda e: e.tensor_copy(out=Uend, in_=U3[:, :, endcol]))(), reads=["Ut"], writes=["Uend"])
            P.op("pool", lambda e: e.memset(Uprev, 0.0), reads=[], writes=["Uprev"])
            if d == 0:
                P.op("dve", lambda e: e.tensor_copy(out=Uprev[:, 1:NCH], in_=Uend[:, 0:NCH - 1]), reads=["Uend", "Uprev"], writes=["Uprev"])
            else:
                P.op("dve", lambda e: e.tensor_copy(out=Uprev[:, 2:NCH - 1], in_=Uend[:, 3:NCH]), reads=["Uend", "Uprev"], writes=["Uprev"])
                P.op("dve", lambda e: e.tensor_copy(out=Uprev[:, NCH - 1:NCH], in_=Uend[:, 0:1]), reads=["Uend", "Uprev"], writes=["Uprev"])
                P.op("dve", lambda e: e.tensor_copy(out=Uprev[:, 0:1], in_=Uend[:, 1:2]), reads=["Uend", "Uprev"], writes=["Uprev"])
            upb = Uprev.unsqueeze(2).to_broadcast([4, NCH, 128])
            t3 = tmp.rearrange("p (c t) -> p c t", t=128)
            g3 = gi.rearrange("p (c t) -> p c t", t=128)
            Rd = Rr[d]
            P.op("dve", (lambda g3=g3: lambda e: e.tensor_tensor(out=t3, in0=g3, in1=upb, op=ALU.subtract))(),
                 reads=[gin, "Uprev"], writes=["tmp"])
            P.op("act", (lambda Rd=Rd: lambda e: e.activation(out=Rd[:, :, 0, :], in_=t3, func=AF.Exp, bias=cst[0:4, 2:3], scale=1.0))(),
                 reads=["tmp"] + CST, writes=[f"Rr{d}a"])
            P.op("dve", lambda e: e.tensor_tensor(out=t3, in0=upb, in1=U3, op=ALU.subtract), reads=["Ut", "Uprev", f"Rr{d}a"], writes=["tmp"])
            P.op("act", (lambda Rd=Rd: lambda e: e.activation(out=Rd[:, :, 1, :], in_=t3, func=AF.Exp))(),
                 reads=["tmp"], writes=[f"Rr{d}b"])
            P.op("dve", lambda e: e.tensor_tensor(out=tmp, in0=Nb, in1=Ut, op=ALU.subtract), reads=["Nb", "Ut", f"Rr{d}b"], writes=["tmp"])
            P.op("act", lambda e: e.activation(out=tmp, in_=tmp, func=AF.Exp), reads=["tmp"], writes=["tmp"])
            eb = 6 + d
            for c in range(NCH):
                P.op("pe", (lambda c=c, eb=eb: lambda e: e.matmul(bank(eb)[:, c * 4:c * 4 + 4], lhsT=tmp128[:, c * 128:(c + 1) * 128],
                                                                  rhs=identF[:, 0:4], start=True, stop=True))(),
                     reads=["tmp", "identF"], writes=[f"pb{eb}"])
            P.op("dve", (lambda d=d, eb=eb: lambda e: e.tensor_copy(out=emtT[d], in_=bank(eb)[:, 0:NCH * 4].rearrange("p (c h) -> p c h", h=4)))(),
                 reads=[f"pb{eb}"], writes=[f"emtT{d}"])
            P.op("dve", lambda e: e.tensor_tensor(out=dec, in0=Uprev, in1=Uend, op=ALU.subtract), reads=["Uprev", "Uend"], writes=["dec"])
            P.op("act", lambda e: e.activation(out=dec, in_=dec, func=AF.Exp), reads=["dec"], writes=["dec"])
            db = 4 + d
            for h in range(4):
                P.op("pe", (lambda h=h, db=db: lambda e: e.matmul(bank(db)[:, h * NCH:(h + 1) * NCH], lhsT=selF[:, h, :], rhs=dec128,
                                                                  start=True, stop=True))(),
                     reads=["dec", "selF"], writes=[f"pb{db}"])
            P.op("dve", (lambda d=d, db=db: lambda e: e.tensor_copy(out=decbc[d], in_=bank(db)[:, 0:4 * NCH].rearrange("p (h c) -> p h c", h=4)))(),
                 reads=[f"pb{db}"], writes=[f"decbc{d}"])
        P.barrier()

        A = Alloc(PB2)
        qkT = A.get([8, T], BF16)
        vaug = A.get([NCH, 4, 129], BF16)
        Cst = A.get([8, 129], F32)
        Cbf = A.get([8, 129], BF16)
        PB3 = A.off
        A2 = Alloc(PB3)
        PL = 2048
        xc = [A2.get([PL + 2], F32), A2.get([PL + 2], F32)]
        acc = A2.get([PL], F32)
        P.op("pool", lambda e: e.memset(vaug, 1.0), writes=["vaug"])
        for h in range(4):
            dma("sp", vaug[:, :, h, 0:128], tmv_d[:, 128 + h * 128:256 + h * 128].rearrange("(c p) d -> p c d", p=128),
                reads=["vaug"], writes=[f"vaugv{h}"])
        P.op("pool", lambda e: e.memset(Cst, 0.0), writes=["Cst"])
        P.op("pool", lambda e: e.memset(Cbf, 0.0), writes=["Cbf"])
        pieces = [(0, TC, 0, TC), (TC, TC + PL, TC, T), (TC + PL, T, TC, T)]
        it = 0
        for cr in range(8):
            w = [convc[:, (L * 3 + a) * 8 + cr:(L * 3 + a) * 8 + cr + 1] for a in range(3)]
            for (a_, b_, sa, sb) in pieces:
                ln_ = b_ - a_
                lh = a_ > sa
                rh = b_ < sb
                x_, xn_ = xc[it % 2], f"xc{it % 2}"
                it += 1
                lo = a_ - 1 if lh else a_
                hi = b_ + 1 if rh else b_
                c0 = 0 if lh else 1
                dma("sp", x_[:, c0:c0 + hi - lo], mqkT_d[cr, :, lo:hi], writes=[xn_])
                P.op("pool", (lambda x_=x_, w=w, ln_=ln_: lambda e: e.tensor_scalar_mul(out=acc[:, 0:ln_], in0=x_[:, 1:ln_ + 1], scalar1=w[1]))(),
                     reads=[xn_], writes=["acc"])
                o0 = 0 if lh else 1
                P.op("dve", (lambda x_=x_, w=w, o0=o0, ln_=ln_: lambda e: e.scalar_tensor_tensor(
                    out=acc[:, o0:ln_], in0=x_[:, o0:ln_], scalar=w[0], in1=acc[:, o0:ln_], op0=ALU.mult, op1=ALU.add))(),
                    reads=[xn_, "acc"], writes=["acc"])
                o1 = ln_ if rh else ln_ - 1
                P.op("dve", (lambda x_=x_, w=w, o1=o1: lambda e: e.scalar_tensor_tensor(
                    out=acc[:, 0:o1], in0=x_[:, 2:o1 + 2], scalar=w[2], in1=acc[:, 0:o1], op0=ALU.mult, op1=ALU.add))(),
                    reads=[xn_, "acc"], writes=["acc"])
                P.op("act", (lambda cr=cr, a_=a_, b_=b_, ln_=ln_: lambda e: e.activation(out=qkT[:, cr, a_:b_], in_=acc[:, 0:ln_], func=AF.Silu))(),
                     reads=["acc"], writes=[f"qkT{cr}"])
        P.barrier()

        A3 = Alloc(PB3)
        NR = 8
        kTs = [A3.get([128], BF16) for _ in range(NR)]
        qTs = [A3.get([128], BF16) for _ in range(NR)]
        Sw = [A3.get([128], BF16) for _ in range(NR)]
        k2 = [A3.get([128], BF16) for _ in range(NR)]
        dmx = [A3.get([2], F32) for _ in range(NR)]
        hst = [[A3.get([4, 128], F32) for _ in range(2)] for _ in range(2)]
        order = [list(range(NCH)), [1, 0] + list(range(NCH - 1, 1, -1))]
        hcnt = [0, 0]

        def do_step(i):
            ch = []
            for d in range(2):
                c = order[d][i]
                emit_out = emit_ctx or c >= 2
                hs_, hsn = hst[d][hcnt[d] % 2], f"hst{d}{hcnt[d] % 2}"
                for h in range(4):
                    ch.append((d, h, d * 4 + h, c, emit_out, hs_, hsn))
            for (d, h, x, c, eo, hs_, hsn) in ch:
                Rd = Rr128[d]
                P.op("pe", (lambda x=x, h=h, c=c, Rd=Rd: lambda e: e.matmul(
                    bank(x)[:, 0:256], lhsT=selB[:, h, :], rhs=Rd[:, c, :, :].rearrange("p a b -> p (a b)"),
                    start=True, stop=True))(), reads=[], writes=[f"pb{x}"])
            for (d, h, x, c, eo, hs_, hsn) in ch:
                ts = slice(c * 128, (c + 1) * 128)
                P.op("dve", (lambda x=x, h=h, ts=ts: lambda e: e.tensor_tensor(
                    out=kTs[x], in0=qkT[:, 4 + h, ts], in1=bank(x)[:, 0:128], op=ALU.mult))(),
                    reads=[f"pb{x}"], writes=[f"kTs{x}"])
                P.op("dve", (lambda x=x, h=h, ts=ts: lambda e: e.tensor_tensor(
                    out=qTs[x], in0=qkT[:, h, ts], in1=bank(x)[:, 128:256], op=ALU.mult))(),
                    reads=[f"pb{x}"], writes=[f"qTs{x}"])
            for (d, h, x, c, eo, hs_, hsn) in ch:
                P.op("pe", (lambda x=x: lambda e: e.matmul(bank(x)[:, 256:384], lhsT=kTs[x], rhs=qTs[x], start=True, stop=True))(),
                     reads=[f"kTs{x}", f"qTs{x}"], writes=[f"pb{x}"])
                P.op("pe", (lambda x=x: lambda e: e.transpose(out=bank_bf(x)[:, 768:896], in_=kTs[x], identity=identB))(),
                     reads=[f"kTs{x}"], writes=[f"pb{x}"])
            for (d, h, x, c, eo, hs_, hsn) in ch:
                P.op("dve", (lambda x=x, d=d: lambda e: e.tensor_tensor(
                    out=Sw[x], in0=bank(x)[:, 256:384], in1=maskT[d], op=ALU.mult))(),
                    reads=[f"pb{x}"], writes=[f"Sw{x}"])
                P.op("act", (lambda x=x, d=d, h=h, c=c: lambda e: e.activation(
                    out=k2[x], in_=bank_bf(x)[:, 768:896], func=AF.Identity, scale=decbc[d][:, h, c:c + 1]))(),
                    reads=[f"pb{x}"], writes=[f"k2{x}"])
            for (d, h, x, c, eo, hs_, hsn) in ch:
                if eo:
                    P.op("pe", (lambda x=x: lambda e: e.matmul(
                        bank(x)[:, 0:129], lhsT=qTs[x], rhs=Cbf[:, x, :], start=True, stop=False))(),
                        reads=[f"qTs{x}", f"Cbf{x}"], writes=[f"pb{x}"])
                    P.op("pe", (lambda x=x, c=c, h=h: lambda e: e.matmul(
                        bank(x)[:, 0:129], lhsT=Sw[x], rhs=vaug[:, c, h, :], start=False, stop=True))(),
                        reads=[f"Sw{x}"], writes=[f"pb{x}"])
                P.op("pe", (lambda x=x, c=c, h=h: lambda e: e.matmul(
                    bank(x)[:, 129:258], lhsT=k2[x], rhs=vaug[:, c, h, :], start=True, stop=True))(),
                    reads=[f"k2{x}"], writes=[f"pb{x}"])
            for (d, h, x, c, eo, hs_, hsn) in ch:
                if eo:
                    P.op("act", (lambda x=x: lambda e: e.activation(
                        out=dmx[x][:, 0:1], in_=bank(x)[:, 128:129], func=AF.Abs))(), reads=[f"pb{x}"], writes=[f"dmx{x}"])
                    P.op("dve", (lambda x=x, d=d, c=c, h=h: lambda e: e.tensor_tensor(
                        out=dmx[x][:, 0:1], in0=dmx[x][:, 0:1], in1=emtT[d][:, c, h:h + 1], op=ALU.max))(),
                        reads=[f"dmx{x}"], writes=[f"dmx{x}"])
                    P.op("dve", (lambda x=x: lambda e: e.reciprocal(out=dmx[x][:, 1:2], in_=dmx[x][:, 0:1]))(),
                         reads=[f"dmx{x}"], writes=[f"dmr{x}"])
                    P.op("act", (lambda x=x, h=h, hs_=hs_: lambda e: e.activation(
                        out=hs_[:, h, :], in_=bank(x)[:, 0:128], func=AF.Identity, scale=dmx[x][:, 1:2]))(),
                        reads=[f"pb{x}", f"dmr{x}"], writes=[hsn + str(h)])
                P.op("dve", (lambda x=x, d=d, h=h, c=c: lambda e: e.scalar_tensor_tensor(
                    out=Cst[:, x, :], in0=Cst[:, x, :], scalar=decbc[d][:, h, c:c + 1], in1=bank(x)[:, 129:258],
                    op0=ALU.mult, op1=ALU.add))(), reads=[f"pb{x}", f"Cst{x}"], writes=[f"Cst{x}"])
                P.op("pool", (lambda x=x: lambda e: e.tensor_copy(out=Cbf[:, x, :], in_=Cst[:, x, :]))(),
                     reads=[f"Cst{x}"], writes=[f"Cbf{x}"])
            for d in range(2):
                c = order[d][i]
                if emit_ctx or c >= 2:
                    hs_, hsn = hst[d][hcnt[d] % 2], f"hst{d}{hcnt[d] % 2}"
                    hd = hf_d if d == 0 else hb_d
                    dma("sp", hd[c * 128:(c + 1) * 128, :], hs_.rearrange("p h d -> p (h d)"),
                        reads=[hsn + str(h) for h in range(4)], writes=tres("hfb%d" % d, c * 128, 128))
                    hcnt[d] += 1

        for i in range(NCH):
            do_step(i)
        P.barrier()

        A4 = Alloc(PB2)
        mlg = A4.get([512], F32)
        hfa = [A4.get([512], F32), A4.get([512], F32)]
        hba = [A4.get([512], F32), A4.get([512], F32)]
        mo = [A4.get([512], BF16), A4.get([512], BF16)]
        sq4 = A4.get([512], F32)
        s4 = A4.get([4], F32)
        memb = A4.get([512], BF16)
        memst = [A4.get([4, 128], BF16), A4.get([4, 128], BF16)]
        dma("sp", mlg, mlg_d[:, L, :], writes=["mlg"])
        trot = Rot([0, 1, 2, 3])
        gs = list(range(0 if emit_ctx else 2, NCH))
        for gi_, g in enumerate(gs):
            p2 = gi_ % 2
            ha, hbv, mov, mst = hfa[p2], hba[p2], mo[p2], memst[p2]
            rows = slice(g * 128, (g + 1) * 128)
            dma("sp", ha, hf_d[rows, :], writes=[f"hfa{p2}"])
            dma("sp", hbv, hb_d[rows, :], writes=[f"hba{p2}"])
            dma("sp", mov, tmv_d[rows, 640:1152], writes=[f"mo{p2}"])
            P.op("dve", (lambda ha=ha, hbv=hbv: lambda e: e.tensor_tensor(out=ha, in0=ha, in1=hbv, op=ALU.add))(),
                 reads=[f"hfa{p2}", f"hba{p2}"], writes=[f"hfa{p2}"])
            P.op("pool", (lambda ha=ha: lambda e: e.tensor_tensor(out=sq4, in0=ha, in1=ha, op=ALU.mult))(), reads=[f"hfa{p2}"], writes=["sq4"])
            P.op("dve", lambda e: e.reduce_sum(out=s4, in_=sq4.rearrange("p (h d) -> p h d", h=4), axis=AX.X), reads=["sq4"], writes=["s4"])
            P.op("act", lambda e: e.activation(out=s4, in_=s4, func=AF.Sqrt, scale=1.0 / 128, bias=cst[:, 0:1]), reads=["s4"] + CST, writes=["s4"])
            P.op("dve", lambda e: e.reciprocal(out=s4, in_=s4), reads=["s4"], writes=["s4"])
            P.op("dve", (lambda ha=ha: lambda e: e.tensor_tensor(out=ha.rearrange("p (h d) -> p h d", h=4),
                                                                in0=ha.rearrange("p (h d) -> p h d", h=4),
                                                                in1=s4.unsqueeze(2).to_broadcast([128, 4, 128]), op=ALU.mult))(),
                 reads=[f"hfa{p2}", "s4"], writes=[f"hfa{p2}"])
            P.op("pool", (lambda ha=ha: lambda e: e.tensor_tensor(out=ha, in0=ha, in1=mlg, op=ALU.mult))(), reads=[f"hfa{p2}", "mlg"], writes=[f"hfa{p2}"])
            P.op("pool", (lambda ha=ha, mov=mov: lambda e: e.tensor_tensor(out=memb, in0=ha, in1=mov, op=ALU.mult))(),
                 reads=[f"hfa{p2}", f"mo{p2}"], writes=["memb"])
            tb = trot.next()
            for h in range(4):
                P.op("pe", (lambda tb=tb, h=h: lambda e: e.transpose(out=bank_bf(tb)[:, h * 128:(h + 1) * 128],
                                                                    in_=memb[:, h * 128:(h + 1) * 128], identity=identB))(),
                     reads=["memb", "identB"], writes=[f"pb{tb}"])
            P.op("act", (lambda tb=tb, mst=mst: lambda e: e.activation(out=mst, in_=bank_bf(tb)[:, 0:512].rearrange("p (a b) -> p a b", a=4),
                                                                       func=AF.Copy))(), reads=[f"pb{tb}"], writes=[f"memst{p2}"])
            dma("sp", attT_d[4:8, :, g * 128:(g + 1) * 128].rearrange("j p t -> p j t"), mst, reads=[f"memst{p2}"],
                writes=tres("attT_m", g * 128, 128))
        P.barrier()

    def phase_mlp(L, emit_ctx, prefetched):
        w1, w2 = w4_views()
        if not prefetched:
            for k in range(8):
                dma("pool", w1[:, k, :], w1_d[L, k * 128:(k + 1) * 128, :], writes=[f"w1_{k}"])
            for k4 in range(8):
                dma("pool", w2[:, k4 * 4:(k4 + 1) * 4, :],
                    w2_d[L, k4 * 512:(k4 + 1) * 512, :].rearrange("(k p) c -> p k c", p=128), writes=[f"w2_{k4}"])
        W1 = [f"w1_{k}" for k in range(8)]
        W2 = [f"w2_{k}" for k in range(8)]
        A = Alloc(PBASE)
        NB = 256
        w_out = A.get([8, D], BF16)
        dma("pool", w_out, w_out_d[L].rearrange("(k p) c -> p k c", p=128), writes=["w_out"])
        attTb = [A.get([8, NB], BF16)]
        xTb = [A.get([8, NB], F32), A.get([8, NB], F32)]
        h2T = A.get([8, NB], BF16)
        sq = [A.get([NB], BF16), A.get([NB], BF16)]
        rstd = A.get([NB], F32)
        xn = [A.get([NB], F32), A.get([NB], F32)]
        rl = [A.get([NB], BF16), A.get([NB], BF16)]
        uT = A.get([32, NB], BF16)
        assert A.off <= WOFF, A.off
        rot = Rot([1, 2, 3, 4, 5, 6, 7])
        t0s = list(range(0 if emit_ctx else TC, T, NB))
        def do_block(bi, t0):
            n = NB
            r = 1 if t0 < TC else 0
            ab, abn = attTb[0], "attTb0"
            xb, xbn = xTb[bi % 2], f"xTb{bi % 2}"
            dma("sp", ab, attT_d[:, :, t0:t0 + n].rearrange("j p t -> p j t"), writes=[abn])
            dma("sp", xb, xT_d[:, :, t0:t0 + n].rearrange("j p t -> p j t"), writes=[xbn])
            for cj in range(8):
                b = rot.next()
                for k in range(8):
                    P.op("pe", (lambda b=b, k=k, cj=cj, ab=ab: lambda e: e.matmul(
                        bank(b)[:, 0:n], lhsT=w_out[:, k, cj * 128:(cj + 1) * 128], rhs=ab[:, k, :], start=(k == 0), stop=(k == 7)))(),
                        reads=["w_out", abn], writes=[f"pb{b}"])
                P.op("dve", (lambda b=b, cj=cj, xb=xb, r=r: lambda e: e.scalar_tensor_tensor(
                    out=xb[:, cj, :], in0=bank(b)[:, 0:n], scalar=modcol(L, 16 + cj, r), in1=xb[:, cj, :], op0=ALU.mult, op1=ALU.add))(),
                    reads=[f"pb{b}", xbn, "mod"], writes=[xbn])
            norm_mod(xb, xbn, n, sq, rstd, xn, h2T, "h2T",
                     lambda j: s2[:, L * 8 + j, r:r + 1], lambda j: modcol(L, 24 + j, r), 0)
            for fc in range(32):
                b = rot.next()
                for k in range(8):
                    P.op("pe", (lambda b=b, k=k, fc=fc: lambda e: e.matmul(
                        bank(b)[:, 0:n], lhsT=w1[:, k, fc * 128:(fc + 1) * 128], rhs=h2T[:, k, :], start=(k == 0), stop=(k == 7)))(),
                        reads=[f"w1_{k}", "h2T"], writes=[f"pb{b}"])
                r_, rn_ = rl[fc % 2], f"rl{fc % 2}"
                P.op("act", (lambda b=b, r_=r_: lambda e: e.activation(out=r_, in_=bank(b)[:, 0:n], func=AF.Relu))(),
                     reads=[f"pb{b}"], writes=[rn_])
                P.op("pool", (lambda r_=r_, fc=fc: lambda e: e.tensor_tensor(out=uT[:, fc, :], in0=r_, in1=r_, op=ALU.mult))(),
                     reads=[rn_], writes=[f"uT{fc}"])
            for cj in range(8):
                b = rot.next()
                for fc in range(32):
                    P.op("pe", (lambda b=b, fc=fc, cj=cj: lambda e: e.matmul(
                        bank(b)[:, 0:n], lhsT=w2[:, fc, cj * 128:(cj + 1) * 128], rhs=uT[:, fc, :], start=(fc == 0), stop=(fc == 31)))(),
                        reads=[f"w2_{fc // 4}", f"uT{fc}"], writes=[f"pb{b}"])
                P.op("dve", (lambda b=b, cj=cj, xb=xb, r=r: lambda e: e.scalar_tensor_tensor(
                    out=xb[:, cj, :], in0=bank(b)[:, 0:n], scalar=modcol(L, 40 + cj, r), in1=xb[:, cj, :], op0=ALU.mult, op1=ALU.add))(),
                    reads=[f"pb{b}", xbn, "mod"], writes=[xbn])
            dma("sp", xT_d[:, :, t0:t0 + n].rearrange("j p t -> p j t"), xb, reads=[xbn], writes=tres("xTo", t0, n))
        for bi, t0 in enumerate(t0s):
            do_block(bi, t0)
        P.barrier()

    def phase_final():
        A = Alloc(PBASE)
        xTb = [A.get([8, 512], F32), A.get([8, 512], F32)]
        sq = [A.get([512], BF16), A.get([512], BF16)]
        rstd = A.get([512], F32)
        xnf = A.get([8, 512], F32)
        ost = [A.get([1024], F32), A.get([1024], F32)]
        rot = Rot([1, 2, 3, 4, 5, 6])
        oc = 0
        def do_block(bi):
            nonlocal oc
            t0 = TC + bi * 512
            n = 512
            xb, xbn = xTb[bi % 2], f"xTb{bi % 2}"
            dma("sp", xb, xT_d[:, :, t0:t0 + n].rearrange("j p t -> p j t"), writes=[xbn])
            ssps = bank(0)
            for j in range(8):
                s_, sn = sq[j % 2], f"sq{j % 2}"
                P.op("act", (lambda s_=s_, j=j, xb=xb: lambda e: e.activation(out=s_, in_=xb[:, j, :], func=AF.Square))(), reads=[xbn], writes=[sn])
                P.op("pe", (lambda s_=s_, j=j: lambda e: e.matmul(ssps, lhsT=onesB, rhs=s_, start=(j == 0), stop=(j == 7)))(),
                     reads=[sn, "onesB"], writes=["pb0"])
            P.op("act", lambda e: e.activation(out=rstd, in_=ssps, func=AF.Sqrt, scale=1.0 / D, bias=cst[:, 0:1]), reads=["pb0"] + CST, writes=["rstd"])
            P.op("dve", lambda e: e.reciprocal(out=rstd, in_=rstd), reads=["rstd"], writes=["rstd"])
            for j in range(8):
                P.op("dve", (lambda j=j, xb=xb: lambda e: e.scalar_tensor_tensor(
                    out=xnf[:, j, :], in0=xb[:, j, :], scalar=nfin[:, j:j + 1], in1=rstd, op0=ALU.mult, op1=ALU.mult))(),
                    reads=[xbn, "rstd", "nfin"], writes=[f"xnf{j}"])
            for m in range(4):
                o_, on_ = ost[oc % 2], f"ost{oc % 2}"
                oc += 1
                for half in range(2):
                    b = rot.next()
                    for jj in range(4):
                        j = half * 4 + jj
                        P.op("pe", (lambda b=b, jj=jj, j=j, m=m: lambda e: e.transpose(
                            out=bank(b)[:, jj * 128:(jj + 1) * 128], in_=xnf[:, j, m * 128:(m + 1) * 128], identity=identF))(),
                            reads=[f"xnf{j}", "identF"], writes=[f"pb{b}"])
                    if half == 0:
                        P.op("act", (lambda b=b, o_=o_: lambda e: e.activation(out=o_[:, 0:512], in_=bank(b), func=AF.Copy))(),
                             reads=[f"pb{b}"], writes=[on_ + "a"])
                    else:
                        P.op("dve", (lambda b=b, o_=o_: lambda e: e.tensor_copy(out=o_[:, 512:1024], in_=bank(b)))(),
                             reads=[f"pb{b}"], writes=[on_ + "b"])
                r0 = bi * 512 + m * 128
                dma("sp", out_d[r0:r0 + 128, :], o_, reads=[on_ + "a", on_ + "b"], writes=[f"out{r0}"])

        for bi in range(8):
            do_block(bi)

    steps = [("mod", phase_mod), ("xT", phase_xT)]
    for L in range(n_layers):
        emit_ctx = L < DEPTH - 1
        steps.append((f"inproj{L}", (lambda L=L: phase_inproj(L))))
        steps.append((f"mlstm{L}", (lambda L=L, ec=emit_ctx: phase_mlstm(L, ec))))
        steps.append((f"attn{L}", (lambda L=L, ec=emit_ctx: phase_attn(L, ec, prefetch=True))))
        steps.append((f"mlp{L}", (lambda L=L, ec=emit_ctx: phase_mlp(L, ec, prefetched=True))))
    steps.append(("final", phase_final))
    for name, fn in steps:
        fn()
        if stop is not None and name == stop:
            break
    P.emit(es)
    es.close()
    return nc, P


def _perm_att():
    idx = np.zeros(512, dtype=np.int64)
    for j in range(4):
        for p in range(128):
            head = j if p < 64 else j + 4
            idx[j * 128 + p] = head * 64 + (p % 64)
    return idx


def _rope_tables():
    rows = TL // 64
    row_idx = np.repeat(np.arange(rows, dtype=np.float32), 64)
    col_idx = np.tile(np.arange(64, dtype=np.float32), rows)
    inv_freq = np.power(np.float32(10000.0), -np.arange(0, 32, 2, dtype=np.float32) / np.float32(32)).astype(np.float32)
    ang = np.concatenate([row_idx[:, None] * inv_freq, col_idx[:, None] * inv_freq], axis=-1).astype(np.float32)
    cos = np.cos(ang).astype(np.float32).reshape(32, 128, 32).transpose(1, 0, 2)
    sin = np.sin(ang).astype(np.float32).reshape(32, 128, 32).transpose(1, 0, 2)
    return np.ascontiguousarray(cos), np.ascontiguousarray(sin)


def host_inputs(inputs, cores):
    f = lambda a: np.ascontiguousarray(np.asarray(a, dtype=np.float32))
    perm = _perm_att()
    w_in = f(inputs["w_in"])
    w_in_p = np.zeros((DEPTH, D, INCP), np.float32)
    w_in_p[:, :, 0:INC] = w_in
    w_in_p[:, :, 0:512] = w_in[:, :, perm]
    w_out = f(inputs["w_out"])
    w_out_p = w_out.copy()
    w_out_p[:, 0:512, :] = w_out[:, perm, :]
    cos, sin = _rope_tables()
    sel = np.zeros((128, 4, 128), np.float32)
    for h in range(4):
        sel[h, h, :] = 1.0
    s_ = np.arange(128)
    maskf = (s_[:, None] <= s_[None, :]).astype(np.float32)
    maskb = (s_[:, None] >= s_[None, :]).astype(np.float32)
    colL = lambda a, nj: np.ascontiguousarray(f(a).reshape(DEPTH, nj, 128).transpose(2, 0, 1))
    qk_gain = np.concatenate([np.tile(f(inputs["q_norm"]), (1, 8)), np.tile(f(inputs["k_norm"]), (1, 2))], axis=1)
    shared = {
        "w_ada": f(inputs["w_ada"]),
        "b_ada_c": colL(inputs["b_ada"], 48),
        "nmix_c": colL(inputs["norm_mix"], 8),
        "nmlp_c": colL(inputs["norm_mlp"], 8),
        "w_in_p": w_in_p,
        "bg_c": np.ascontiguousarray(f(inputs["b_gates"]).T),
        "conv_c": np.ascontiguousarray(f(inputs["conv_qk"]).reshape(DEPTH, 3, 8, 128).transpose(3, 0, 1, 2)),
        "qk_gain": np.ascontiguousarray(np.broadcast_to(qk_gain[None], (128, DEPTH, 640))),
        "ml_gain": np.ascontiguousarray(np.broadcast_to(f(inputs["mlstm_norm"])[None], (128, DEPTH, 512))),
        "w_out_p": w_out_p,
        "w_mlp_in": f(inputs["w_mlp_in"]),
        "w_mlp_out": f(inputs["w_mlp_out"]),
        "nfin_c": np.ascontiguousarray(f(inputs["norm_final"]).reshape(8, 128).T),
        "ident": np.eye(128, dtype=np.float32),
        "maskf": maskf,
        "maskb": maskb,
        "sel4": sel,
        "rope_cos": cos,
        "rope_sin": sin,
    }
    x = f(inputs["x"])
    ctx = f(inputs["ctx"])
    c = f(inputs["c"])
    c_ctx = f(inputs["c_ctx"])
    maps = []
    for b in cores:
        cc = np.stack([c[b].reshape(8, 128).T, c_ctx.reshape(8, 128).T], axis=-1)
        m = dict(shared)
        m["x"] = x[b]
        m["ctx"] = ctx[b]
        m["cc"] = np.ascontiguousarray(cc)
        maps.append(m)
    return maps


_NC_CACHE = {}


def kernel(**inputs):
    if "nc" not in _NC_CACHE:
        _NC_CACHE["nc"] = build()[0]
    nc = _NC_CACHE["nc"]
    maps = host_inputs(inputs, list(range(8)))
    res = run_bass_kernel_spmd(nc, maps, core_ids=list(range(8)))
    out = np.stack([np.asarray(r["out"], dtype=np.float32) for r in res.results], axis=0)
    return out
```

```python
import math
import os
KSKIP = os.environ.get('KSKIP', '')
from contextlib import ExitStack
import numpy as np
import concourse.bass as bass
import concourse.mybir as mybir
from concourse.bass_utils import run_bass_kernel_spmd

F32 = mybir.dt.float32
BF16 = mybir.dt.bfloat16
ALU = mybir.AluOpType
AF = mybir.ActivationFunctionType
AX = mybir.AxisListType

DEPTH = 4
D = 1024
TC = 256
TL = 4096
T = TC + TL
NCH = T // 128
INC = 2832
INCP = 2944
FFN = 4096
EPS = 1e-6
ARENA = 210944

CH = 2048
RROT = 8
NDMA = {"sp": 28, "pool": 20}
CENG = ("pe", "act", "dve", "pool")
ENGS = ("pe", "act", "dve", "pool", "sp")


class Res:
    __slots__ = ("writers", "readers")

    def __init__(self):
        self.writers = []
        self.readers = []


class Rec:
    __slots__ = ("eng", "fn", "idx", "cwaits", "dwaits", "signal", "is_dma", "dma_id", "sig_k")

    def __init__(self, eng, fn, is_dma):
        self.eng = eng
        self.fn = fn
        self.is_dma = is_dma
        self.cwaits = {}
        self.dwaits = {}
        self.signal = False
        self.dma_id = None
        self.sig_k = None


class Prog:
    def __init__(self, nc):
        self.nc = nc
        self.streams = {e: [] for e in ENGS}
        self.maxw = {e: {} for e in ENGS}
        self.dw = {e: set() for e in ENGS}
        self.ndma = {q: 0 for q in NDMA}
        self.res = {}
        self.last_real = {e: None for e in CENG}
        self.dma_live = {q: {} for q in NDMA}

    def R(self, name):
        r = self.res.get(name)
        if r is None:
            r = Res()
            self.res[name] = r
        return r

    def _need(self, rec, tok, same_ok):
        kind, e, i, prod = tok
        if kind == "c":
            if e == rec.eng and (same_ok or e == "pe"):
                return
            if self.maxw[rec.eng].get(e, -1) >= i:
                return
            if rec.cwaits.get(e, (-1, None))[0] < i:
                rec.cwaits[e] = (i, prod)
        else:
            key = (e, i)
            if key in self.dw[rec.eng]:
                return
            rec.dwaits[key] = prod

    def _commit(self, rec):
        for e, (i, prod) in rec.cwaits.items():
            self.maxw[rec.eng][e] = i
            prod.signal = True
        for key, prod in rec.dwaits.items():
            self.dw[rec.eng].add(key)

    def op(self, eng, fn, reads=(), writes=(), is_dma=False):
        rec = Rec(eng, fn, is_dma)
        st = self.streams[eng]
        rec.idx = len(st)
        psn = sorted({n for n in list(reads) + list(writes) if n.startswith("pb") or n == "modps"})
        reads = [self.R(r) for r in reads if r not in psn]
        writes = [self.R(r) for r in writes if r not in psn]
        psr = [self.R(n) for n in psn]
        for r in psr:
            for tok in r.writers:
                self._need(rec, tok, True)
        for r in reads:
            for tok in r.writers:
                self._need(rec, tok, False)
        for w in writes:
            for tok in w.writers:
                self._need(rec, tok, True)
            for tok in w.readers:
                self._need(rec, tok, True)
        self._commit(rec)
        if is_dma:
            rec.dma_id = self.ndma[eng]
            self.ndma[eng] += 1
            self.dma_live[eng][rec.dma_id % NDMA[eng]] = rec
            tok = ("d", eng, rec.dma_id, rec)
        else:
            tok = ("c", eng, rec.idx, rec)
            self.last_real[eng] = rec
        for r in reads:
            r.readers.append(tok)
        for w in writes:
            w.writers = [tok]
            w.readers = []
        for r in psr:
            r.writers = [tok]
            r.readers = []
        st.append(rec)
        return rec

    def barrier(self):
        for eng in ENGS:
            rec = Rec(eng, None, False)
            rec.idx = len(self.streams[eng])
            for e in CENG:
                lr = self.last_real[e]
                if lr is not None:
                    self._need(rec, ("c", e, lr.idx, lr), True)
            for q in NDMA:
                for slot, d in self.dma_live[q].items():
                    self._need(rec, ("d", q, d.dma_id, d), True)
            self._commit(rec)
            self.streams[eng].append(rec)
        self.res = {}

    def emit(self, es):
        nc = self.nc
        sems = {e: [es.enter_context(nc.semaphore(f"c_{e}_{i}")) for i in range(RROT)] for e in CENG}
        dsems = {q: [es.enter_context(nc.semaphore(f"d_{q}_{i}")) for i in range(n)] for q, n in NDMA.items()}
        for e in ENGS:
            k = 0
            for rec in self.streams[e]:
                if rec.is_dma or rec.fn is None:
                    continue
                if rec.signal:
                    rec.sig_k = k
                    k += 1

        def csem(e, k):
            return sems[e][(k // CH) % RROT], (k // (CH * RROT)) * CH + (k % CH) + 1

        def dsem(q, i):
            n = NDMA[q]
            return dsems[q][i % n], 16 * (i // n + 1)

        block = es.enter_context(nc.Block())

        def run(ename):
            def body(eng):
                for rec in self.streams[ename]:
                    for e, (i, prod) in rec.cwaits.items():
                        s, v = csem(e, prod.sig_k)
                        eng.wait_ge(s, v)
                    for (q, i) in rec.dwaits:
                        s, v = dsem(q, i)
                        eng.wait_ge(s, v)
                    if rec.fn is None:
                        continue
                    if rec.is_dma:
                        n = NDMA[ename]
                        if rec.dma_id >= n:
                            s, v = dsem(ename, rec.dma_id - n)
                            eng.wait_ge(s, v)
                    ins = rec.fn(eng)
                    if rec.is_dma:
                        s, v = dsem(ename, rec.dma_id)
                        ins.then_inc(s, 16)
                    elif rec.signal:
                        s, v = csem(ename, rec.sig_k)
                        ins.then_inc(s, 1)
                if ename in NDMA and self.ndma[ename] > 0:
                    n = NDMA[ename]
                    tot = self.ndma[ename]
                    for slot in range(min(n, tot)):
                        last = ((tot - 1 - slot) // n) * n + slot
                        s, v = dsem(ename, last)
                        eng.wait_ge(s, v)
            return body

        block.sync(run("sp"))
        block.tensor(run("pe"))
        block.scalar(run("act"))
        block.vector(run("dve"))
        block.gpsimd(run("pool"))


def tres(name, t0, n):
    return [f"{name}:{i}" for i in range(t0 // 128, (t0 + n + 127) // 128)]


def build(n_layers=DEPTH, debug=False, stop=None):
    nc = bass.Bass("TRN2", target_bir_lowering=False)

    def din(name, shape, dt=F32):
        return nc.dram_tensor(name, list(shape), dt, kind="ExternalInput").ap()

    def dscr(name, shape, dt):
        return nc.dram_tensor(name, list(shape), dt, kind="ExternalOutput" if debug else "Internal").ap()

    x_d = din("x", [TL, D])
    ctx_d = din("ctx", [TC, D])
    cc_d = din("cc", [128, 8, 2])
    w_ada_d = din("w_ada", [DEPTH, D, 6 * D])
    b_ada_d = din("b_ada_c", [128, DEPTH, 48])
    nmix_d = din("nmix_c", [128, DEPTH, 8])
    nmlp_d = din("nmlp_c", [128, DEPTH, 8])
    w_in_d = din("w_in_p", [DEPTH, D, INCP])
    bg_d = din("bg_c", [16, DEPTH])
    conv_d = din("conv_c", [128, DEPTH, 3, 8])
    qkg_d = din("qk_gain", [128, DEPTH, 640])
    mlg_d = din("ml_gain", [128, DEPTH, 512])
    w_out_d = din("w_out_p", [DEPTH, D, D])
    w1_d = din("w_mlp_in", [DEPTH, D, FFN])
    w2_d = din("w_mlp_out", [DEPTH, FFN, D])
    nfin_d = din("nfin_c", [128, 8])
    ident_d = din("ident", [128, 128])
    maskf_d = din("maskf", [128, 128])
    maskb_d = din("maskb", [128, 128])
    sel_d = din("sel4", [128, 4, 128])
    cos_d = din("rope_cos", [128, 32, 32])
    sin_d = din("rope_sin", [128, 32, 32])
    out_d = nc.dram_tensor("out", [TL, D], F32, kind="ExternalOutput").ap()

    xT_d = dscr("xT", [8, 128, T], F32)
    mqkT_d = dscr("mqkT", [8, 128, T], F32)
    gT_d = dscr("gT", [16, T], F32)
    QT_d = dscr("QT", [4, 128, T], BF16)
    KT_d = dscr("KT", [128, T], BF16)
    tmv_d = dscr("tmv", [T, 1152], BF16)
    attT_d = dscr("attT", [8, 128, T], BF16)
    hf_d = dscr("hf", [T, 512], F32)
    hb_d = dscr("hb", [T, 512], F32)

    P = Prog(nc)
    es = ExitStack()
    arena = es.enter_context(nc.sbuf_tensor("arena", [128, ARENA // 4], F32))
    psum = es.enter_context(nc.psum_tensor("psum", [128, 4096], F32))

    class Alloc:
        def __init__(self, base=0):
            self.off = base

        def get(self, shape, dt, parts=128):
            n = 1
            for s in shape:
                n *= s
            es_ = 4 if dt == F32 else 2
            nb = (n * es_ + 3) // 4 * 4
            w0 = self.off // 4
            ap = arena[0:parts, w0:w0 + nb // 4]
            self.off += nb
            assert self.off <= ARENA, f"arena overflow {self.off}"
            if dt == BF16:
                ap = ap.bitcast(BF16)
            if len(shape) == 2:
                ap = ap.rearrange("p (a b) -> p a b", a=shape[0])
            elif len(shape) == 3:
                ap = ap.rearrange("p (a b c) -> p a b c", a=shape[0], b=shape[1])
            return ap

    def bank(i):
        return psum[:, i * 512:(i + 1) * 512]

    def bank_bf(i):
        return psum[:, i * 512:(i + 1) * 512].bitcast(BF16)

    class Rot:
        def __init__(self, items):
            self.items = items
            self.i = 0

        def next(self):
            it = self.items[self.i % len(self.items)]
            self.i += 1
            return it

    def dma(q, out, in_, reads=(), writes=()):
        P.op(q, lambda e: e.dma_start(out=out, in_=in_), reads=reads, writes=writes, is_dma=True)

    A0 = Alloc(0)
    identF = A0.get([128], F32)
    identB = A0.get([128], BF16)
    onesB = A0.get([128], BF16)
    maskT = [A0.get([128], F32), A0.get([128], F32)]
    selF = A0.get([4, 128], F32)
    selB = A0.get([4, 128], BF16)
    cst = A0.get([4], F32)
    cc = A0.get([8, 2], F32)
    sc = A0.get([8, 2], F32)
    mod = A0.get([DEPTH * 48, 2], F32)
    s1 = A0.get([DEPTH * 8, 2], F32)
    s2 = A0.get([DEPTH * 8, 2], F32)
    nmix = A0.get([DEPTH * 8], F32)
    nmlp = A0.get([DEPTH * 8], F32)
    bada = A0.get([DEPTH * 48], F32)
    bgc = A0.get([DEPTH], F32, parts=16)
    convc = A0.get([DEPTH * 3 * 8], F32)
    nfin = A0.get([8], F32)
    PBASE = A0.off

    def modcol(L, j, r):
        return mod[:, L * 48 + j, r:r + 1]

    dma("sp", identF, ident_d, writes=["identF"])
    dma("sp", maskT[0], maskf_d, writes=["maskf"])
    dma("sp", maskT[1], maskb_d, writes=["maskb"])
    dma("sp", selF, sel_d, writes=["selF"])
    dma("sp", cc, cc_d, writes=["cc"])
    dma("sp", nmix, nmix_d.rearrange("p l j -> p (l j)"), writes=["nmix"])
    dma("sp", nmlp, nmlp_d.rearrange("p l j -> p (l j)"), writes=["nmlp"])
    dma("sp", bada, b_ada_d.rearrange("p l j -> p (l j)"), writes=["bada"])
    dma("sp", bgc, bg_d, writes=["bgc"])
    dma("sp", convc, conv_d.rearrange("p l a c -> p (l a c)"), writes=["convc"])
    dma("sp", nfin, nfin_d, writes=["nfin"])
    P.op("act", lambda e: e.activation(out=identB, in_=identF, func=AF.Copy), reads=["identF"], writes=["identB"])
    P.op("pool", lambda e: e.memset(onesB, 1.0), writes=["onesB"])
    P.op("act", lambda e: e.activation(out=selB, in_=selF, func=AF.Copy), reads=["selF"], writes=["selB"])
    P.op("pool", lambda e: e.memset(cst[:, 0:1], EPS), writes=["cst0"])
    P.op("pool", lambda e: e.memset(cst[:, 1:2], 1.0), writes=["cst1"])
    P.op("pool", lambda e: e.memset(cst[:, 2:3], -0.5 * math.log(128.0)), writes=["cst2"])
    P.op("pool", lambda e: e.memset(cst[:, 3:4], 0.0), writes=["cst3"])
    CST = ["cst0", "cst1", "cst2", "cst3"]
    P.op("act", lambda e: e.activation(out=sc, in_=cc, func=AF.Silu), reads=["cc"], writes=["sc"])

    def phase_mod():
        A = Alloc(PBASE)
        wa = [A.get([8, 512], F32), A.get([8, 512], F32)]
        modps = bank(0)
        it = 0
        for L in range(n_layers):
            wsrc = w_ada_d[L].rearrange("(k p) c -> p k c", p=128)
            for pc in range(12):
                buf = wa[it % 2]
                bn = f"wa{it % 2}"
                it += 1
                dma("sp", buf, wsrc[:, :, pc * 512:(pc + 1) * 512], writes=[bn])
                for cjj in range(4):
                    j = pc * 4 + cjj
                    col = (L * 48 + j) * 2
                    for k in range(8):
                        P.op("pe", (lambda buf=buf, k=k, cjj=cjj, col=col: lambda e: e.matmul(
                            modps[:, col:col + 2], lhsT=buf[:, k, cjj * 128:(cjj + 1) * 128], rhs=sc[:, k, :],
                            start=(k == 0), stop=(k == 7)))(),
                            reads=[bn, "sc"], writes=["modps"])
        nl = n_layers * 48
        P.op("dve", lambda e: e.tensor_tensor(
            out=mod[:, 0:nl, :], in0=modps[:, 0:nl * 2].rearrange("p (a b) -> p a b", b=2),
            in1=bada[:, 0:nl].unsqueeze(2).to_broadcast([128, nl, 2]), op=ALU.add),
            reads=["modps", "bada"], writes=["mod"])
        for L in range(n_layers):
            for (dst, nm, nmn, jo) in ((s1, nmix, "nmix", 8), (s2, nmlp, "nmlp", 32)):
                P.op("dve", (lambda dst=dst, L=L, jo=jo: lambda e: e.tensor_scalar_add(
                    out=dst[:, L * 8:(L + 1) * 8, :], in0=mod[:, L * 48 + jo:L * 48 + jo + 8, :], scalar1=1.0))(),
                    reads=["mod"], writes=["s12"])
                P.op("dve", (lambda dst=dst, L=L, nm=nm: lambda e: e.tensor_tensor(
                    out=dst[:, L * 8:(L + 1) * 8, :], in0=dst[:, L * 8:(L + 1) * 8, :],
                    in1=nm[:, L * 8:(L + 1) * 8].unsqueeze(2).to_broadcast([128, 8, 2]), op=ALU.mult))(),
                    reads=["s12", nmn], writes=["s12"])
        P.barrier()

    def phase_xT():
        A = Alloc(PBASE)
        xin = [A.get([1024], F32), A.get([1024], F32)]
        xst = [A.get([8, 128], F32), A.get([8, 128], F32)]
        rot = Rot([1, 2, 3, 4])
        for g in range(NCH):
            src = ctx_d[g * 128:(g + 1) * 128, :] if g < 2 else x_d[(g - 2) * 128:(g - 1) * 128, :]
            xb, xn_ = xin[g % 2], f"xin{g % 2}"
            st, stn = xst[g % 2], f"xst{g % 2}"
            dma("sp", xb, src, writes=[xn_])
            for half in range(2):
                b = rot.next()
                bk = bank(b)
                for jj in range(4):
                    j = half * 4 + jj
                    P.op("pe", (lambda bk=bk, jj=jj, j=j, xb=xb: lambda e: e.transpose(
                        out=bk[:, jj * 128:(jj + 1) * 128], in_=xb[:, j * 128:(j + 1) * 128], identity=identF))(),
                        reads=[xn_, "identF"], writes=[f"pb{b}"])
                eng = "act" if half == 0 else "dve"
                if eng == "act":
                    P.op("act", (lambda bk=bk, st=st, half=half: lambda e: e.activation(
                        out=st[:, half * 4:half * 4 + 4, :], in_=bk.rearrange("p (a b) -> p a b", a=4), func=AF.Copy))(),
                        reads=[f"pb{b}"], writes=[stn])
                else:
                    P.op("dve", (lambda bk=bk, st=st, half=half: lambda e: e.tensor_copy(
                        out=st[:, half * 4:half * 4 + 4, :], in_=bk.rearrange("p (a b) -> p a b", a=4)))(),
                        reads=[f"pb{b}"], writes=[stn])
            dma("sp", xT_d[:, :, g * 128:(g + 1) * 128].rearrange("j p t -> p j t"), st, reads=[stn],
                writes=tres("xT", g * 128, 128))
        P.barrier()

    def norm_mod(xTb, xname, n, sq, rstd, xn, hT, hname, scol, bcol, ssb):
        ssps = bank(ssb)
        for j in range(8):
            s_, sn = sq[j % 2], f"sq{j % 2}"
            P.op("act", (lambda s_=s_, j=j: lambda e: e.activation(out=s_[:, 0:n], in_=xTb[:, j, 0:n], func=AF.Square))(),
                 reads=[xname], writes=[sn])
            P.op("pe", (lambda s_=s_, j=j: lambda e: e.matmul(ssps[:, 0:n], lhsT=onesB, rhs=s_[:, 0:n],
                                                              start=(j == 0), stop=(j == 7)))(),
                 reads=[sn, "onesB"], writes=[f"pb{ssb}"])
        P.op("act", lambda e: e.activation(out=rstd[:, 0:n], in_=ssps[:, 0:n], func=AF.Sqrt, scale=1.0 / D, bias=cst[:, 0:1]),
             reads=[f"pb{ssb}"] + CST, writes=["rstd"])
        P.op("dve", lambda e: e.reciprocal(out=rstd[:, 0:n], in_=rstd[:, 0:n]), reads=["rstd"], writes=["rstd"])
        for j in range(8):
            x_, xn_ = xn[j % 2], f"xn{j % 2}"
            P.op("dve", (lambda x_=x_, j=j: lambda e: e.tensor_tensor(out=x_[:, 0:n], in0=xTb[:, j, 0:n], in1=rstd[:, 0:n],
                                                                      op=ALU.mult))(),
                 reads=[xname, "rstd"], writes=[xn_])
            P.op("act", (lambda x_=x_, j=j: lambda e: e.activation(out=hT[:, j, 0:n], in_=x_[:, 0:n], func=AF.Identity,
                                                                   scale=scol(j), bias=bcol(j)))(),
                 reads=[xn_, "s12", "mod"], writes=[hname])

    def phase_inproj(L):
        A = Alloc(PBASE)
        w_in = A.get([8, INCP], BF16)
        qkg = A.get([640], F32)
        cosT = A.get([32, 32], F32)
        sinT = A.get([32, 32], F32)
        dma("sp", cosT, cos_d, writes=["cosT"])
        dma("sp", sinT, sin_d, writes=["sinT"])
        xTb = [A.get([8, 512], F32), A.get([8, 512], F32)]
        sq = [A.get([512], BF16), A.get([512], BF16)]
        rstd = A.get([512], F32)
        xn = [A.get([512], F32), A.get([512], F32)]
        hT = A.get([8, 512], BF16)
        qk_sb = A.get([640], F32)
        sqq = A.get([640], F32)
        ss = A.get([10], F32)
        qn = A.get([640], F32)
        ra = A.get([320], F32)
        rb = A.get([320], F32)
        qr = A.get([640], BF16)
        QTst = A.get([4, 512], BF16)
        KTst = A.get([512], BF16)
        tmvst = [A.get([1152], BF16), A.get([1152], BF16)]
        fmst = A.get([8, 512], F32)
        gst = A.get([512], F32, parts=16)

        wsrc = w_in_d[L].rearrange("(k p) c -> p k c", p=128)
        for (c0, c1) in ((0, 512), (512, 768), (768, 1792), (1792, 2304), (2304, 2816), (2816, 2944)):
            if 'w' not in KSKIP:
                dma("pool", w_in[:, :, c0:c1], wsrc[:, :, c0:c1], writes=[f"w_in{c0}"])
        dma("sp", qkg, qkg_d[:, L, :], writes=["qkg"])
        rot = Rot([2, 3, 4] if 'b' in KSKIP else [2, 3, 4, 5, 6, 7])
        blocks = [(0, 256)] + [(256 + 512 * i, 512) for i in range(8)]
        def do_block(bi, t0, n):
            r = 1 if t0 < TC else 0
            xb, xname = xTb[bi % 2], f"xTb{bi % 2}"
            dma("sp", xb[:, :, 0:n], xT_d[:, :, t0:t0 + n].rearrange("j p t -> p j t"),
                reads=tres("xT", t0, n), writes=[xname])
            if 'n' not in KSKIP:
                norm_mod(xb, xname, n, sq, rstd, xn, hT, "hT",
                         lambda j: s1[:, L * 8 + j, r:r + 1], lambda j: modcol(L, j, r), 0)
            for m in range(0 if 'm' in KSKIP else n // 128):
                tok0 = t0 + m * 128
                ms = slice(m * 128, (m + 1) * 128)
                tv, tvn = tmvst[m % 2], f"tmvst{m % 2}"
                groups = ((0, 512, "w_in0"), (512, 768, "w_in512"), (1792, 2304, "w_in1792"), (2304, 2816, "w_in2304"))
                pbs = []
                for (c0, c1, wn) in groups:
                    b = rot.next()
                    pbs.append(b)
                    for k in range(8):
                        P.op("pe", (lambda b=b, k=k, c0=c0, c1=c1, ms=ms: lambda e: e.matmul(
                            bank(b)[:, 0:c1 - c0], lhsT=hT[:, k, ms], rhs=w_in[:, k, c0:c1], start=(k == 0), stop=(k == 7)))(),
                            reads=["hT", wn], writes=[f"pb{b}"])
                b0, b1, b2, b3 = pbs
                P.op("act", (lambda b0=b0: lambda e: e.activation(out=qk_sb[:, 0:512], in_=bank(b0), func=AF.Copy))(),
                     reads=[f"pb{b0}"], writes=["qk_sb_a"])
                P.op("act", (lambda b1=b1: lambda e: e.activation(out=qk_sb[:, 512:640], in_=bank(b1)[:, 0:128], func=AF.Copy))(),
                     reads=[f"pb{b1}"], writes=["qk_sb_b"])
                P.op("dve", (lambda b1=b1, tv=tv: lambda e: e.tensor_copy(out=tv[:, 0:128], in_=bank(b1)[:, 128:256]))(),
                     reads=[f"pb{b1}"], writes=[tvn + "a"])
                P.op("dve", (lambda b2=b2, tv=tv: lambda e: e.tensor_copy(out=tv[:, 128:640], in_=bank(b2)))(),
                     reads=[f"pb{b2}"], writes=[tvn + "b"])
                P.op("act", (lambda b3=b3, tv=tv: lambda e: e.activation(out=tv[:, 640:1152], in_=bank(b3), func=(AF.Copy if 's' in KSKIP else AF.Sigmoid)))(),
                     reads=[f"pb{b3}"], writes=[tvn + "c"])
                dma("sp", tmv_d[tok0:tok0 + 128, :], tv, reads=[tvn + "a", tvn + "b", tvn + "c"], writes=tres("tmv", tok0, 128))
                if 'q' not in KSKIP:
                    QKS = ["qk_sb_a", "qk_sb_b"]
                    P.op("dve", lambda e: e.tensor_tensor(out=sqq, in0=qk_sb, in1=qk_sb, op=ALU.mult), reads=QKS, writes=["sqq"])
                    P.op("dve", lambda e: e.reduce_sum(out=ss, in_=sqq.rearrange("p (h d) -> p h d", h=10), axis=AX.X),
                         reads=["sqq"], writes=["ss"])
                    P.op("act", lambda e: e.activation(out=ss, in_=ss, func=AF.Sqrt, scale=1.0 / 64, bias=cst[:, 0:1]),
                         reads=["ss"] + CST, writes=["ss"])
                    P.op("dve", lambda e: e.reciprocal(out=ss, in_=ss), reads=["ss"], writes=["ss"])
                    P.op("pool", lambda e: e.tensor_tensor(out=qn.rearrange("p (h d) -> p h d", h=10),
                                                           in0=qk_sb.rearrange("p (h d) -> p h d", h=10),
                                                           in1=ss.unsqueeze(2).to_broadcast([128, 10, 64]), op=ALU.mult),
                         reads=QKS + ["ss"], writes=["qn"])
                    P.op("pool", lambda e: e.tensor_tensor(out=qn, in0=qn, in1=qkg, op=ALU.mult), reads=["qn", "qkg"], writes=["qn"])
                    if r == 0:
                        gi = (tok0 - TC) // 128
                        q4 = qn.rearrange("p (h i two) -> p h i two", h=10, two=2)
                        o4 = qr.rearrange("p (h i two) -> p h i two", h=10, two=2)
                        x0, x1 = q4[:, :, :, 0], q4[:, :, :, 1]
                        cb = cosT[:, gi, :].unsqueeze(1).to_broadcast([128, 10, 32])
                        sb_ = sinT[:, gi, :].unsqueeze(1).to_broadcast([128, 10, 32])
                        ra3 = ra.rearrange("p (h i) -> p h i", h=10)
                        rb3 = rb.rearrange("p (h i) -> p h i", h=10)
                        P.op("pool", (lambda x0=x0, cb=cb: lambda e: e.tensor_tensor(out=ra3, in0=x0, in1=cb, op=ALU.mult))(),
                             reads=["qn", "cosT"], writes=["ra"])
                        P.op("pool", (lambda x1=x1, sb_=sb_: lambda e: e.tensor_tensor(out=rb3, in0=x1, in1=sb_, op=ALU.mult))(),
                             reads=["qn", "sinT"], writes=["rb"])
                        P.op("pool", (lambda o4=o4: lambda e: e.tensor_tensor(out=o4[:, :, :, 0], in0=ra3, in1=rb3, op=ALU.subtract))(),
                             reads=["ra", "rb"], writes=["qr0"])
                        P.op("pool", (lambda x0=x0, sb_=sb_: lambda e: e.tensor_tensor(out=ra3, in0=x0, in1=sb_, op=ALU.mult))(),
                             reads=["qn", "sinT", "qr0"], writes=["ra"])
                        P.op("pool", (lambda x1=x1, cb=cb: lambda e: e.tensor_tensor(out=rb3, in0=x1, in1=cb, op=ALU.mult))(),
                             reads=["qn", "cosT", "qr0"], writes=["rb"])
                        P.op("pool", (lambda o4=o4: lambda e: e.tensor_tensor(out=o4[:, :, :, 1], in0=ra3, in1=rb3, op=ALU.add))(),
                             reads=["ra", "rb"], writes=["qr1"])
                        QR = ["qr0", "qr1"]
                    else:
                        P.op("pool", lambda e: e.tensor_copy(out=qr, in_=qn), reads=["qn"], writes=["qr0"])
                        QR = ["qr0"]
                    tb = bank_bf(1)
                    for jq in range(5):
                        P.op("pe", (lambda jq=jq: lambda e: e.transpose(out=tb[:, jq * 128:(jq + 1) * 128],
                                                                        in_=qr[:, jq * 128:(jq + 1) * 128], identity=identB))(),
                             reads=QR + ["identB"], writes=["pb1"])
                    P.op("dve", (lambda ms=ms: lambda e: e.tensor_copy(out=QTst[:, :, ms],
                                                                       in_=tb[:, 0:512].rearrange("p (a b) -> p a b", a=4)))(),
                         reads=["pb1"], writes=["QTst"])
                    P.op("act", (lambda ms=ms: lambda e: e.activation(out=KTst[:, ms], in_=tb[:, 512:640], func=AF.Copy))(),
                         reads=["pb1"], writes=["KTst"])
            dma("sp", QT_d[:, :, t0:t0 + n].rearrange("j p t -> p j t"), QTst[:, :, 0:n], reads=["QTst"], writes=tres("QT", t0, n))
            dma("sp", KT_d[:, t0:t0 + n], KTst[:, 0:n], reads=["KTst"], writes=tres("KT", t0, n))
            if 'f' not in KSKIP:
                for cj in range(8):
                    b = rot.next()
                    for k in range(8):
                        P.op("pe", (lambda b=b, k=k, cj=cj: lambda e: e.matmul(
                            bank(b)[:, 0:n], lhsT=w_in[:, k, 768 + cj * 128:768 + (cj + 1) * 128], rhs=hT[:, k, 0:n],
                            start=(k == 0), stop=(k == 7)))(), reads=["hT", "w_in768"], writes=[f"pb{b}"])
                    if cj % 2 == 0:
                        P.op("act", (lambda b=b, cj=cj: lambda e: e.activation(out=fmst[:, cj, 0:n], in_=bank(b)[:, 0:n], func=AF.Copy))(),
                             reads=[f"pb{b}"], writes=[f"fmst{cj}"])
                    else:
                        P.op("dve", (lambda b=b, cj=cj: lambda e: e.tensor_copy(out=fmst[:, cj, 0:n], in_=bank(b)[:, 0:n]))(),
                             reads=[f"pb{b}"], writes=[f"fmst{cj}"])
                dma("sp", mqkT_d[:, :, t0:t0 + n].rearrange("j p t -> p j t"), fmst[:, :, 0:n],
                    reads=[f"fmst{cj}" for cj in range(8)], writes=tres("mqkT", t0, n))
                b = rot.next()
                for k in range(8):
                    P.op("pe", (lambda b=b, k=k: lambda e: e.matmul(bank(b)[:, 0:n], lhsT=w_in[:, k, 2816:2944], rhs=hT[:, k, 0:n],
                                                                  start=(k == 0), stop=(k == 7)))(),
                         reads=["hT", "w_in2816"], writes=[f"pb{b}"])
                P.op("act", (lambda b=b: lambda e: e.activation(out=gst[:, 0:n], in_=bank(b)[0:16, 0:n], func=AF.Identity,
                                                                bias=bgc[:, L:L + 1], scale=1.0))(),
                     reads=[f"pb{b}", "bgc"], writes=["gst"])
                dma("sp", gT_d[:, t0:t0 + n], gst[:, 0:n], reads=["gst"], writes=tres("gT", t0, n))
        for bi, (t0, n) in enumerate(blocks):
            do_block(bi, t0, n)
        P.barrier()

    WOFF = ARENA - 131072

    def w4_views():
        A = Alloc(WOFF)
        return A.get([8, FFN], BF16), A.get([32, D], BF16)

    def phase_attn(L, emit_ctx, prefetch):
        A = Alloc(PBASE)
        KTs = A.get([T], BF16)
        Vaug = A.get([NCH, 2, 128], BF16)
        QA = [[A.get([4, 512], BF16), A.get([4, 512], BF16)] for _ in range(2)]
        PT = [A.get([512], BF16) for _ in range(4)]
        rec = A.get([512], F32)
        attst = [A.get([4, 512], BF16), A.get([4, 512], BF16)]
        assert A.off <= WOFF
        if prefetch:
            w1, w2 = w4_views()
            for k in range(8):
                dma("pool", w1[:, k, :], w1_d[L, k * 128:(k + 1) * 128, :], writes=[f"w1_{k}"])
            for k4 in range(8):
                dma("pool", w2[:, k4 * 4:(k4 + 1) * 4, :],
                    w2_d[L, k4 * 512:(k4 + 1) * 512, :].rearrange("(k p) c -> p k c", p=128), writes=[f"w2_{k4}"])
        dma("sp", KTs, KT_d, writes=["KTs"])
        for bq in range(2):
            P.op("pool", (lambda bq=bq: lambda e: e.memset(QA[bq][0][64:128], 0.0))(), writes=[f"QAz{bq}0"])
            P.op("pool", (lambda bq=bq: lambda e: e.memset(QA[bq][1][0:64], 0.0))(), writes=[f"QAz{bq}1"])
        P.op("pool", lambda e: e.memset(Vaug, 1.0), writes=["Vaug"])
        vsrc = tmv_d.rearrange("(c p) f -> p c f", p=128)
        dma("sp", Vaug[:, :, 0, 0:64], vsrc[:, :, 0:64], reads=["Vaug"], writes=["Vaug0"])
        dma("sp", Vaug[:, :, 1, 64:128], vsrc[:, :, 64:128], reads=["Vaug"], writes=["Vaug1"])
        srot = Rot([0, 1, 2, 3])
        orot = Rot([4, 5])
        prot = Rot([0, 1, 2, 3])
        qblocks = [(256 + 512 * i, 512, list(range(NCH))) for i in range(8)]
        if emit_ctx:
            qblocks = [(0, 256, [0, 1])] + qblocks
        def do_q(qi, t0, n, kbs):
            qa, qbn = QA[qi % 2], f"QA{qi % 2}"
            ast, astn = attst[qi % 2], f"attst{qi % 2}"
            dma("sp", qa[0][0:64, :, 0:n], QT_d[:, 0:64, t0:t0 + n].rearrange("j p t -> p j t"), reads=[f"QAz{qi % 2}0"], writes=[qbn + "h0"])
            dma("sp", qa[1][64:128, :, 0:n], QT_d[:, 64:128, t0:t0 + n].rearrange("j p t -> p j t"), reads=[f"QAz{qi % 2}1"], writes=[qbn + "h1"])
            SK = 2
            its = []
            for j in range(4):
                for hh in range(2):
                    ob = orot.next()
                    for ki, kb in enumerate(kbs):
                        its.append((j, hh, ob, ki, kb))
            nk = len(kbs)
            slots = {}

            def emit_S(i):
                j, hh, ob, ki, kb = its[i]
                sbk = srot.next()
                pi = prot.next()
                pt = PT[pi]
                slots[i] = (pi, pt)
                P.op("pe", (lambda sbk=sbk, kb=kb, hh=hh, j=j: lambda e: e.matmul(
                    bank(sbk)[:, 0:n], lhsT=KTs[:, kb * 128:(kb + 1) * 128], rhs=qa[hh][:, j, 0:n], start=True, stop=True))(),
                    reads=["KTs", qbn + f"h{hh}"], writes=[f"pb{sbk}"])
                P.op("act", (lambda sbk=sbk, pt=pt: lambda e: e.activation(out=pt[:, 0:n], in_=bank(sbk)[:, 0:n], func=AF.Exp,
                                                                           scale=0.125))(),
                     reads=[f"pb{sbk}"], writes=[f"PT{pi}"])

            def emit_PV(i):
                j, hh, ob, ki, kb = its[i]
                pi, pt = slots.pop(i)
                hs = slice(hh * 64, hh * 64 + 64)
                os_ = slice((1 - hh) * 64, (1 - hh) * 64 + 64)
                P.op("pe", (lambda ob=ob, kb=kb, hh=hh, pt=pt, ki=ki: lambda e: e.matmul(
                    bank(ob)[:, 0:n], lhsT=Vaug[:, kb, hh, :], rhs=pt[:, 0:n], start=(ki == 0), stop=(ki == nk - 1)))(),
                    reads=["Vaug0", "Vaug1", f"PT{pi}"], writes=[f"pb{ob}"])
                if ki == nk - 1:
                    P.op("dve", (lambda ob=ob, hs=hs, os_=os_: lambda e: e.reciprocal(out=rec[hs, 0:n], in_=bank(ob)[os_, 0:n]))(),
                         reads=[f"pb{ob}"], writes=["rec"])
                    P.op("dve", (lambda ob=ob, hs=hs, j=j: lambda e: e.tensor_tensor(
                        out=ast[hs, j, 0:n], in0=bank(ob)[hs, 0:n], in1=rec[hs, 0:n], op=ALU.mult))(),
                        reads=[f"pb{ob}", "rec"], writes=[astn])

            for i in range(len(its) + SK):
                if i < len(its):
                    emit_S(i)
                if i - SK >= 0:
                    emit_PV(i - SK)
            dma("sp", attT_d[0:4, :, t0:t0 + n].rearrange("j p t -> p j t"), ast[:, :, 0:n], reads=[astn],
                writes=tres("attT_a", t0, n))
        for qi, (t0, n, kbs) in enumerate(qblocks):
            do_q(qi, t0, n, kbs)
        P.barrier()

    def phase_mlstm(L, emit_ctx):
        ATOP = Alloc(PBASE)
        Rr128 = [ATOP.get([NCH, 2, 128], BF16), ATOP.get([NCH, 2, 128], BF16)]
        Rr = [Rr128[0][0:4], Rr128[1][0:4]]
        emtT = [ATOP.get([NCH, 4], F32), ATOP.get([NCH, 4], F32)]
        decbc = [ATOP.get([4, NCH], F32), ATOP.get([4, NCH], F32)]
        PB2 = ATOP.off
        A = Alloc(PB2)
        Gi = [A.get([T], F32, parts=4), A.get([T], F32, parts=4)]
        Gf = [A.get([T], F32, parts=4), A.get([T], F32, parts=4)]
        zer = A.get([T], F32, parts=4)
        Nb = A.get([T], F32, parts=4)
        Ut = A.get([T], F32, parts=4)
        tmp128 = A.get([T], F32)
        tmp = tmp128[0:4]
        Uend = A.get([NCH], F32, parts=4)
        Uprev = A.get([NCH], F32, parts=4)
        dec128 = A.get([NCH], F32)
        dec = dec128[0:4]
        dma("sp", Gi[0], gT_d[0:4, :], writes=["Gi0"])
        dma("sp", Gf[0], gT_d[4:8, :], writes=["Gf0"])
        dma("sp", Gi[1], gT_d[8:12, :], writes=["Gi1"])
        dma("sp", Gf[1], gT_d[12:16, :], writes=["Gf1"])
        P.op("pool", lambda e: e.memset(zer, 0.0), writes=["zer"])
        P.op("pool", lambda e: e.memset(tmp128, 0.0), writes=["tmp"])
        P.op("pool", lambda e: e.memset(dec128, 0.0), writes=["dec"])
        P.op("pool", lambda e: e.memset(Rr128[0], 0.0), writes=["Rr0a", "Rr0b"])
        P.op("pool", lambda e: e.memset(Rr128[1], 0.0), writes=["Rr1a", "Rr1b"])
        c1 = cst[0:4, 1:2]
        for d in range(2):
            gi, gf = Gi[d], Gf[d]
            gin, gfn = f"Gi{d}", f"Gf{d}"
            P.op("act", (lambda gf=gf: lambda e: e.activation(out=gf, in_=gf, func=AF.Exp, scale=-1.0))(), reads=[gfn], writes=[gfn])
            P.op("act", (lambda gf=gf: lambda e: e.activation(out=gf, in_=gf, func=AF.Ln, bias=c1, scale=1.0))(),
                 reads=[gfn] + CST, writes=[gfn])

            def seg(ap, lo, hi, d=d):
                v = ap[:, lo:hi]
                return v[:, ::-1] if d == 1 else v
            if d == 0:
                P.op("dve", (lambda gf=gf: lambda e: e.tensor_tensor_scan(out=Nb, data0=gf, data1=zer, initial=0.0,
                                                                          op0=ALU.add, op1=ALU.add))(),
                     reads=[gfn, "zer"], writes=["Nb"])
            else:
                P.op("dve", (lambda gf=gf: lambda e: e.tensor_tensor_scan(out=seg(Nb, 0, TC), data0=seg(gf, 0, TC), data1=zer[:, 0:TC],
                                                                          initial=0.0, op0=ALU.add, op1=ALU.add))(),
                     reads=[gfn, "zer"], writes=["Nb"])
                P.op("dve", (lambda gf=gf: lambda e: e.tensor_tensor_scan(out=seg(Nb, TC, T), data0=seg(gf, TC, T), data1=zer[:, TC:T],
                                                                          initial=Nb[:, 0:1], op0=ALU.add, op1=ALU.add))(),
                     reads=[gfn, "zer", "Nb"], writes=["Nb"])
            P.op("dve", (lambda gi=gi: lambda e: e.tensor_tensor(out=gi, in0=gi, in1=Nb, op=ALU.add))(), reads=[gin, "Nb"], writes=[gin])
            if d == 0:
                P.op("dve", (lambda gi=gi: lambda e: e.tensor_tensor_scan(out=Ut, data0=gi, data1=zer, initial=0.0,
                                                                          op0=ALU.max, op1=ALU.max))(),
                     reads=[gin, "zer"], writes=["Ut"])
            else:
                P.op("dve", (lambda gi=gi: lambda e: e.tensor_tensor_scan(out=seg(Ut, 0, TC), data0=seg(gi, 0, TC), data1=zer[:, 0:TC],
                                                                          initial=0.0, op0=ALU.max, op1=ALU.max))(),
                     reads=[gin, "zer"], writes=["Ut"])
                P.op("dve", (lambda gi=gi: lambda e: e.tensor_tensor_scan(out=seg(Ut, TC, T), data0=seg(gi, TC, T), data1=zer[:, TC:T],
                                                                          initial=Ut[:, 0:1], op0=ALU.max, op1=ALU.max))(),
                     reads=[gin, "zer", "Ut"], writes=["Ut"])
            U3 = Ut.rearrange("p (c t) -> p c t", t=128)
            endcol = 127 if d == 0 else 0
            P.op("dve", (lambda endcol=endcol: lambda e: e.tensor_copy(out=Uend, in_=U3[:, :, endcol]))(), reads=["Ut"], writes=["Uend"])
            P.op("pool", lambda e: e.memset(Uprev, 0.0), reads=[], writes=["Uprev"])
            if d == 0:
                P.op("dve", lambda e: e.tensor_copy(out=Uprev[:, 1:NCH], in_=Uend[:, 0:NCH - 1]), reads=["Uend", "Uprev"], writes=["Uprev"])
            else:
                P.op("dve", lambda e: e.tensor_copy(out=Uprev[:, 2:NCH - 1], in_=Uend[:, 3:NCH]), reads=["Uend", "Uprev"], writes=["Uprev"])
                P.op("dve", lambda e: e.tensor_copy(out=Uprev[:, NCH - 1:NCH], in_=Uend[:, 0:1]), reads=["Uend", "Uprev"], writes=["Uprev"])
                P.op("dve", lambda e: e.tensor_copy(out=Uprev[:, 0:1], in_=Uend[:, 1:2]), reads=["Uend", "Uprev"], writes=["Uprev"])
            upb = Uprev.unsqueeze(2).to_broadcast([4, NCH, 128])
            t3 = tmp.rearrange("p (c t) -> p c t", t=128)
            g3 = gi.rearrange("p (c t) -> p c t", t=128)
            Rd = Rr[d]
            P.op("dve", (lambda g3=g3: lambda e: e.tensor_tensor(out=t3, in0=g3, in1=upb, op=ALU.subtract))(),
                 reads=[gin, "Uprev"], writes=["tmp"])
            P.op("act", (lambda Rd=Rd: lambda e: e.activation(out=Rd[:, :, 0, :], in_=t3, func=AF.Exp, bias=cst[0:4, 2:3], scale=1.0))(),
                 reads=["tmp"] + CST, writes=[f"Rr{d}a"])
            P.op("dve", lambda e: e.tensor_tensor(out=t3, in0=upb, in1=U3, op=ALU.subtract), reads=["Ut", "Uprev", f"Rr{d}a"], writes=["tmp"])
            P.op("act", (lambda Rd=Rd: lambda e: e.activation(out=Rd[:, :, 1, :], in_=t3, func=AF.Exp))(),
                 reads=["tmp"], writes=[f"Rr{d}b"])
            P.op("dve", lambda e: e.tensor_tensor(out=tmp, in0=Nb, in1=Ut, op=ALU.subtract), reads=["Nb", "Ut", f"Rr{d}b"], writes=["tmp"])
            P.op("act", lambda e: e.activation(out=tmp, in_=tmp, func=AF.Exp), reads=["tmp"], writes=["tmp"])
            eb = 6 + d
            for c in range(NCH):
                P.op("pe", (lambda c=c, eb=eb: lambda e: e.matmul(bank(eb)[:, c * 4:c * 4 + 4], lhsT=tmp128[:, c * 128:(c + 1) * 128],
                                                                  rhs=identF[:, 0:4], start=True, stop=True))(),
                     reads=["tmp", "identF"], writes=[f"pb{eb}"])
            P.op("dve", (lambda d=d, eb=eb: lambda e: e.tensor_copy(out=emtT[d], in_=bank(eb)[:, 0:NCH * 4].rearrange("p (c h) -> p c h", h=4)))(),
                 reads=[f"pb{eb}"], writes=[f"emtT{d}"])
            P.op("dve", lambda e: e.tensor_tensor(out=dec, in0=Uprev, in1=Uend, op=ALU.subtract), reads=["Uprev", "Uend"], writes=["dec"])
            P.op("act", lambda e: e.activation(out=dec, in_=dec, func=AF.Exp), reads=["dec"], writes=["dec"])
            db = 4 + d
            for h in range(4):
                P.op("pe", (lambda h=h, db=db: lambda e: e.matmul(bank(db)[:, h * NCH:(h + 1) * NCH], lhsT=selF[:, h, :], rhs=dec128,
                                                                  start=True, stop=True))(),
                     reads=["dec", "selF"], writes=[f"pb{db}"])
            P.op("dve", (lambda d=d, db=db: lambda e: e.tensor_copy(out=decbc[d], in_=bank(db)[:, 0:4 * NCH].rearrange("p (h c) -> p h c", h=4)))(),
                 reads=[f"pb{db}"], writes=[f"decbc{d}"])
        P.barrier()

        A = Alloc(PB2)
        qkT = A.get([8, T], BF16)
        vaug = A.get([NCH, 4, 129], BF16)
        Cst = A.get([8, 129], F32)
        Cbf = A.get([8, 129], BF16)
        PB3 = A.off
        A2 = Alloc(PB3)
        PL = 2048
        xc = [A2.get([PL + 2], F32), A2.get([PL + 2], F32)]
        accs = [A2.get([PL], F32), A2.get([PL], F32)]
        P.op("pool", lambda e: e.memset(vaug, 1.0), writes=["vaug"])
        for h in range(4):
            dma("sp", vaug[:, :, h, 0:128], tmv_d[:, 128 + h * 128:256 + h * 128].rearrange("(c p) d -> p c d", p=128),
                reads=["vaug"], writes=[f"vaugv{h}"])
        P.op("pool", lambda e: e.memset(Cst, 0.0), writes=["Cst"])
        P.op("pool", lambda e: e.memset(Cbf, 0.0), writes=["Cbf"])
        pieces = [(0, TC, 0, TC), (TC, TC + PL, TC, T), (TC + PL, T, TC, T)]
        it = 0
        for cr in range(8):
            w = [convc[:, (L * 3 + a) * 8 + cr:(L * 3 + a) * 8 + cr + 1] for a in range(3)]
            for (a_, b_, sa, sb) in pieces:
                ln_ = b_ - a_
                lh = a_ > sa
                rh = b_ < sb
                x_, xn_ = xc[it % 2], f"xc{it % 2}"
                acc, accn = accs[it % 2], f"acc{it % 2}"
                it += 1
                lo = a_ - 1 if lh else a_
                hi = b_ + 1 if rh else b_
                c0 = 0 if lh else 1
                dma("sp", x_[:, c0:c0 + hi - lo], mqkT_d[cr, :, lo:hi], writes=[xn_])
                P.op("pool", (lambda x_=x_, w=w, ln_=ln_, acc=acc: lambda e: e.tensor_scalar_mul(out=acc[:, 0:ln_], in0=x_[:, 1:ln_ + 1], scalar1=w[1]))(),
                     reads=[xn_], writes=[accn])
                o0 = 0 if lh else 1
                P.op("dve", (lambda x_=x_, w=w, o0=o0, ln_=ln_, acc=acc: lambda e: e.scalar_tensor_tensor(
                    out=acc[:, o0:ln_], in0=x_[:, o0:ln_], scalar=w[0], in1=acc[:, o0:ln_], op0=ALU.mult, op1=ALU.add))(),
                    reads=[xn_, accn], writes=[accn])
                o1 = ln_ if rh else ln_ - 1
                P.op("dve", (lambda x_=x_, w=w, o1=o1, acc=acc: lambda e: e.scalar_tensor_tensor(
                    out=acc[:, 0:o1], in0=x_[:, 2:o1 + 2], scalar=w[2], in1=acc[:, 0:o1], op0=ALU.mult, op1=ALU.add))(),
                    reads=[xn_, accn], writes=[accn])
                P.op("act", (lambda cr=cr, a_=a_, b_=b_, ln_=ln_, acc=acc: lambda e: e.activation(out=qkT[:, cr, a_:b_], in_=acc[:, 0:ln_], func=AF.Silu))(),
                     reads=[accn], writes=[f"qkT{cr}"])
        P.barrier()

        A3 = Alloc(PB3)
        NR = 8
        kTs = [A3.get([128], BF16) for _ in range(NR)]
        qTs = [A3.get([128], BF16) for _ in range(NR)]
        Sw = [A3.get([128], BF16) for _ in range(NR)]
        k2 = [A3.get([128], BF16) for _ in range(NR)]
        dmx = [A3.get([2], F32) for _ in range(NR)]
        hst = [[A3.get([4, 128], F32) for _ in range(2)] for _ in range(2)]
        order = [list(range(NCH)), [1, 0] + list(range(NCH - 1, 1, -1))]
        hcnt = [0, 0]

        def do_step(i):
            ch = []
            for d in range(2):
                c = order[d][i]
                emit_out = emit_ctx or c >= 2
                hs_, hsn = hst[d][hcnt[d] % 2], f"hst{d}{hcnt[d] % 2}"
                for h in range(4):
                    ch.append((d, h, d * 4 + h, c, emit_out, hs_, hsn))
            for (d, h, x, c, eo, hs_, hsn) in ch:
                Rd = Rr128[d]
                P.op("pe", (lambda x=x, h=h, c=c, Rd=Rd: lambda e: e.matmul(
                    bank(x)[:, 0:256], lhsT=selB[:, h, :], rhs=Rd[:, c, :, :].rearrange("p a b -> p (a b)"),
                    start=True, stop=True))(), reads=[], writes=[f"pb{x}"])
            for (d, h, x, c, eo, hs_, hsn) in ch:
                ts = slice(c * 128, (c + 1) * 128)
                P.op("dve", (lambda x=x, h=h, ts=ts: lambda e: e.tensor_tensor(
                    out=kTs[x], in0=qkT[:, 4 + h, ts], in1=bank(x)[:, 0:128], op=ALU.mult))(),
                    reads=[f"pb{x}"], writes=[f"kTs{x}"])
                P.op("dve", (lambda x=x, h=h, ts=ts: lambda e: e.tensor_tensor(
                    out=qTs[x], in0=qkT[:, h, ts], in1=bank(x)[:, 128:256], op=ALU.mult))(),
                    reads=[f"pb{x}"], writes=[f"qTs{x}"])
            for (d, h, x, c, eo, hs_, hsn) in ch:
                P.op("pe", (lambda x=x: lambda e: e.matmul(bank(x)[:, 256:384], lhsT=kTs[x], rhs=qTs[x], start=True, stop=True))(),
                     reads=[f"kTs{x}", f"qTs{x}"], writes=[f"pb{x}"])
                P.op("pe", (lambda x=x: lambda e: e.transpose(out=bank_bf(x)[:, 768:896], in_=kTs[x], identity=identB))(),
                     reads=[f"kTs{x}"], writes=[f"pb{x}"])
            for (d, h, x, c, eo, hs_, hsn) in ch:
                P.op("dve", (lambda x=x, d=d: lambda e: e.tensor_tensor(
                    out=Sw[x], in0=bank(x)[:, 256:384], in1=maskT[d], op=ALU.mult))(),
                    reads=[f"pb{x}"], writes=[f"Sw{x}"])
                P.op("act", (lambda x=x, d=d, h=h, c=c: lambda e: e.activation(
                    out=k2[x], in_=bank_bf(x)[:, 768:896], func=AF.Identity, scale=decbc[d][:, h, c:c + 1]))(),
                    reads=[f"pb{x}"], writes=[f"k2{x}"])
            for (d, h, x, c, eo, hs_, hsn) in ch:
                if eo:
                    P.op("pe", (lambda x=x: lambda e: e.matmul(
                        bank(x)[:, 0:129], lhsT=qTs[x], rhs=Cbf[:, x, :], start=True, stop=False))(),
                        reads=[f"qTs{x}", f"Cbf{x}"], writes=[f"pb{x}"])
                    P.op("pe", (lambda x=x, c=c, h=h: lambda e: e.matmul(
                        bank(x)[:, 0:129], lhsT=Sw[x], rhs=vaug[:, c, h, :], start=False, stop=True))(),
                        reads=[f"Sw{x}"], writes=[f"pb{x}"])
                P.op("pe", (lambda x=x, c=c, h=h: lambda e: e.matmul(
                    bank(x)[:, 129:258], lhsT=k2[x], rhs=vaug[:, c, h, :], start=True, stop=True))(),
                    reads=[f"k2{x}"], writes=[f"pb{x}"])
            for (d, h, x, c, eo, hs_, hsn) in ch:
                if eo:
                    P.op("act", (lambda x=x: lambda e: e.activation(
                        out=dmx[x][:, 0:1], in_=bank(x)[:, 128:129], func=AF.Abs))(), reads=[f"pb{x}"], writes=[f"dmx{x}"])
                    P.op("dve", (lambda x=x, d=d, c=c, h=h: lambda e: e.tensor_tensor(
                        out=dmx[x][:, 0:1], in0=dmx[x][:, 0:1], in1=emtT[d][:, c, h:h + 1], op=ALU.max))(),
                        reads=[f"dmx{x}"], writes=[f"dmx{x}"])
                    P.op("dve", (lambda x=x: lambda e: e.reciprocal(out=dmx[x][:, 1:2], in_=dmx[x][:, 0:1]))(),
                         reads=[f"dmx{x}"], writes=[f"dmr{x}"])
                    P.op("act", (lambda x=x, h=h, hs_=hs_: lambda e: e.activation(
                        out=hs_[:, h, :], in_=bank(x)[:, 0:128], func=AF.Identity, scale=dmx[x][:, 1:2]))(),
                        reads=[f"pb{x}", f"dmr{x}"], writes=[hsn + str(h)])
                P.op("dve", (lambda x=x, d=d, h=h, c=c: lambda e: e.scalar_tensor_tensor(
                    out=Cst[:, x, :], in0=Cst[:, x, :], scalar=decbc[d][:, h, c:c + 1], in1=bank(x)[:, 129:258],
                    op0=ALU.mult, op1=ALU.add))(), reads=[f"pb{x}", f"Cst{x}"], writes=[f"Cst{x}"])
                P.op("pool", (lambda x=x: lambda e: e.tensor_copy(out=Cbf[:, x, :], in_=Cst[:, x, :]))(),
                     reads=[f"Cst{x}"], writes=[f"Cbf{x}"])
            for d in range(2):
                c = order[d][i]
                if emit_ctx or c >= 2:
                    hs_, hsn = hst[d][hcnt[d] % 2], f"hst{d}{hcnt[d] % 2}"
                    hd = hf_d if d == 0 else hb_d
                    dma("sp", hd[c * 128:(c + 1) * 128, :], hs_.rearrange("p h d -> p (h d)"),
                        reads=[hsn + str(h) for h in range(4)], writes=tres("hfb%d" % d, c * 128, 128))
                    hcnt[d] += 1

        for i in range(NCH):
            do_step(i)
        P.barrier()

        A4 = Alloc(PB2)
        mlg = A4.get([512], F32)
        hfa = [A4.get([512], F32), A4.get([512], F32)]
        hba = [A4.get([512], F32), A4.get([512], F32)]
        mo = [A4.get([512], BF16), A4.get([512], BF16)]
        sq4 = A4.get([512], F32)
        s4 = A4.get([4], F32)
        memb = A4.get([512], BF16)
        memst = [A4.get([4, 128], BF16), A4.get([4, 128], BF16)]
        dma("sp", mlg, mlg_d[:, L, :], writes=["mlg"])
        trot = Rot([0, 1, 2, 3])
        gs = list(range(0 if emit_ctx else 2, NCH))
        for gi_, g in enumerate(gs):
            p2 = gi_ % 2
            ha, hbv, mov, mst = hfa[p2], hba[p2], mo[p2], memst[p2]
            rows = slice(g * 128, (g + 1) * 128)
            dma("sp", ha, hf_d[rows, :], writes=[f"hfa{p2}"])
            dma("sp", hbv, hb_d[rows, :], writes=[f"hba{p2}"])
            dma("sp", mov, tmv_d[rows, 640:1152], writes=[f"mo{p2}"])
            P.op("dve", (lambda ha=ha, hbv=hbv: lambda e: e.tensor_tensor(out=ha, in0=ha, in1=hbv, op=ALU.add))(),
                 reads=[f"hfa{p2}", f"hba{p2}"], writes=[f"hfa{p2}"])
            P.op("pool", (lambda ha=ha: lambda e: e.tensor_tensor(out=sq4, in0=ha, in1=ha, op=ALU.mult))(), reads=[f"hfa{p2}"], writes=["sq4"])
            P.op("dve", lambda e: e.reduce_sum(out=s4, in_=sq4.rearrange("p (h d) -> p h d", h=4), axis=AX.X), reads=["sq4"], writes=["s4"])
            P.op("act", lambda e: e.activation(out=s4, in_=s4, func=AF.Sqrt, scale=1.0 / 128, bias=cst[:, 0:1]), reads=["s4"] + CST, writes=["s4"])
            P.op("dve", lambda e: e.reciprocal(out=s4, in_=s4), reads=["s4"], writes=["s4"])
            P.op("dve", (lambda ha=ha: lambda e: e.tensor_tensor(out=ha.rearrange("p (h d) -> p h d", h=4),
                                                                in0=ha.rearrange("p (h d) -> p h d", h=4),
                                                                in1=s4.unsqueeze(2).to_broadcast([128, 4, 128]), op=ALU.mult))(),
                 reads=[f"hfa{p2}", "s4"], writes=[f"hfa{p2}"])
            P.op("pool", (lambda ha=ha: lambda e: e.tensor_tensor(out=ha, in0=ha, in1=mlg, op=ALU.mult))(), reads=[f"hfa{p2}", "mlg"], writes=[f"hfa{p2}"])
            P.op("pool", (lambda ha=ha, mov=mov: lambda e: e.tensor_tensor(out=memb, in0=ha, in1=mov, op=ALU.mult))(),
                 reads=[f"hfa{p2}", f"mo{p2}"], writes=["memb"])
            tb = trot.next()
            for h in range(4):
                P.op("pe", (lambda tb=tb, h=h: lambda e: e.transpose(out=bank_bf(tb)[:, h * 128:(h + 1) * 128],
                                                                    in_=memb[:, h * 128:(h + 1) * 128], identity=identB))(),
                     reads=["memb", "identB"], writes=[f"pb{tb}"])
            P.op("act", (lambda tb=tb, mst=mst: lambda e: e.activation(out=mst, in_=bank_bf(tb)[:, 0:512].rearrange("p (a b) -> p a b", a=4),
                                                                       func=AF.Copy))(), reads=[f"pb{tb}"], writes=[f"memst{p2}"])
            dma("sp", attT_d[4:8, :, g * 128:(g + 1) * 128].rearrange("j p t -> p j t"), mst, reads=[f"memst{p2}"],
                writes=tres("attT_m", g * 128, 128))
        P.barrier()

    def phase_mlp(L, emit_ctx, prefetched):
        w1, w2 = w4_views()
        if not prefetched:
            for k in range(8):
                dma("pool", w1[:, k, :], w1_d[L, k * 128:(k + 1) * 128, :], writes=[f"w1_{k}"])
            for k4 in range(8):
                dma("pool", w2[:, k4 * 4:(k4 + 1) * 4, :],
                    w2_d[L, k4 * 512:(k4 + 1) * 512, :].rearrange("(k p) c -> p k c", p=128), writes=[f"w2_{k4}"])
        W1 = [f"w1_{k}" for k in range(8)]
        W2 = [f"w2_{k}" for k in range(8)]
        A = Alloc(PBASE)
        NB = 256
        w_out = A.get([8, D], BF16)
        dma("pool", w_out, w_out_d[L].rearrange("(k p) c -> p k c", p=128), writes=["w_out"])
        attTb = [A.get([8, NB], BF16)]
        xTb = [A.get([8, NB], F32), A.get([8, NB], F32)]
        h2T = A.get([8, NB], BF16)
        sq = [A.get([NB], BF16), A.get([NB], BF16)]
        rstd = A.get([NB], F32)
        xn = [A.get([NB], F32), A.get([NB], F32)]
        rl = [A.get([NB], BF16), A.get([NB], BF16)]
        uT = A.get([32, NB], BF16)
        assert A.off <= WOFF, A.off
        rot = Rot([1, 2, 3, 4, 5, 6, 7])
        t0s = list(range(0 if emit_ctx else TC, T, NB))
        def do_block(bi, t0):
            n = NB
            r = 1 if t0 < TC else 0
            ab, abn = attTb[0], "attTb0"
            xb, xbn = xTb[bi % 2], f"xTb{bi % 2}"
            dma("sp", ab, attT_d[:, :, t0:t0 + n].rearrange("j p t -> p j t"), writes=[abn])
            dma("sp", xb, xT_d[:, :, t0:t0 + n].rearrange("j p t -> p j t"), writes=[xbn])
            for cj in range(8):
                b = rot.next()
                for k in range(8):
                    P.op("pe", (lambda b=b, k=k, cj=cj, ab=ab: lambda e: e.matmul(
                        bank(b)[:, 0:n], lhsT=w_out[:, k, cj * 128:(cj + 1) * 128], rhs=ab[:, k, :], start=(k == 0), stop=(k == 7)))(),
                        reads=["w_out", abn], writes=[f"pb{b}"])
                P.op("dve", (lambda b=b, cj=cj, xb=xb, r=r: lambda e: e.scalar_tensor_tensor(
                    out=xb[:, cj, :], in0=bank(b)[:, 0:n], scalar=modcol(L, 16 + cj, r), in1=xb[:, cj, :], op0=ALU.mult, op1=ALU.add))(),
                    reads=[f"pb{b}", xbn, "mod"], writes=[xbn])
            norm_mod(xb, xbn, n, sq, rstd, xn, h2T, "h2T",
                     lambda j: s2[:, L * 8 + j, r:r + 1], lambda j: modcol(L, 24 + j, r), 0)
            for fc in range(32):
                b = rot.next()
                for k in range(8):
                    P.op("pe", (lambda b=b, k=k, fc=fc: lambda e: e.matmul(
                        bank(b)[:, 0:n], lhsT=w1[:, k, fc * 128:(fc + 1) * 128], rhs=h2T[:, k, :], start=(k == 0), stop=(k == 7)))(),
                        reads=[f"w1_{k}", "h2T"], writes=[f"pb{b}"])
                r_, rn_ = rl[fc % 2], f"rl{fc % 2}"
                P.op("act", (lambda b=b, r_=r_: lambda e: e.activation(out=r_, in_=bank(b)[:, 0:n], func=AF.Relu))(),
                     reads=[f"pb{b}"], writes=[rn_])
                P.op("pool", (lambda r_=r_, fc=fc: lambda e: e.tensor_tensor(out=uT[:, fc, :], in0=r_, in1=r_, op=ALU.mult))(),
                     reads=[rn_], writes=[f"uT{fc}"])
            for cj in range(8):
                b = rot.next()
                for fc in range(32):
                    P.op("pe", (lambda b=b, fc=fc, cj=cj: lambda e: e.matmul(
                        bank(b)[:, 0:n], lhsT=w2[:, fc, cj * 128:(cj + 1) * 128], rhs=uT[:, fc, :], start=(fc == 0), stop=(fc == 31)))(),
                        reads=[f"w2_{fc // 4}", f"uT{fc}"], writes=[f"pb{b}"])
                P.op("dve", (lambda b=b, cj=cj, xb=xb, r=r: lambda e: e.scalar_tensor_tensor(
                    out=xb[:, cj, :], in0=bank(b)[:, 0:n], scalar=modcol(L, 40 + cj, r), in1=xb[:, cj, :], op0=ALU.mult, op1=ALU.add))(),
                    reads=[f"pb{b}", xbn, "mod"], writes=[xbn])
            dma("sp", xT_d[:, :, t0:t0 + n].rearrange("j p t -> p j t"), xb, reads=[xbn], writes=tres("xTo", t0, n))
        for bi, t0 in enumerate(t0s):
            do_block(bi, t0)
        P.barrier()

    def phase_final():
        A = Alloc(PBASE)
        xTb = [A.get([8, 512], F32), A.get([8, 512], F32)]
        sq = [A.get([512], BF16), A.get([512], BF16)]
        rstd = A.get([512], F32)
        xnf = A.get([8, 512], F32)
        ost = [A.get([1024], F32), A.get([1024], F32)]
        rot = Rot([1, 2, 3, 4, 5, 6])
        oc = 0
        def do_block(bi):
            nonlocal oc
            t0 = TC + bi * 512
            n = 512
            xb, xbn = xTb[bi % 2], f"xTb{bi % 2}"
            dma("sp", xb, xT_d[:, :, t0:t0 + n].rearrange("j p t -> p j t"), writes=[xbn])
            ssps = bank(0)
            for j in range(8):
                s_, sn = sq[j % 2], f"sq{j % 2}"
                P.op("act", (lambda s_=s_, j=j, xb=xb: lambda e: e.activation(out=s_, in_=xb[:, j, :], func=AF.Square))(), reads=[xbn], writes=[sn])
                P.op("pe", (lambda s_=s_, j=j: lambda e: e.matmul(ssps, lhsT=onesB, rhs=s_, start=(j == 0), stop=(j == 7)))(),
                     reads=[sn, "onesB"], writes=["pb0"])
            P.op("act", lambda e: e.activation(out=rstd, in_=ssps, func=AF.Sqrt, scale=1.0 / D, bias=cst[:, 0:1]), reads=["pb0"] + CST, writes=["rstd"])
            P.op("dve", lambda e: e.reciprocal(out=rstd, in_=rstd), reads=["rstd"], writes=["rstd"])
            for j in range(8):
                P.op("dve", (lambda j=j, xb=xb: lambda e: e.scalar_tensor_tensor(
                    out=xnf[:, j, :], in0=xb[:, j, :], scalar=nfin[:, j:j + 1], in1=rstd, op0=ALU.mult, op1=ALU.mult))(),
                    reads=[xbn, "rstd", "nfin"], writes=[f"xnf{j}"])
            for m in range(4):
                o_, on_ = ost[oc % 2], f"ost{oc % 2}"
                oc += 1
                for half in range(2):
                    b = rot.next()
                    for jj in range(4):
                        j = half * 4 + jj
                        P.op("pe", (lambda b=b, jj=jj, j=j, m=m: lambda e: e.transpose(
                            out=bank(b)[:, jj * 128:(jj + 1) * 128], in_=xnf[:, j, m * 128:(m + 1) * 128], identity=identF))(),
                            reads=[f"xnf{j}", "identF"], writes=[f"pb{b}"])
                    if half == 0:
                        P.op("act", (lambda b=b, o_=o_: lambda e: e.activation(out=o_[:, 0:512], in_=bank(b), func=AF.Copy))(),
                             reads=[f"pb{b}"], writes=[on_ + "a"])
                    else:
                        P.op("dve", (lambda b=b, o_=o_: lambda e: e.tensor_copy(out=o_[:, 512:1024], in_=bank(b)))(),
                             reads=[f"pb{b}"], writes=[on_ + "b"])
                r0 = bi * 512 + m * 128
                dma("sp", out_d[r0:r0 + 128, :], o_, reads=[on_ + "a", on_ + "b"], writes=[f"out{r0}"])

        for bi in range(8):
            do_block(bi)

    steps = [("mod", phase_mod), ("xT", phase_xT)]
    for L in range(n_layers):
        emit_ctx = L < DEPTH - 1
        steps.append((f"inproj{L}", (lambda L=L: phase_inproj(L))))
        steps.append((f"mlstm{L}", (lambda L=L, ec=emit_ctx: phase_mlstm(L, ec))))
        steps.append((f"attn{L}", (lambda L=L, ec=emit_ctx: phase_attn(L, ec, prefetch=True))))
        steps.append((f"mlp{L}", (lambda L=L, ec=emit_ctx: phase_mlp(L, ec, prefetched=True))))
    steps.append(("final", phase_final))
    for name, fn in steps:
        fn()
        if stop is not None and name == stop:
            break
    P.emit(es)
    es.close()
    return nc, P


def _perm_att():
    idx = np.zeros(512, dtype=np.int64)
    for j in range(4):
        for p in range(128):
            head = j if p < 64 else j + 4
            idx[j * 128 + p] = head * 64 + (p % 64)
    return idx


def _rope_tables():
    rows = TL // 64
    row_idx = np.repeat(np.arange(rows, dtype=np.float32), 64)
    col_idx = np.tile(np.arange(64, dtype=np.float32), rows)
    inv_freq = np.power(np.float32(10000.0), -np.arange(0, 32, 2, dtype=np.float32) / np.float32(32)).astype(np.float32)
    ang = np.concatenate([row_idx[:, None] * inv_freq, col_idx[:, None] * inv_freq], axis=-1).astype(np.float32)
    cos = np.cos(ang).astype(np.float32).reshape(32, 128, 32).transpose(1, 0, 2)
    sin = np.sin(ang).astype(np.float32).reshape(32, 128, 32).transpose(1, 0, 2)
    return np.ascontiguousarray(cos), np.ascontiguousarray(sin)


def host_inputs(inputs, cores):
    f = lambda a: np.ascontiguousarray(np.asarray(a, dtype=np.float32))
    perm = _perm_att()
    w_in = f(inputs["w_in"])
    w_in_p = np.zeros((DEPTH, D, INCP), np.float32)
    w_in_p[:, :, 0:INC] = w_in
    w_in_p[:, :, 0:512] = w_in[:, :, perm]
    w_out = f(inputs["w_out"])
    w_out_p = w_out.copy()
    w_out_p[:, 0:512, :] = w_out[:, perm, :]
    cos, sin = _rope_tables()
    sel = np.zeros((128, 4, 128), np.float32)
    for h in range(4):
        sel[h, h, :] = 1.0
    s_ = np.arange(128)
    maskf = (s_[:, None] <= s_[None, :]).astype(np.float32)
    maskb = (s_[:, None] >= s_[None, :]).astype(np.float32)
    colL = lambda a, nj: np.ascontiguousarray(f(a).reshape(DEPTH, nj, 128).transpose(2, 0, 1))
    qk_gain = np.concatenate([np.tile(f(inputs["q_norm"]), (1, 8)), np.tile(f(inputs["k_norm"]), (1, 2))], axis=1)
    shared = {
        "w_ada": f(inputs["w_ada"]),
        "b_ada_c": colL(inputs["b_ada"], 48),
        "nmix_c": colL(inputs["norm_mix"], 8),
        "nmlp_c": colL(inputs["norm_mlp"], 8),
        "w_in_p": w_in_p,
        "bg_c": np.ascontiguousarray(f(inputs["b_gates"]).T),
        "conv_c": np.ascontiguousarray(f(inputs["conv_qk"]).reshape(DEPTH, 3, 8, 128).transpose(3, 0, 1, 2)),
        "qk_gain": np.ascontiguousarray(np.broadcast_to(qk_gain[None], (128, DEPTH, 640))),
        "ml_gain": np.ascontiguousarray(np.broadcast_to(f(inputs["mlstm_norm"])[None], (128, DEPTH, 512))),
        "w_out_p": w_out_p,
        "w_mlp_in": f(inputs["w_mlp_in"]),
        "w_mlp_out": f(inputs["w_mlp_out"]),
        "nfin_c": np.ascontiguousarray(f(inputs["norm_final"]).reshape(8, 128).T),
        "ident": np.eye(128, dtype=np.float32),
        "maskf": maskf,
        "maskb": maskb,
        "sel4": sel,
        "rope_cos": cos,
        "rope_sin": sin,
    }
    x = f(inputs["x"])
    ctx = f(inputs["ctx"])
    c = f(inputs["c"])
    c_ctx = f(inputs["c_ctx"])
    maps = []
    for b in cores:
        cc = np.stack([c[b].reshape(8, 128).T, c_ctx.reshape(8, 128).T], axis=-1)
        m = dict(shared)
        m["x"] = x[b]
        m["ctx"] = ctx[b]
        m["cc"] = np.ascontiguousarray(cc)
        maps.append(m)
    return maps


_NC_CACHE = {}


def kernel(**inputs):
    if "nc" not in _NC_CACHE:
        _NC_CACHE["nc"] = build()[0]
    nc = _NC_CACHE["nc"]
    maps = host_inputs(inputs, list(range(8)))
    res = run_bass_kernel_spmd(nc, maps, core_ids=list(range(8)))
    out = np.stack([np.asarray(r["out"], dtype=np.float32) for r in res.results], axis=0)
    return out
```

```python
import math
import os
KSKIP = os.environ.get('KSKIP', '')
from contextlib import ExitStack
import numpy as np
import concourse.bass as bass
import concourse.mybir as mybir
from concourse.bass_utils import run_bass_kernel_spmd

F32 = mybir.dt.float32
BF16 = mybir.dt.bfloat16
ALU = mybir.AluOpType
AF = mybir.ActivationFunctionType
AX = mybir.AxisListType

DEPTH = 4
D = 1024
TC = 256
TL = 4096
T = TC + TL
NCH = T // 128
INC = 2832
INCP = 2944
FFN = 4096
EPS = 1e-6
ARENA = 210944

CH = 2048
RROT = 8
NDMA = {"sp": 28, "pool": 20}
CENG = ("pe", "act", "dve", "pool")
ENGS = ("pe", "act", "dve", "pool", "sp")


class Res:
    __slots__ = ("writers", "readers")

    def __init__(self):
        self.writers = []
        self.readers = []


class Rec:
    __slots__ = ("eng", "fn", "idx", "cwaits", "dwaits", "signal", "is_dma", "dma_id", "sig_k")

    def __init__(self, eng, fn, is_dma):
        self.eng = eng
        self.fn = fn
        self.is_dma = is_dma
        self.cwaits = {}
        self.dwaits = {}
        self.signal = False
        self.dma_id = None
        self.sig_k = None


class Prog:
    def __init__(self, nc):
        self.nc = nc
        self.streams = {e: [] for e in ENGS}
        self.maxw = {e: {} for e in ENGS}
        self.dw = {e: set() for e in ENGS}
        self.ndma = {q: 0 for q in NDMA}
        self.res = {}
        self.last_real = {e: None for e in CENG}
        self.dma_live = {q: {} for q in NDMA}

    def R(self, name):
        r = self.res.get(name)
        if r is None:
            r = Res()
            self.res[name] = r
        return r

    def _need(self, rec, tok, same_ok):
        kind, e, i, prod = tok
        if kind == "c":
            if e == rec.eng and (same_ok or e == "pe"):
                return
            if self.maxw[rec.eng].get(e, -1) >= i:
                return
            if rec.cwaits.get(e, (-1, None))[0] < i:
                rec.cwaits[e] = (i, prod)
        else:
            key = (e, i)
            if key in self.dw[rec.eng]:
                return
            rec.dwaits[key] = prod

    def _commit(self, rec):
        for e, (i, prod) in rec.cwaits.items():
            self.maxw[rec.eng][e] = i
            prod.signal = True
        for key, prod in rec.dwaits.items():
            self.dw[rec.eng].add(key)

    def op(self, eng, fn, reads=(), writes=(), is_dma=False):
        rec = Rec(eng, fn, is_dma)
        st = self.streams[eng]
        rec.idx = len(st)
        psn = sorted({n for n in list(reads) + list(writes) if n.startswith("pb") or n == "modps"})
        reads = [self.R(r) for r in reads if r not in psn]
        writes = [self.R(r) for r in writes if r not in psn]
        psr = [self.R(n) for n in psn]
        for r in psr:
            for tok in r.writers:
                self._need(rec, tok, True)
        for r in reads:
            for tok in r.writers:
                self._need(rec, tok, False)
        for w in writes:
            for tok in w.writers:
                self._need(rec, tok, True)
            for tok in w.readers:
                self._need(rec, tok, True)
        self._commit(rec)
        if is_dma:
            rec.dma_id = self.ndma[eng]
            self.ndma[eng] += 1
            self.dma_live[eng][rec.dma_id % NDMA[eng]] = rec
            tok = ("d", eng, rec.dma_id, rec)
        else:
            tok = ("c", eng, rec.idx, rec)
            self.last_real[eng] = rec
        for r in reads:
            r.readers.append(tok)
        for w in writes:
            w.writers = [tok]
            w.readers = []
        for r in psr:
            r.writers = [tok]
            r.readers = []
        st.append(rec)
        return rec

    def barrier(self):
        for eng in ENGS:
            rec = Rec(eng, None, False)
            rec.idx = len(self.streams[eng])
            for e in CENG:
                lr = self.last_real[e]
                if lr is not None:
                    self._need(rec, ("c", e, lr.idx, lr), True)
            for q in NDMA:
                for slot, d in self.dma_live[q].items():
                    self._need(rec, ("d", q, d.dma_id, d), True)
            self._commit(rec)
            self.streams[eng].append(rec)
        self.res = {}

    def emit(self, es):
        nc = self.nc
        sems = {e: [es.enter_context(nc.semaphore(f"c_{e}_{i}")) for i in range(RROT)] for e in CENG}
        dsems = {q: [es.enter_context(nc.semaphore(f"d_{q}_{i}")) for i in range(n)] for q, n in NDMA.items()}
        for e in ENGS:
            k = 0
            for rec in self.streams[e]:
                if rec.is_dma or rec.fn is None:
                    continue
                if rec.signal:
                    rec.sig_k = k
                    k += 1

        def csem(e, k):
            return sems[e][(k // CH) % RROT], (k // (CH * RROT)) * CH + (k % CH) + 1

        def dsem(q, i):
            n = NDMA[q]
            return dsems[q][i % n], 16 * (i // n + 1)

        block = es.enter_context(nc.Block())

        def run(ename):
            def body(eng):
                for rec in self.streams[ename]:
                    for e, (i, prod) in rec.cwaits.items():
                        s, v = csem(e, prod.sig_k)
                        eng.wait_ge(s, v)
                    for (q, i) in rec.dwaits:
                        s, v = dsem(q, i)
                        eng.wait_ge(s, v)
                    if rec.fn is None:
                        continue
                    if rec.is_dma:
                        n = NDMA[ename]
                        if rec.dma_id >= n:
                            s, v = dsem(ename, rec.dma_id - n)
                            eng.wait_ge(s, v)
                    ins = rec.fn(eng)
                    if rec.is_dma:
                        s, v = dsem(ename, rec.dma_id)
                        ins.then_inc(s, 16)
                    elif rec.signal:
                        s, v = csem(ename, rec.sig_k)
                        ins.then_inc(s, 1)
                if ename in NDMA and self.ndma[ename] > 0:
                    n = NDMA[ename]
                    tot = self.ndma[ename]
                    for slot in range(min(n, tot)):
                        last = ((tot - 1 - slot) // n) * n + slot
                        s, v = dsem(ename, last)
                        eng.wait_ge(s, v)
            return body

        block.sync(run("sp"))
        block.tensor(run("pe"))
        block.scalar(run("act"))
        block.vector(run("dve"))
        block.gpsimd(run("pool"))


def tres(name, t0, n):
    return [f"{name}:{i}" for i in range(t0 // 128, (t0 + n + 127) // 128)]


def build(n_layers=DEPTH, debug=False, stop=None):
    nc = bass.Bass("TRN2", target_bir_lowering=False)

    def din(name, shape, dt=F32):
        return nc.dram_tensor(name, list(shape), dt, kind="ExternalInput").ap()

    def dscr(name, shape, dt):
        return nc.dram_tensor(name, list(shape), dt, kind="ExternalOutput" if debug else "Internal").ap()

    x_d = din("x", [TL, D])
    ctx_d = din("ctx", [TC, D])
    cc_d = din("cc", [128, 8, 2])
    w_ada_d = din("w_ada", [DEPTH, D, 6 * D])
    b_ada_d = din("b_ada_c", [128, DEPTH, 48])
    nmix_d = din("nmix_c", [128, DEPTH, 8])
    nmlp_d = din("nmlp_c", [128, DEPTH, 8])
    w_in_d = din("w_in_p", [DEPTH, D, INCP])
    bg_d = din("bg_c", [16, DEPTH])
    conv_d = din("conv_c", [128, DEPTH, 3, 8])
    qkg_d = din("qk_gain", [128, DEPTH, 640])
    mlg_d = din("ml_gain", [128, DEPTH, 512])
    w_out_d = din("w_out_p", [DEPTH, D, D])
    w1_d = din("w_mlp_in", [DEPTH, D, FFN])
    w2_d = din("w_mlp_out", [DEPTH, FFN, D])
    nfin_d = din("nfin_c", [128, 8])
    ident_d = din("ident", [128, 128])
    maskf_d = din("maskf", [128, 128])
    maskb_d = din("maskb", [128, 128])
    sel_d = din("sel4", [128, 4, 128])
    cos_d = din("rope_cos", [128, 32, 32])
    sin_d = din("rope_sin", [128, 32, 32])
    out_d = nc.dram_tensor("out", [TL, D], F32, kind="ExternalOutput").ap()

    xT_d = dscr("xT", [8, 128, T], F32)
    mqkT_d = dscr("mqkT", [8, 128, T], F32)
    gT_d = dscr("gT", [16, T], F32)
    QT_d = dscr("QT", [4, 128, T], BF16)
    KT_d = dscr("KT", [128, T], BF16)
    tmv_d = dscr("tmv", [T, 1152], BF16)
    attT_d = dscr("attT", [8, 128, T], BF16)
    hf_d = dscr("hf", [T, 512], F32)
    hb_d = dscr("hb", [T, 512], F32)

    P = Prog(nc)
    es = ExitStack()
    arena = es.enter_context(nc.sbuf_tensor("arena", [128, ARENA // 4], F32))
    psum = es.enter_context(nc.psum_tensor("psum", [128, 4096], F32))

    class Alloc:
        def __init__(self, base=0):
            self.off = base

        def get(self, shape, dt, parts=128):
            n = 1
            for s in shape:
                n *= s
            es_ = 4 if dt == F32 else 2
            nb = (n * es_ + 3) // 4 * 4
            w0 = self.off // 4
            ap = arena[0:parts, w0:w0 + nb // 4]
            self.off += nb
            assert self.off <= ARENA, f"arena overflow {self.off}"
            if dt == BF16:
                ap = ap.bitcast(BF16)
            if len(shape) == 2:
                ap = ap.rearrange("p (a b) -> p a b", a=shape[0])
            elif len(shape) == 3:
                ap = ap.rearrange("p (a b c) -> p a b c", a=shape[0], b=shape[1])
            return ap

    def bank(i):
        return psum[:, i * 512:(i + 1) * 512]

    def bank_bf(i):
        return psum[:, i * 512:(i + 1) * 512].bitcast(BF16)

    class Rot:
        def __init__(self, items):
            self.items = items
            self.i = 0

        def next(self):
            it = self.items[self.i % len(self.items)]
            self.i += 1
            return it

    def dma(q, out, in_, reads=(), writes=()):
        P.op(q, lambda e: e.dma_start(out=out, in_=in_), reads=reads, writes=writes, is_dma=True)

    A0 = Alloc(0)
    identF = A0.get([128], F32)
    identB = A0.get([128], BF16)
    onesB = A0.get([128], BF16)
    maskT = [A0.get([128], F32), A0.get([128], F32)]
    selF = A0.get([4, 128], F32)
    selB = A0.get([4, 128], BF16)
    cst = A0.get([4], F32)
    cc = A0.get([8, 2], F32)
    sc = A0.get([8, 2], F32)
    mod = A0.get([DEPTH * 48, 2], F32)
    s1 = A0.get([DEPTH * 8, 2], F32)
    s2 = A0.get([DEPTH * 8, 2], F32)
    nmix = A0.get([DEPTH * 8], F32)
    nmlp = A0.get([DEPTH * 8], F32)
    bada = A0.get([DEPTH * 48], F32)
    bgc = A0.get([DEPTH], F32, parts=16)
    convc = A0.get([DEPTH * 3 * 8], F32)
    nfin = A0.get([8], F32)
    PBASE = A0.off

    def modcol(L, j, r):
        return mod[:, L * 48 + j, r:r + 1]

    dma("sp", identF, ident_d, writes=["identF"])
    dma("sp", maskT[0], maskf_d, writes=["maskf"])
    dma("sp", maskT[1], maskb_d, writes=["maskb"])
    dma("sp", selF, sel_d, writes=["selF"])
    dma("sp", cc, cc_d, writes=["cc"])
    dma("sp", nmix, nmix_d.rearrange("p l j -> p (l j)"), writes=["nmix"])
    dma("sp", nmlp, nmlp_d.rearrange("p l j -> p (l j)"), writes=["nmlp"])
    dma("sp", bada, b_ada_d.rearrange("p l j -> p (l j)"), writes=["bada"])
    dma("sp", bgc, bg_d, writes=["bgc"])
    dma("sp", convc, conv_d.rearrange("p l a c -> p (l a c)"), writes=["convc"])
    dma("sp", nfin, nfin_d, writes=["nfin"])
    P.op("act", lambda e: e.activation(out=identB, in_=identF, func=AF.Copy), reads=["identF"], writes=["identB"])
    P.op("pool", lambda e: e.memset(onesB, 1.0), writes=["onesB"])
    P.op("act", lambda e: e.activation(out=selB, in_=selF, func=AF.Copy), reads=["selF"], writes=["selB"])
    P.op("pool", lambda e: e.memset(cst[:, 0:1], EPS), writes=["cst0"])
    P.op("pool", lambda e: e.memset(cst[:, 1:2], 1.0), writes=["cst1"])
    P.op("pool", lambda e: e.memset(cst[:, 2:3], -0.5 * math.log(128.0)), writes=["cst2"])
    P.op("pool", lambda e: e.memset(cst[:, 3:4], 0.0), writes=["cst3"])
    CST = ["cst0", "cst1", "cst2", "cst3"]
    P.op("act", lambda e: e.activation(out=sc, in_=cc, func=AF.Silu), reads=["cc"], writes=["sc"])

    def phase_mod():
        A = Alloc(PBASE)
        wa = [A.get([8, 512], F32), A.get([8, 512], F32)]
        modps = bank(0)
        it = 0
        for L in range(n_layers):
            wsrc = w_ada_d[L].rearrange("(k p) c -> p k c", p=128)
            for pc in range(12):
                buf = wa[it % 2]
                bn = f"wa{it % 2}"
                it += 1
                dma("sp", buf, wsrc[:, :, pc * 512:(pc + 1) * 512], writes=[bn])
                for cjj in range(4):
                    j = pc * 4 + cjj
                    col = (L * 48 + j) * 2
                    for k in range(8):
                        P.op("pe", (lambda buf=buf, k=k, cjj=cjj, col=col: lambda e: e.matmul(
                            modps[:, col:col + 2], lhsT=buf[:, k, cjj * 128:(cjj + 1) * 128], rhs=sc[:, k, :],
                            start=(k == 0), stop=(k == 7)))(),
                            reads=[bn, "sc"], writes=["modps"])
        nl = n_layers * 48
        P.op("dve", lambda e: e.tensor_tensor(
            out=mod[:, 0:nl, :], in0=modps[:, 0:nl * 2].rearrange("p (a b) -> p a b", b=2),
            in1=bada[:, 0:nl].unsqueeze(2).to_broadcast([128, nl, 2]), op=ALU.add),
            reads=["modps", "bada"], writes=["mod"])
        for L in range(n_layers):
            for (dst, nm, nmn, jo) in ((s1, nmix, "nmix", 8), (s2, nmlp, "nmlp", 32)):
                P.op("dve", (lambda dst=dst, L=L, jo=jo: lambda e: e.tensor_scalar_add(
                    out=dst[:, L * 8:(L + 1) * 8, :], in0=mod[:, L * 48 + jo:L * 48 + jo + 8, :], scalar1=1.0))(),
                    reads=["mod"], writes=["s12"])
                P.op("dve", (lambda dst=dst, L=L, nm=nm: lambda e: e.tensor_tensor(
                    out=dst[:, L * 8:(L + 1) * 8, :], in0=dst[:, L * 8:(L + 1) * 8, :],
                    in1=nm[:, L * 8:(L + 1) * 8].unsqueeze(2).to_broadcast([128, 8, 2]), op=ALU.mult))(),
                    reads=["s12", nmn], writes=["s12"])
        P.barrier()

    def phase_xT():
        A = Alloc(PBASE)
        xin = [A.get([1024], F32), A.get([1024], F32)]
        xst = [A.get([8, 128], F32), A.get([8, 128], F32)]
        rot = Rot([1, 2, 3, 4])
        for g in range(NCH):
            src = ctx_d[g * 128:(g + 1) * 128, :] if g < 2 else x_d[(g - 2) * 128:(g - 1) * 128, :]
            xb, xn_ = xin[g % 2], f"xin{g % 2}"
            st, stn = xst[g % 2], f"xst{g % 2}"
            dma("sp", xb, src, writes=[xn_])
            for half in range(2):
                b = rot.next()
                bk = bank(b)
                for jj in range(4):
                    j = half * 4 + jj
                    P.op("pe", (lambda bk=bk, jj=jj, j=j, xb=xb: lambda e: e.transpose(
                        out=bk[:, jj * 128:(jj + 1) * 128], in_=xb[:, j * 128:(j + 1) * 128], identity=identF))(),
                        reads=[xn_, "identF"], writes=[f"pb{b}"])
                eng = "act" if half == 0 else "dve"
                if eng == "act":
                    P.op("act", (lambda bk=bk, st=st, half=half: lambda e: e.activation(
                        out=st[:, half * 4:half * 4 + 4, :], in_=bk.rearrange("p (a b) -> p a b", a=4), func=AF.Copy))(),
                        reads=[f"pb{b}"], writes=[stn])
                else:
                    P.op("dve", (lambda bk=bk, st=st, half=half: lambda e: e.tensor_copy(
                        out=st[:, half * 4:half * 4 + 4, :], in_=bk.rearrange("p (a b) -> p a b", a=4)))(),
                        reads=[f"pb{b}"], writes=[stn])
            dma("sp", xT_d[:, :, g * 128:(g + 1) * 128].rearrange("j p t -> p j t"), st, reads=[stn],
                writes=tres("xT", g * 128, 128))
        P.barrier()

    def norm_mod(xTb, xname, n, sq, rstd, xn, hT, hname, scol, bcol, ssb):
        ssps = bank(ssb)
        for j in range(8):
            s_, sn = sq[j % 2], f"sq{j % 2}"
            P.op("act", (lambda s_=s_, j=j: lambda e: e.activation(out=s_[:, 0:n], in_=xTb[:, j, 0:n], func=AF.Square))(),
                 reads=[xname], writes=[sn])
            P.op("pe", (lambda s_=s_, j=j: lambda e: e.matmul(ssps[:, 0:n], lhsT=onesB, rhs=s_[:, 0:n],
                                                              start=(j == 0), stop=(j == 7)))(),
                 reads=[sn, "onesB"], writes=[f"pb{ssb}"])
        P.op("act", lambda e: e.activation(out=rstd[:, 0:n], in_=ssps[:, 0:n], func=AF.Sqrt, scale=1.0 / D, bias=cst[:, 0:1]),
             reads=[f"pb{ssb}"] + CST, writes=["rstd"])
        P.op("dve", lambda e: e.reciprocal(out=rstd[:, 0:n], in_=rstd[:, 0:n]), reads=["rstd"], writes=["rstd"])
        for j in range(8):
            x_, xn_ = xn[j % 2], f"xn{j % 2}"
            P.op("dve", (lambda x_=x_, j=j: lambda e: e.tensor_tensor(out=x_[:, 0:n], in0=xTb[:, j, 0:n], in1=rstd[:, 0:n],
                                                                      op=ALU.mult))(),
                 reads=[xname, "rstd"], writes=[xn_])
            P.op("act", (lambda x_=x_, j=j: lambda e: e.activation(out=hT[:, j, 0:n], in_=x_[:, 0:n], func=AF.Identity,
                                                                   scale=scol(j), bias=bcol(j)))(),
                 reads=[xn_, "s12", "mod"], writes=[hname])

    def phase_inproj(L):
        A = Alloc(PBASE)
        w_in = A.get([8, INCP], BF16)
        qkg = A.get([640], F32)
        cosT = A.get([32, 32], F32)
        sinT = A.get([32, 32], F32)
        dma("sp", cosT, cos_d, writes=["cosT"])
        dma("sp", sinT, sin_d, writes=["sinT"])
        xTb = [A.get([8, 512], F32), A.get([8, 512], F32)]
        sq = [A.get([512], BF16), A.get([512], BF16)]
        rstd = A.get([512], F32)
        xn = [A.get([512], F32), A.get([512], F32)]
        hT = A.get([8, 512], BF16)
        qk_sb = A.get([640], F32)
        sqq = A.get([640], F32)
        ss = A.get([10], F32)
        qn = A.get([640], F32)
        ra = A.get([320], F32)
        rb = A.get([320], F32)
        qr = A.get([640], BF16)
        QTst = A.get([4, 512], BF16)
        KTst = A.get([512], BF16)
        tmvst = [A.get([1152], BF16), A.get([1152], BF16)]
        fmst = A.get([8, 512], F32)
        gst = A.get([512], F32, parts=16)

        wsrc = w_in_d[L].rearrange("(k p) c -> p k c", p=128)
        for (c0, c1) in ((0, 512), (512, 768), (768, 1792), (1792, 2304), (2304, 2816), (2816, 2944)):
            if 'w' not in KSKIP:
                dma("pool", w_in[:, :, c0:c1], wsrc[:, :, c0:c1], writes=[f"w_in{c0}"])
        dma("sp", qkg, qkg_d[:, L, :], writes=["qkg"])
        rot = Rot([2, 3, 4] if 'b' in KSKIP else [2, 3, 4, 5, 6, 7])
        blocks = [(0, 256)] + [(256 + 512 * i, 512) for i in range(8)]
        def do_block(bi, t0, n):
            r = 1 if t0 < TC else 0
            xb, xname = xTb[bi % 2], f"xTb{bi % 2}"
            dma("sp", xb[:, :, 0:n], xT_d[:, :, t0:t0 + n].rearrange("j p t -> p j t"),
                reads=tres("xT", t0, n), writes=[xname])
            if 'n' not in KSKIP:
                norm_mod(xb, xname, n, sq, rstd, xn, hT, "hT",
                         lambda j: s1[:, L * 8 + j, r:r + 1], lambda j: modcol(L, j, r), 0)
            for m in range(0 if 'm' in KSKIP else n // 128):
                tok0 = t0 + m * 128
                ms = slice(m * 128, (m + 1) * 128)
                tv, tvn = tmvst[m % 2], f"tmvst{m % 2}"
                groups = ((0, 512, "w_in0"), (512, 768, "w_in512"), (1792, 2304, "w_in1792"), (2304, 2816, "w_in2304"))
                pbs = []
                for (c0, c1, wn) in groups:
                    b = rot.next()
                    pbs.append(b)
                    for k in range(8):
                        P.op("pe", (lambda b=b, k=k, c0=c0, c1=c1, ms=ms: lambda e: e.matmul(
                            bank(b)[:, 0:c1 - c0], lhsT=hT[:, k, ms], rhs=w_in[:, k, c0:c1], start=(k == 0), stop=(k == 7)))(),
                            reads=["hT", wn], writes=[f"pb{b}"])
                b0, b1, b2, b3 = pbs
                P.op("act", (lambda b0=b0: lambda e: e.activation(out=qk_sb[:, 0:512], in_=bank(b0), func=AF.Copy))(),
                     reads=[f"pb{b0}"], writes=["qk_sb_a"])
                P.op("act", (lambda b1=b1: lambda e: e.activation(out=qk_sb[:, 512:640], in_=bank(b1)[:, 0:128], func=AF.Copy))(),
                     reads=[f"pb{b1}"], writes=["qk_sb_b"])
                P.op("dve", (lambda b1=b1, tv=tv: lambda e: e.tensor_copy(out=tv[:, 0:128], in_=bank(b1)[:, 128:256]))(),
                     reads=[f"pb{b1}"], writes=[tvn + "a"])
                P.op("dve", (lambda b2=b2, tv=tv: lambda e: e.tensor_copy(out=tv[:, 128:640], in_=bank(b2)))(),
                     reads=[f"pb{b2}"], writes=[tvn + "b"])
                P.op("act", (lambda b3=b3, tv=tv: lambda e: e.activation(out=tv[:, 640:1152], in_=bank(b3), func=(AF.Copy if 's' in KSKIP else AF.Sigmoid)))(),
                     reads=[f"pb{b3}"], writes=[tvn + "c"])
                dma("sp", tmv_d[tok0:tok0 + 128, :], tv, reads=[tvn + "a", tvn + "b", tvn + "c"], writes=tres("tmv", tok0, 128))
                if 'q' not in KSKIP:
                    QKS = ["qk_sb_a", "qk_sb_b"]
                    P.op("dve", lambda e: e.tensor_tensor(out=sqq, in0=qk_sb, in1=qk_sb, op=ALU.mult), reads=QKS, writes=["sqq"])
                    P.op("dve", lambda e: e.reduce_sum(out=ss, in_=sqq.rearrange("p (h d) -> p h d", h=10), axis=AX.X),
                         reads=["sqq"], writes=["ss"])
                    P.op("act", lambda e: e.activation(out=ss, in_=ss, func=AF.Sqrt, scale=1.0 / 64, bias=cst[:, 0:1]),
                         reads=["ss"] + CST, writes=["ss"])
                    P.op("dve", lambda e: e.reciprocal(out=ss, in_=ss), reads=["ss"], writes=["ss"])
                    P.op("pool", lambda e: e.tensor_tensor(out=qn.rearrange("p (h d) -> p h d", h=10),
                                                           in0=qk_sb.rearrange("p (h d) -> p h d", h=10),
                                                           in1=ss.unsqueeze(2).to_broadcast([128, 10, 64]), op=ALU.mult),
                         reads=QKS + ["ss"], writes=["qn"])
                    P.op("pool", lambda e: e.tensor_tensor(out=qn, in0=qn, in1=qkg, op=ALU.mult), reads=["qn", "qkg"], writes=["qn"])
                    if r == 0:
                        gi = (tok0 - TC) // 128
                        q4 = qn.rearrange("p (h i two) -> p h i two", h=10, two=2)
                        o4 = qr.rearrange("p (h i two) -> p h i two", h=10, two=2)
                        x0, x1 = q4[:, :, :, 0], q4[:, :, :, 1]
                        cb = cosT[:, gi, :].unsqueeze(1).to_broadcast([128, 10, 32])
                        sb_ = sinT[:, gi, :].unsqueeze(1).to_broadcast([128, 10, 32])
                        ra3 = ra.rearrange("p (h i) -> p h i", h=10)
                        rb3 = rb.rearrange("p (h i) -> p h i", h=10)
                        P.op("pool", (lambda x0=x0, cb=cb: lambda e: e.tensor_tensor(out=ra3, in0=x0, in1=cb, op=ALU.mult))(),
                             reads=["qn", "cosT"], writes=["ra"])
                        P.op("pool", (lambda x1=x1, sb_=sb_: lambda e: e.tensor_tensor(out=rb3, in0=x1, in1=sb_, op=ALU.mult))(),
                             reads=["qn", "sinT"], writes=["rb"])
                        P.op("pool", (lambda o4=o4: lambda e: e.tensor_tensor(out=o4[:, :, :, 0], in0=ra3, in1=rb3, op=ALU.subtract))(),
                             reads=["ra", "rb"], writes=["qr0"])
                        P.op("pool", (lambda x0=x0, sb_=sb_: lambda e: e.tensor_tensor(out=ra3, in0=x0, in1=sb_, op=ALU.mult))(),
                             reads=["qn", "sinT", "qr0"], writes=["ra"])
                        P.op("pool", (lambda x1=x1, cb=cb: lambda e: e.tensor_tensor(out=rb3, in0=x1, in1=cb, op=ALU.mult))(),
                             reads=["qn", "cosT", "qr0"], writes=["rb"])
                        P.op("pool", (lambda o4=o4: lambda e: e.tensor_tensor(out=o4[:, :, :, 1], in0=ra3, in1=rb3, op=ALU.add))(),
                             reads=["ra", "rb"], writes=["qr1"])
                        QR = ["qr0", "qr1"]
                    else:
                        P.op("pool", lambda e: e.tensor_copy(out=qr, in_=qn), reads=["qn"], writes=["qr0"])
                        QR = ["qr0"]
                    tb = bank_bf(1)
                    for jq in range(5):
                        P.op("pe", (lambda jq=jq: lambda e: e.transpose(out=tb[:, jq * 128:(jq + 1) * 128],
                                                                        in_=qr[:, jq * 128:(jq + 1) * 128], identity=identB))(),
                             reads=QR + ["identB"], writes=["pb1"])
                    P.op("dve", (lambda ms=ms: lambda e: e.tensor_copy(out=QTst[:, :, ms],
                                                                       in_=tb[:, 0:512].rearrange("p (a b) -> p a b", a=4)))(),
                         reads=["pb1"], writes=["QTst"])
                    P.op("act", (lambda ms=ms: lambda e: e.activation(out=KTst[:, ms], in_=tb[:, 512:640], func=AF.Copy))(),
                         reads=["pb1"], writes=["KTst"])
            dma("sp", QT_d[:, :, t0:t0 + n].rearrange("j p t -> p j t"), QTst[:, :, 0:n], reads=["QTst"], writes=tres("QT", t0, n))
            dma("sp", KT_d[:, t0:t0 + n], KTst[:, 0:n], reads=["KTst"], writes=tres("KT", t0, n))
            if 'f' not in KSKIP:
                for cj in range(8):
                    b = rot.next()
                    for k in range(8):
                        P.op("pe", (lambda b=b, k=k, cj=cj: lambda e: e.matmul(
                            bank(b)[:, 0:n], lhsT=w_in[:, k, 768 + cj * 128:768 + (cj + 1) * 128], rhs=hT[:, k, 0:n],
                            start=(k == 0), stop=(k == 7)))(), reads=["hT", "w_in768"], writes=[f"pb{b}"])
                    if cj % 2 == 0:
                        P.op("act", (lambda b=b, cj=cj: lambda e: e.activation(out=fmst[:, cj, 0:n], in_=bank(b)[:, 0:n], func=AF.Copy))(),
                             reads=[f"pb{b}"], writes=[f"fmst{cj}"])
                    else:
                        P.op("dve", (lambda b=b, cj=cj: lambda e: e.tensor_copy(out=fmst[:, cj, 0:n], in_=bank(b)[:, 0:n]))(),
                             reads=[f"pb{b}"], writes=[f"fmst{cj}"])
                dma("sp", mqkT_d[:, :, t0:t0 + n].rearrange("j p t -> p j t"), fmst[:, :, 0:n],
                    reads=[f"fmst{cj}" for cj in range(8)], writes=tres("mqkT", t0, n))
                b = rot.next()
                for k in range(8):
                    P.op("pe", (lambda b=b, k=k: lambda e: e.matmul(bank(b)[:, 0:n], lhsT=w_in[:, k, 2816:2944], rhs=hT[:, k, 0:n],
                                                                  start=(k == 0), stop=(k == 7)))(),
                         reads=["hT", "w_in2816"], writes=[f"pb{b}"])
                P.op("act", (lambda b=b: lambda e: e.activation(out=gst[:, 0:n], in_=bank(b)[0:16, 0:n], func=AF.Identity,
                                                                bias=bgc[:, L:L + 1], scale=1.0))(),
                     reads=[f"pb{b}", "bgc"], writes=["gst"])
                dma("sp", gT_d[:, t0:t0 + n], gst[:, 0:n], reads=["gst"], writes=tres("gT", t0, n))
        for bi, (t0, n) in enumerate(blocks):
            do_block(bi, t0, n)
        P.barrier()

    WOFF = ARENA - 131072

    def w4_views():
        A = Alloc(WOFF)
        return A.get([8, FFN], BF16), A.get([32, D], BF16)

    def phase_attn(L, emit_ctx, prefetch):
        A = Alloc(PBASE)
        KTs = A.get([T], BF16)
        Vaug = A.get([NCH, 2, 128], BF16)
        QA = [[A.get([4, 512], BF16), A.get([4, 512], BF16)] for _ in range(2)]
        PT = [A.get([512], BF16) for _ in range(4)]
        rec = A.get([512], F32)
        attst = [A.get([4, 512], BF16), A.get([4, 512], BF16)]
        assert A.off <= WOFF
        if prefetch:
            w1, w2 = w4_views()
            for k in range(8):
                dma("pool", w1[:, k, :], w1_d[L, k * 128:(k + 1) * 128, :], writes=[f"w1_{k}"])
            for k4 in range(8):
                dma("pool", w2[:, k4 * 4:(k4 + 1) * 4, :],
                    w2_d[L, k4 * 512:(k4 + 1) * 512, :].rearrange("(k p) c -> p k c", p=128), writes=[f"w2_{k4}"])
        dma("sp", KTs, KT_d, writes=["KTs"])
        for bq in range(2):
            P.op("pool", (lambda bq=bq: lambda e: e.memset(QA[bq][0][64:128], 0.0))(), writes=[f"QAz{bq}0"])
            P.op("pool", (lambda bq=bq: lambda e: e.memset(QA[bq][1][0:64], 0.0))(), writes=[f"QAz{bq}1"])
        P.op("pool", lambda e: e.memset(Vaug, 1.0), writes=["Vaug"])
        vsrc = tmv_d.rearrange("(c p) f -> p c f", p=128)
        dma("sp", Vaug[:, :, 0, 0:64], vsrc[:, :, 0:64], reads=["Vaug"], writes=["Vaug0"])
        dma("sp", Vaug[:, :, 1, 64:128], vsrc[:, :, 64:128], reads=["Vaug"], writes=["Vaug1"])
        srot = Rot([0, 1, 2, 3])
        orot = Rot([4, 5])
        prot = Rot([0, 1, 2, 3])
        qblocks = [(256 + 512 * i, 512, list(range(NCH))) for i in range(8)]
        if emit_ctx:
            qblocks = [(0, 256, [0, 1])] + qblocks
        def do_q(qi, t0, n, kbs):
            qa, qbn = QA[qi % 2], f"QA{qi % 2}"
            ast, astn = attst[qi % 2], f"attst{qi % 2}"
            dma("sp", qa[0][0:64, :, 0:n], QT_d[:, 0:64, t0:t0 + n].rearrange("j p t -> p j t"), reads=[f"QAz{qi % 2}0"], writes=[qbn + "h0"])
            dma("sp", qa[1][64:128, :, 0:n], QT_d[:, 64:128, t0:t0 + n].rearrange("j p t -> p j t"), reads=[f"QAz{qi % 2}1"], writes=[qbn + "h1"])
            SK = 2
            its = []
            for j in range(4):
                for hh in range(2):
                    ob = orot.next()
                    for ki, kb in enumerate(kbs):
                        its.append((j, hh, ob, ki, kb))
            nk = len(kbs)
            slots = {}

            def emit_S(i):
                j, hh, ob, ki, kb = its[i]
                sbk = srot.next()
                pi = prot.next()
                pt = PT[pi]
                slots[i] = (pi, pt)
                P.op("pe", (lambda sbk=sbk, kb=kb, hh=hh, j=j: lambda e: e.matmul(
                    bank(sbk)[:, 0:n], lhsT=KTs[:, kb * 128:(kb + 1) * 128], rhs=qa[hh][:, j, 0:n], start=True, stop=True))(),
                    reads=["KTs", qbn + f"h{hh}"], writes=[f"pb{sbk}"])
                P.op("act", (lambda sbk=sbk, pt=pt: lambda e: e.activation(out=pt[:, 0:n], in_=bank(sbk)[:, 0:n], func=AF.Exp,
                                                                           scale=0.125))(),
                     reads=[f"pb{sbk}"], writes=[f"PT{pi}"])

            def emit_PV(i):
                j, hh, ob, ki, kb = its[i]
                pi, pt = slots.pop(i)
                hs = slice(hh * 64, hh * 64 + 64)
                os_ = slice((1 - hh) * 64, (1 - hh) * 64 + 64)
                P.op("pe", (lambda ob=ob, kb=kb, hh=hh, pt=pt, ki=ki: lambda e: e.matmul(
                    bank(ob)[:, 0:n], lhsT=Vaug[:, kb, hh, :], rhs=pt[:, 0:n], start=(ki == 0), stop=(ki == nk - 1)))(),
                    reads=["Vaug0", "Vaug1", f"PT{pi}"], writes=[f"pb{ob}"])
                if ki == nk - 1:
                    P.op("dve", (lambda ob=ob, hs=hs, os_=os_: lambda e: e.reciprocal(out=rec[hs, 0:n], in_=bank(ob)[os_, 0:n]))(),
                         reads=[f"pb{ob}"], writes=["rec"])
                    P.op("dve", (lambda ob=ob, hs=hs, j=j: lambda e: e.tensor_tensor(
                        out=ast[hs, j, 0:n], in0=bank(ob)[hs, 0:n], in1=rec[hs, 0:n], op=ALU.mult))(),
                        reads=[f"pb{ob}", "rec"], writes=[astn])

            for i in range(len(its) + SK):
                if i < len(its):
                    emit_S(i)
                if i - SK >= 0:
                    emit_PV(i - SK)
            dma("sp", attT_d[0:4, :, t0:t0 + n].rearrange("j p t -> p j t"), ast[:, :, 0:n], reads=[astn],
                writes=tres("attT_a", t0, n))
        for qi, (t0, n, kbs) in enumerate(qblocks):
            do_q(qi, t0, n, kbs)
        P.barrier()

    def phase_mlstm(L, emit_ctx):
        ATOP = Alloc(PBASE)
        Rr128 = [ATOP.get([NCH, 2, 128], BF16), ATOP.get([NCH, 2, 128], BF16)]
        Rr = [Rr128[0][0:4], Rr128[1][0:4]]
        emtT = [ATOP.get([NCH, 4], F32), ATOP.get([NCH, 4], F32)]
        decbc = [ATOP.get([4, NCH], F32), ATOP.get([4, NCH], F32)]
        PB2 = ATOP.off
        A = Alloc(PB2)
        Gi = [A.get([T], F32, parts=4), A.get([T], F32, parts=4)]
        Gf = [A.get([T], F32, parts=4), A.get([T], F32, parts=4)]
        zer = A.get([T], F32, parts=4)
        Nb = A.get([T], F32, parts=4)
        Ut = A.get([T], F32, parts=4)
        tmp128 = A.get([T], F32)
        tmp = tmp128[0:4]
        Uend = A.get([NCH], F32, parts=4)
        Uprev = A.get([NCH], F32, parts=4)
        dec128 = A.get([NCH], F32)
        dec = dec128[0:4]
        dma("sp", Gi[0], gT_d[0:4, :], writes=["Gi0"])
        dma("sp", Gf[0], gT_d[4:8, :], writes=["Gf0"])
        dma("sp", Gi[1], gT_d[8:12, :], writes=["Gi1"])
        dma("sp", Gf[1], gT_d[12:16, :], writes=["Gf1"])
        P.op("pool", lambda e: e.memset(zer, 0.0), writes=["zer"])
        P.op("pool", lambda e: e.memset(tmp128, 0.0), writes=["tmp"])
        P.op("pool", lambda e: e.memset(dec128, 0.0), writes=["dec"])
        P.op("pool", lambda e: e.memset(Rr128[0], 0.0), writes=["Rr0a", "Rr0b"])
        P.op("pool", lambda e: e.memset(Rr128[1], 0.0), writes=["Rr1a", "Rr1b"])
        c1 = cst[0:4, 1:2]
        for d in range(2):
            gi, gf = Gi[d], Gf[d]
            gin, gfn = f"Gi{d}", f"Gf{d}"
            P.op("act", (lambda gf=gf: lambda e: e.activation(out=gf, in_=gf, func=AF.Exp, scale=-1.0))(), reads=[gfn], writes=[gfn])
            P.op("act", (lambda gf=gf: lambda e: e.activation(out=gf, in_=gf, func=AF.Ln, bias=c1, scale=1.0))(),
                 reads=[gfn] + CST, writes=[gfn])

            def seg(ap, lo, hi, d=d):
                v = ap[:, lo:hi]
                return v[:, ::-1] if d == 1 else v
            if d == 0:
                P.op("dve", (lambda gf=gf: lambda e: e.tensor_tensor_scan(out=Nb, data0=gf, data1=zer, initial=0.0,
                                                                          op0=ALU.add, op1=ALU.add))(),
                     reads=[gfn, "zer"], writes=["Nb"])
            else:
                P.op("dve", (lambda gf=gf: lambda e: e.tensor_tensor_scan(out=seg(Nb, 0, TC), data0=seg(gf, 0, TC), data1=zer[:, 0:TC],
                                                                          initial=0.0, op0=ALU.add, op1=ALU.add))(),
                     reads=[gfn, "zer"], writes=["Nb"])
                P.op("dve", (lambda gf=gf: lambda e: e.tensor_tensor_scan(out=seg(Nb, TC, T), data0=seg(gf, TC, T), data1=zer[:, TC:T],
                                                                          initial=Nb[:, 0:1], op0=ALU.add, op1=ALU.add))(),
                     reads=[gfn, "zer", "Nb"], writes=["Nb"])
            P.op("dve", (lambda gi=gi: lambda e: e.tensor_tensor(out=gi, in0=gi, in1=Nb, op=ALU.add))(), reads=[gin, "Nb"], writes=[gin])
            if d == 0:
                P.op("dve", (lambda gi=gi: lambda e: e.tensor_tensor_scan(out=Ut, data0=gi, data1=zer, initial=0.0,
                                                                          op0=ALU.max, op1=ALU.max))(),
                     reads=[gin, "zer"], writes=["Ut"])
            else:
                P.op("dve", (lambda gi=gi: lambda e: e.tensor_tensor_scan(out=seg(Ut, 0, TC), data0=seg(gi, 0, TC), data1=zer[:, 0:TC],
                                                                          initial=0.0, op0=ALU.max, op1=ALU.max))(),
                     reads=[gin, "zer"], writes=["Ut"])
                P.op("dve", (lambda gi=gi: lambda e: e.tensor_tensor_scan(out=seg(Ut, TC, T), data0=seg(gi, TC, T), data1=zer[:, TC:T],
                                                                          initial=Ut[:, 0:1], op0=ALU.max, op1=ALU.max))(),
                     reads=[gin, "zer", "Ut"], writes=["Ut"])
            U3 = Ut.rearrange("p (c t) -> p c t", t=128)
            endcol = 127 if d == 0 else 0
            P.op("dve", (lambda endcol=endcol: lambda e: e.tensor_copy(out=Uend, in_=U3[:, :, endcol]))(), reads=["Ut"], writes=["Uend"])
            P.op("pool", lambda e: e.memset(Uprev, 0.0), reads=[], writes=["Uprev"])
            if d == 0:
                P.op("dve", lambda e: e.tensor_copy(out=Uprev[:, 1:NCH], in_=Uend[:, 0:NCH - 1]), reads=["Uend", "Uprev"], writes=["Uprev"])
            else:
                P.op("dve", lambda e: e.tensor_copy(out=Uprev[:, 2:NCH - 1], in_=Uend[:, 3:NCH]), reads=["Uend", "Uprev"], writes=["Uprev"])
                P.op("dve", lambda e: e.tensor_copy(out=Uprev[:, NCH - 1:NCH], in_=Uend[:, 0:1]), reads=["Uend", "Uprev"], writes=["Uprev"])
                P.op("dve", lambda e: e.tensor_copy(out=Uprev[:, 0:1], in_=Uend[:, 1:2]), reads=["Uend", "Uprev"], writes=["Uprev"])
            upb = Uprev.unsqueeze(2).to_broadcast([4, NCH, 128])
            t3 = tmp.rearrange("p (c t) -> p c t", t=128)
            g3 = gi.rearrange("p (c t) -> p c t", t=128)
            Rd = Rr[d]
            P.op("dve", (lambda g3=g3: lambda e: e.tensor_tensor(out=t3, in0=g3, in1=upb, op=ALU.subtract))(),
                 reads=[gin, "Uprev"], writes=["tmp"])
            P.op("act", (lambda Rd=Rd: lambda e: e.activation(out=Rd[:, :, 0, :], in_=t3, func=AF.Exp, bias=cst[0:4, 2:3], scale=1.0))(),
                 reads=["tmp"] + CST, writes=[f"Rr{d}a"])
            P.op("dve", lambda e: e.tensor_tensor(out=t3, in0=upb, in1=U3, op=ALU.subtract), reads=["Ut", "Uprev", f"Rr{d}a"], writes=["tmp"])
            P.op("act", (lambda Rd=Rd: lambda e: e.activation(out=Rd[:, :, 1, :], in_=t3, func=AF.Exp))(),
                 reads=["tmp"], writes=[f"Rr{d}b"])
            P.op("dve", lambda e: e.tensor_tensor(out=tmp, in0=Nb, in1=Ut, op=ALU.subtract), reads=["Nb", "Ut", f"Rr{d}b"], writes=["tmp"])
            P.op("act", lambda e: e.activation(out=tmp, in_=tmp, func=AF.Exp), reads=["tmp"], writes=["tmp"])
            eb = 6 + d
            for c in range(NCH):
                P.op("pe", (lambda c=c, eb=eb: lambda e: e.matmul(bank(eb)[:, c * 4:c * 4 + 4], lhsT=tmp128[:, c * 128:(c + 1) * 128],
                                                                  rhs=identF[:, 0:4], start=True, stop=True))(),
                     reads=["tmp", "identF"], writes=[f"pb{eb}"])
            P.op("dve", (lambda d=d, eb=eb: lambda e: e.tensor_copy(out=emtT[d], in_=bank(eb)[:, 0:NCH * 4].rearrange("p (c h) -> p c h", h=4)))(),
                 reads=[f"pb{eb}"], writes=[f"emtT{d}"])
            P.op("dve", lambda e: e.tensor_tensor(out=dec, in0=Uprev, in1=Uend, op=ALU.subtract), reads=["Uprev", "Uend"], writes=["dec"])
            P.op("act", lambda e: e.activation(out=dec, in_=dec, func=AF.Exp), reads=["dec"], writes=["dec"])
            db = 4 + d
            for h in range(4):
                P.op("pe", (lambda h=h, db=db: lambda e: e.matmul(bank(db)[:, h * NCH:(h + 1) * NCH], lhsT=selF[:, h, :], rhs=dec128,
                                                                  start=True, stop=True))(),
                     reads=["dec", "selF"], writes=[f"pb{db}"])
            P.op("dve", (lambda d=d, db=db: lambda e: e.tensor_copy(out=decbc[d], in_=bank(db)[:, 0:4 * NCH].rearrange("p (h c) -> p h c", h=4)))(),
                 reads=[f"pb{db}"], writes=[f"decbc{d}"])
        P.barrier()

        A = Alloc(PB2)
        qkT = A.get([8, T], BF16)
        vaug = A.get([NCH, 4, 129], BF16)
        Cst = A.get([8, 129], F32)
        Cbf = A.get([8, 129], BF16)
        PB3 = A.off
        A2 = Alloc(PB3)
        PL = 2048
        xc = [A2.get([PL + 2], F32), A2.get([PL + 2], F32)]
        accs = [A2.get([PL], F32), A2.get([PL], F32)]
        P.op("pool", lambda e: e.memset(vaug, 1.0), writes=["vaug"])
        for h in range(4):
            dma("sp", vaug[:, :, h, 0:128], tmv_d[:, 128 + h * 128:256 + h * 128].rearrange("(c p) d -> p c d", p=128),
                reads=["vaug"], writes=[f"vaugv{h}"])
        P.op("pool", lambda e: e.memset(Cst, 0.0), writes=["Cst"])
        P.op("pool", lambda e: e.memset(Cbf, 0.0), writes=["Cbf"])
        pieces = [(0, TC, 0, TC), (TC, TC + PL, TC, T), (TC + PL, T, TC, T)]
        it = 0
        for cr in range(8):
            w = [convc[:, (L * 3 + a) * 8 + cr:(L * 3 + a) * 8 + cr + 1] for a in range(3)]
            for (a_, b_, sa, sb) in pieces:
                ln_ = b_ - a_
                lh = a_ > sa
                rh = b_ < sb
                x_, xn_ = xc[it % 2], f"xc{it % 2}"
                acc, accn = accs[it % 2], f"acc{it % 2}"
                it += 1
                lo = a_ - 1 if lh else a_
                hi = b_ + 1 if rh else b_
                c0 = 0 if lh else 1
                dma("sp", x_[:, c0:c0 + hi - lo], mqkT_d[cr, :, lo:hi], writes=[xn_])
                P.op("pool", (lambda x_=x_, w=w, ln_=ln_, acc=acc: lambda e: e.tensor_scalar_mul(out=acc[:, 0:ln_], in0=x_[:, 1:ln_ + 1], scalar1=w[1]))(),
                     reads=[xn_], writes=[accn])
                o0 = 0 if lh else 1
                P.op("dve", (lambda x_=x_, w=w, o0=o0, ln_=ln_, acc=acc: lambda e: e.scalar_tensor_tensor(
                    out=acc[:, o0:ln_], in0=x_[:, o0:ln_], scalar=w[0], in1=acc[:, o0:ln_], op0=ALU.mult, op1=ALU.add))(),
                    reads=[xn_, accn], writes=[accn])
                o1 = ln_ if rh else ln_ - 1
                P.op("dve", (lambda x_=x_, w=w, o1=o1, acc=acc: lambda e: e.scalar_tensor_tensor(
                    out=acc[:, 0:o1], in0=x_[:, 2:o1 + 2], scalar=w[2], in1=acc[:, 0:o1], op0=ALU.mult, op1=ALU.add))(),
                    reads=[xn_, accn], writes=[accn])
                P.op("act", (lambda cr=cr, a_=a_, b_=b_, ln_=ln_, acc=acc: lambda e: e.activation(out=qkT[:, cr, a_:b_], in_=acc[:, 0:ln_], func=AF.Silu))(),
                     reads=[accn], writes=[f"qkT{cr}"])
        P.barrier()

        A3 = Alloc(PB3)
        NR = 8
        kTs = [A3.get([128], BF16) for _ in range(NR)]
        qTs = [A3.get([128], BF16) for _ in range(NR)]
        Sw = [A3.get([128], BF16) for _ in range(NR)]
        k2 = [A3.get([128], BF16) for _ in range(NR)]
        dmx = [A3.get([2], F32) for _ in range(NR)]
        hst = [[A3.get([4, 128], F32) for _ in range(2)] for _ in range(2)]
        order = [list(range(NCH)), [1, 0] + list(range(NCH - 1, 1, -1))]
        hcnt = [0, 0]

        def do_step(i):
            ch = []
            for d in range(2):
                c = order[d][i]
                emit_out = emit_ctx or c >= 2
                hs_, hsn = hst[d][hcnt[d] % 2], f"hst{d}{hcnt[d] % 2}"
                for h in range(4):
                    ch.append((d, h, d * 4 + h, c, emit_out, hs_, hsn))
            for (d, h, x, c, eo, hs_, hsn) in ch:
                Rd = Rr128[d]
                P.op("pe", (lambda x=x, h=h, c=c, Rd=Rd: lambda e: e.matmul(
                    bank(x)[:, 0:256], lhsT=selB[:, h, :], rhs=Rd[:, c, :, :].rearrange("p a b -> p (a b)"),
                    start=True, stop=True))(), reads=[], writes=[f"pb{x}"])
            for (d, h, x, c, eo, hs_, hsn) in ch:
                ts = slice(c * 128, (c + 1) * 128)
                P.op("dve", (lambda x=x, h=h, ts=ts: lambda e: e.tensor_tensor(
                    out=kTs[x], in0=qkT[:, 4 + h, ts], in1=bank(x)[:, 0:128], op=ALU.mult))(),
                    reads=[f"pb{x}"], writes=[f"kTs{x}"])
                P.op("dve", (lambda x=x, h=h, ts=ts: lambda e: e.tensor_tensor(
                    out=qTs[x], in0=qkT[:, h, ts], in1=bank(x)[:, 128:256], op=ALU.mult))(),
                    reads=[f"pb{x}"], writes=[f"qTs{x}"])
            for (d, h, x, c, eo, hs_, hsn) in ch:
                P.op("pe", (lambda x=x: lambda e: e.matmul(bank(x)[:, 256:384], lhsT=kTs[x], rhs=qTs[x], start=True, stop=True))(),
                     reads=[f"kTs{x}", f"qTs{x}"], writes=[f"pb{x}"])
                P.op("pe", (lambda x=x: lambda e: e.transpose(out=bank_bf(x)[:, 768:896], in_=kTs[x], identity=identB))(),
                     reads=[f"kTs{x}"], writes=[f"pb{x}"])
            for (d, h, x, c, eo, hs_, hsn) in ch:
                P.op("dve", (lambda x=x, d=d: lambda e: e.tensor_tensor(
                    out=Sw[x], in0=bank(x)[:, 256:384], in1=maskT[d], op=ALU.mult))(),
                    reads=[f"pb{x}"], writes=[f"Sw{x}"])
                P.op("act", (lambda x=x, d=d, h=h, c=c: lambda e: e.activation(
                    out=k2[x], in_=bank_bf(x)[:, 768:896], func=AF.Identity, scale=decbc[d][:, h, c:c + 1]))(),
                    reads=[f"pb{x}"], writes=[f"k2{x}"])
            for (d, h, x, c, eo, hs_, hsn) in ch:
                if eo:
                    P.op("pe", (lambda x=x: lambda e: e.matmul(
                        bank(x)[:, 0:129], lhsT=qTs[x], rhs=Cbf[:, x, :], start=True, stop=False))(),
                        reads=[f"qTs{x}", f"Cbf{x}"], writes=[f"pb{x}"])
                    P.op("pe", (lambda x=x, c=c, h=h: lambda e: e.matmul(
                        bank(x)[:, 0:129], lhsT=Sw[x], rhs=vaug[:, c, h, :], start=False, stop=True))(),
                        reads=[f"Sw{x}"], writes=[f"pb{x}"])
                P.op("pe", (lambda x=x, c=c, h=h: lambda e: e.matmul(
                    bank(x)[:, 129:258], lhsT=k2[x], rhs=vaug[:, c, h, :], start=True, stop=True))(),
                    reads=[f"k2{x}"], writes=[f"pb{x}"])
            for (d, h, x, c, eo, hs_, hsn) in ch:
                if eo:
                    P.op("act", (lambda x=x: lambda e: e.activation(
                        out=dmx[x][:, 0:1], in_=bank(x)[:, 128:129], func=AF.Abs))(), reads=[f"pb{x}"], writes=[f"dmx{x}"])
                    P.op("dve", (lambda x=x, d=d, c=c, h=h: lambda e: e.tensor_tensor(
                        out=dmx[x][:, 0:1], in0=dmx[x][:, 0:1], in1=emtT[d][:, c, h:h + 1], op=ALU.max))(),
                        reads=[f"dmx{x}"], writes=[f"dmx{x}"])
                    P.op("dve", (lambda x=x: lambda e: e.reciprocal(out=dmx[x][:, 1:2], in_=dmx[x][:, 0:1]))(),
                         reads=[f"dmx{x}"], writes=[f"dmr{x}"])
                    P.op("act", (lambda x=x, h=h, hs_=hs_: lambda e: e.activation(
                        out=hs_[:, h, :], in_=bank(x)[:, 0:128], func=AF.Identity, scale=dmx[x][:, 1:2]))(),
                        reads=[f"pb{x}", f"dmr{x}"], writes=[hsn + str(h)])
                P.op("dve", (lambda x=x, d=d, h=h, c=c: lambda e: e.scalar_tensor_tensor(
                    out=Cst[:, x, :], in0=Cst[:, x, :], scalar=decbc[d][:, h, c:c + 1], in1=bank(x)[:, 129:258],
                    op0=ALU.mult, op1=ALU.add))(), reads=[f"pb{x}", f"Cst{x}"], writes=[f"Cst{x}"])
                P.op("pool", (lambda x=x: lambda e: e.tensor_copy(out=Cbf[:, x, :], in_=Cst[:, x, :]))(),
                     reads=[f"Cst{x}"], writes=[f"Cbf{x}"])
            for d in range(2):
                c = order[d][i]
                if emit_ctx or c >= 2:
                    hs_, hsn = hst[d][hcnt[d] % 2], f"hst{d}{hcnt[d] % 2}"
                    hd = hf_d if d == 0 else hb_d
                    dma("sp", hd[c * 128:(c + 1) * 128, :], hs_.rearrange("p h d -> p (h d)"),
                        reads=[hsn + str(h) for h in range(4)], writes=tres("hfb%d" % d, c * 128, 128))
                    hcnt[d] += 1

        for i in range(NCH):
            do_step(i)
        P.barrier()

        A4 = Alloc(PB2)
        mlg = A4.get([512], F32)
        hfa = [A4.get([512], F32), A4.get([512], F32)]
        hba = [A4.get([512], F32), A4.get([512], F32)]
        mo = [A4.get([512], BF16), A4.get([512], BF16)]
        sq4s = [A4.get([512], F32), A4.get([512], F32)]
        s4s = [A4.get([4], F32), A4.get([4], F32)]
        membs = [A4.get([512], BF16), A4.get([512], BF16)]
        memst = [A4.get([4, 128], BF16), A4.get([4, 128], BF16)]
        dma("sp", mlg, mlg_d[:, L, :], writes=["mlg"])
        trot = Rot([0, 1, 2, 3])
        gs = list(range(0 if emit_ctx else 2, NCH))
        for gi_, g in enumerate(gs):
            p2 = gi_ % 2
            ha, hbv, mov, mst = hfa[p2], hba[p2], mo[p2], memst[p2]
            sq4, s4, memb = sq4s[p2], s4s[p2], membs[p2]
            SQ, S4, MB = f"sq4{p2}", f"s4{p2}", f"memb{p2}"
            rows = slice(g * 128, (g + 1) * 128)
            dma("sp", ha, hf_d[rows, :], writes=[f"hfa{p2}"])
            dma("sp", hbv, hb_d[rows, :], writes=[f"hba{p2}"])
            dma("sp", mov, tmv_d[rows, 640:1152], writes=[f"mo{p2}"])
            P.op("dve", (lambda ha=ha, hbv=hbv: lambda e: e.tensor_tensor(out=ha, in0=ha, in1=hbv, op=ALU.add))(),
                 reads=[f"hfa{p2}", f"hba{p2}"], writes=[f"hfa{p2}"])
            P.op("pool", (lambda ha=ha, sq4=sq4: lambda e: e.tensor_tensor(out=sq4, in0=ha, in1=ha, op=ALU.mult))(), reads=[f"hfa{p2}"], writes=[SQ])
            P.op("dve", (lambda s4=s4, sq4=sq4: lambda e: e.reduce_sum(out=s4, in_=sq4.rearrange("p (h d) -> p h d", h=4), axis=AX.X))(), reads=[SQ], writes=[S4])
            P.op("act", (lambda s4=s4: lambda e: e.activation(out=s4, in_=s4, func=AF.Sqrt, scale=1.0 / 128, bias=cst[:, 0:1]))(), reads=[S4] + CST, writes=[S4])
            P.op("dve", (lambda s4=s4: lambda e: e.reciprocal(out=s4, in_=s4))(), reads=[S4], writes=[S4])
            P.op("dve", (lambda ha=ha, s4=s4: lambda e: e.tensor_tensor(out=ha.rearrange("p (h d) -> p h d", h=4),
                                                                in0=ha.rearrange("p (h d) -> p h d", h=4),
                                                                in1=s4.unsqueeze(2).to_broadcast([128, 4, 128]), op=ALU.mult))(),
                 reads=[f"hfa{p2}", S4], writes=[f"hfa{p2}"])
            P.op("pool", (lambda ha=ha: lambda e: e.tensor_tensor(out=ha, in0=ha, in1=mlg, op=ALU.mult))(), reads=[f"hfa{p2}", "mlg"], writes=[f"hfa{p2}"])
            P.op("pool", (lambda ha=ha, mov=mov, memb=memb: lambda e: e.tensor_tensor(out=memb, in0=ha, in1=mov, op=ALU.mult))(),
                 reads=[f"hfa{p2}", f"mo{p2}"], writes=[MB])
            tb = trot.next()
            for h in range(4):
                P.op("pe", (lambda tb=tb, h=h, memb=memb: lambda e: e.transpose(out=bank_bf(tb)[:, h * 128:(h + 1) * 128],
                                                                    in_=memb[:, h * 128:(h + 1) * 128], identity=identB))(),
                     reads=[MB, "identB"], writes=[f"pb{tb}"])
            P.op("act", (lambda tb=tb, mst=mst: lambda e: e.activation(out=mst, in_=bank_bf(tb)[:, 0:512].rearrange("p (a b) -> p a b", a=4),
                                                                       func=AF.Copy))(), reads=[f"pb{tb}"], writes=[f"memst{p2}"])
            dma("sp", attT_d[4:8, :, g * 128:(g + 1) * 128].rearrange("j p t -> p j t"), mst, reads=[f"memst{p2}"],
                writes=tres("attT_m", g * 128, 128))
        P.barrier()

    def phase_mlp(L, emit_ctx, prefetched):
        w1, w2 = w4_views()
        if not prefetched:
            for k in range(8):
                dma("pool", w1[:, k, :], w1_d[L, k * 128:(k + 1) * 128, :], writes=[f"w1_{k}"])
            for k4 in range(8):
                dma("pool", w2[:, k4 * 4:(k4 + 1) * 4, :],
                    w2_d[L, k4 * 512:(k4 + 1) * 512, :].rearrange("(k p) c -> p k c", p=128), writes=[f"w2_{k4}"])
        W1 = [f"w1_{k}" for k in range(8)]
        W2 = [f"w2_{k}" for k in range(8)]
        A = Alloc(PBASE)
        NB = 256
        w_out = A.get([8, D], BF16)
        dma("pool", w_out, w_out_d[L].rearrange("(k p) c -> p k c", p=128), writes=["w_out"])
        attTb = [A.get([8, NB], BF16)]
        xTb = [A.get([8, NB], F32), A.get([8, NB], F32)]
        h2T = A.get([8, NB], BF16)
        sq = [A.get([NB], BF16), A.get([NB], BF16)]
        rstd = A.get([NB], F32)
        xn = [A.get([NB], F32), A.get([NB], F32)]
        rl = [A.get([NB], BF16), A.get([NB], BF16)]
        uT = A.get([32, NB], BF16)
        assert A.off <= WOFF, A.off
        rot = Rot([1, 2, 3, 4, 5, 6, 7])
        t0s = list(range(0 if emit_ctx else TC, T, NB))
        def do_block(bi, t0):
            n = NB
            r = 1 if t0 < TC else 0
            ab, abn = attTb[0], "attTb0"
            xb, xbn = xTb[bi % 2], f"xTb{bi % 2}"
            dma("sp", ab, attT_d[:, :, t0:t0 + n].rearrange("j p t -> p j t"), writes=[abn])
            dma("sp", xb, xT_d[:, :, t0:t0 + n].rearrange("j p t -> p j t"), writes=[xbn])
            for cj in range(8):
                b = rot.next()
                for k in range(8):
                    P.op("pe", (lambda b=b, k=k, cj=cj, ab=ab: lambda e: e.matmul(
                        bank(b)[:, 0:n], lhsT=w_out[:, k, cj * 128:(cj + 1) * 128], rhs=ab[:, k, :], start=(k == 0), stop=(k == 7)))(),
                        reads=["w_out", abn], writes=[f"pb{b}"])
                P.op("dve", (lambda b=b, cj=cj, xb=xb, r=r: lambda e: e.scalar_tensor_tensor(
                    out=xb[:, cj, :], in0=bank(b)[:, 0:n], scalar=modcol(L, 16 + cj, r), in1=xb[:, cj, :], op0=ALU.mult, op1=ALU.add))(),
                    reads=[f"pb{b}", xbn, "mod"], writes=[xbn])
            norm_mod(xb, xbn, n, sq, rstd, xn, h2T, "h2T",
                     lambda j: s2[:, L * 8 + j, r:r + 1], lambda j: modcol(L, 24 + j, r), 0)
            for fc in range(32):
                b = rot.next()
                for k in range(8):
                    P.op("pe", (lambda b=b, k=k, fc=fc: lambda e: e.matmul(
                        bank(b)[:, 0:n], lhsT=w1[:, k, fc * 128:(fc + 1) * 128], rhs=h2T[:, k, :], start=(k == 0), stop=(k == 7)))(),
                        reads=[f"w1_{k}", "h2T"], writes=[f"pb{b}"])
                r_, rn_ = rl[fc % 2], f"rl{fc % 2}"
                P.op("act", (lambda b=b, r_=r_: lambda e: e.activation(out=r_, in_=bank(b)[:, 0:n], func=AF.Relu))(),
                     reads=[f"pb{b}"], writes=[rn_])
                P.op("pool", (lambda r_=r_, fc=fc: lambda e: e.tensor_tensor(out=uT[:, fc, :], in0=r_, in1=r_, op=ALU.mult))(),
                     reads=[rn_], writes=[f"uT{fc}"])
            for cj in range(8):
                b = rot.next()
                for fc in range(32):
                    P.op("pe", (lambda b=b, fc=fc, cj=cj: lambda e: e.matmul(
                        bank(b)[:, 0:n], lhsT=w2[:, fc, cj * 128:(cj + 1) * 128], rhs=uT[:, fc, :], start=(fc == 0), stop=(fc == 31)))(),
                        reads=[f"w2_{fc // 4}", f"uT{fc}"], writes=[f"pb{b}"])
                P.op("dve", (lambda b=b, cj=cj, xb=xb, r=r: lambda e: e.scalar_tensor_tensor(
                    out=xb[:, cj, :], in0=bank(b)[:, 0:n], scalar=modcol(L, 40 + cj, r), in1=xb[:, cj, :], op0=ALU.mult, op1=ALU.add))(),
                    reads=[f"pb{b}", xbn, "mod"], writes=[xbn])
            dma("sp", xT_d[:, :, t0:t0 + n].rearrange("j p t -> p j t"), xb, reads=[xbn], writes=tres("xTo", t0, n))
        for bi, t0 in enumerate(t0s):
            do_block(bi, t0)
        P.barrier()

    def phase_final():
        A = Alloc(PBASE)
        xTb = [A.get([8, 512], F32), A.get([8, 512], F32)]
        sq = [A.get([512], BF16), A.get([512], BF16)]
        rstd = A.get([512], F32)
        xnf = A.get([8, 512], F32)
        ost = [A.get([1024], F32), A.get([1024], F32)]
        rot = Rot([1, 2, 3, 4, 5, 6])
        oc = 0
        def do_block(bi):
            nonlocal oc
            t0 = TC + bi * 512
            n = 512
            xb, xbn = xTb[bi % 2], f"xTb{bi % 2}"
            dma("sp", xb, xT_d[:, :, t0:t0 + n].rearrange("j p t -> p j t"), writes=[xbn])
            ssps = bank(0)
            for j in range(8):
                s_, sn = sq[j % 2], f"sq{j % 2}"
                P.op("act", (lambda s_=s_, j=j, xb=xb: lambda e: e.activation(out=s_, in_=xb[:, j, :], func=AF.Square))(), reads=[xbn], writes=[sn])
                P.op("pe", (lambda s_=s_, j=j: lambda e: e.matmul(ssps, lhsT=onesB, rhs=s_, start=(j == 0), stop=(j == 7)))(),
                     reads=[sn, "onesB"], writes=["pb0"])
            P.op("act", lambda e: e.activation(out=rstd, in_=ssps, func=AF.Sqrt, scale=1.0 / D, bias=cst[:, 0:1]), reads=["pb0"] + CST, writes=["rstd"])
            P.op("dve", lambda e: e.reciprocal(out=rstd, in_=rstd), reads=["rstd"], writes=["rstd"])
            for j in range(8):
                P.op("dve", (lambda j=j, xb=xb: lambda e: e.scalar_tensor_tensor(
                    out=xnf[:, j, :], in0=xb[:, j, :], scalar=nfin[:, j:j + 1], in1=rstd, op0=ALU.mult, op1=ALU.mult))(),
                    reads=[xbn, "rstd", "nfin"], writes=[f"xnf{j}"])
            for m in range(4):
                o_, on_ = ost[oc % 2], f"ost{oc % 2}"
                oc += 1
                for half in range(2):
                    b = rot.next()
                    for jj in range(4):
                        j = half * 4 + jj
                        P.op("pe", (lambda b=b, jj=jj, j=j, m=m: lambda e: e.transpose(
                            out=bank(b)[:, jj * 128:(jj + 1) * 128], in_=xnf[:, j, m * 128:(m + 1) * 128], identity=identF))(),
                            reads=[f"xnf{j}", "identF"], writes=[f"pb{b}"])
                    if half == 0:
                        P.op("act", (lambda b=b, o_=o_: lambda e: e.activation(out=o_[:, 0:512], in_=bank(b), func=AF.Copy))(),
                             reads=[f"pb{b}"], writes=[on_ + "a"])
                    else:
                        P.op("dve", (lambda b=b, o_=o_: lambda e: e.tensor_copy(out=o_[:, 512:1024], in_=bank(b)))(),
                             reads=[f"pb{b}"], writes=[on_ + "b"])
                r0 = bi * 512 + m * 128
                dma("sp", out_d[r0:r0 + 128, :], o_, reads=[on_ + "a", on_ + "b"], writes=[f"out{r0}"])

        for bi in range(8):
            do_block(bi)

    steps = [("mod", phase_mod), ("xT", phase_xT)]
    for L in range(n_layers):
        emit_ctx = L < DEPTH - 1
        steps.append((f"inproj{L}", (lambda L=L: phase_inproj(L))))
        steps.append((f"mlstm{L}", (lambda L=L, ec=emit_ctx: phase_mlstm(L, ec))))
        steps.append((f"attn{L}", (lambda L=L, ec=emit_ctx: phase_attn(L, ec, prefetch=True))))
        steps.append((f"mlp{L}", (lambda L=L, ec=emit_ctx: phase_mlp(L, ec, prefetched=True))))
    steps.append(("final", phase_final))
    for name, fn in steps:
        fn()
        if stop is not None and name == stop:
            break
    P.emit(es)
    es.close()
    return nc, P


def _perm_att():
    idx = np.zeros(512, dtype=np.int64)
    for j in range(4):
        for p in range(128):
            head = j if p < 64 else j + 4
            idx[j * 128 + p] = head * 64 + (p % 64)
    return idx


def _rope_tables():
    rows = TL // 64
    row_idx = np.repeat(np.arange(rows, dtype=np.float32), 64)
    col_idx = np.tile(np.arange(64, dtype=np.float32), rows)
    inv_freq = np.power(np.float32(10000.0), -np.arange(0, 32, 2, dtype=np.float32) / np.float32(32)).astype(np.float32)
    ang = np.concatenate([row_idx[:, None] * inv_freq, col_idx[:, None] * inv_freq], axis=-1).astype(np.float32)
    cos = np.cos(ang).astype(np.float32).reshape(32, 128, 32).transpose(1, 0, 2)
    sin = np.sin(ang).astype(np.float32).reshape(32, 128, 32).transpose(1, 0, 2)
    return np.ascontiguousarray(cos), np.ascontiguousarray(sin)


def host_inputs(inputs, cores):
    f = lambda a: np.ascontiguousarray(np.asarray(a, dtype=np.float32))
    perm = _perm_att()
    w_in = f(inputs["w_in"])
    w_in_p = np.zeros((DEPTH, D, INCP), np.float32)
    w_in_p[:, :, 0:INC] = w_in
    w_in_p[:, :, 0:512] = w_in[:, :, perm]
    w_out = f(inputs["w_out"])
    w_out_p = w_out.copy()
    w_out_p[:, 0:512, :] = w_out[:, perm, :]
    cos, sin = _rope_tables()
    sel = np.zeros((128, 4, 128), np.float32)
    for h in range(4):
        sel[h, h, :] = 1.0
    s_ = np.arange(128)
    maskf = (s_[:, None] <= s_[None, :]).astype(np.float32)
    maskb = (s_[:, None] >= s_[None, :]).astype(np.float32)
    colL = lambda a, nj: np.ascontiguousarray(f(a).reshape(DEPTH, nj, 128).transpose(2, 0, 1))
    qk_gain = np.concatenate([np.tile(f(inputs["q_norm"]), (1, 8)), np.tile(f(inputs["k_norm"]), (1, 2))], axis=1)
    shared = {
        "w_ada": f(inputs["w_ada"]),
        "b_ada_c": colL(inputs["b_ada"], 48),
        "nmix_c": colL(inputs["norm_mix"], 8),
        "nmlp_c": colL(inputs["norm_mlp"], 8),
        "w_in_p": w_in_p,
        "bg_c": np.ascontiguousarray(f(inputs["b_gates"]).T),
        "conv_c": np.ascontiguousarray(f(inputs["conv_qk"]).reshape(DEPTH, 3, 8, 128).transpose(3, 0, 1, 2)),
        "qk_gain": np.ascontiguousarray(np.broadcast_to(qk_gain[None], (128, DEPTH, 640))),
        "ml_gain": np.ascontiguousarray(np.broadcast_to(f(inputs["mlstm_norm"])[None], (128, DEPTH, 512))),
        "w_out_p": w_out_p,
        "w_mlp_in": f(inputs["w_mlp_in"]),
        "w_mlp_out": f(inputs["w_mlp_out"]),
        "nfin_c": np.ascontiguousarray(f(inputs["norm_final"]).reshape(8, 128).T),
        "ident": np.eye(128, dtype=np.float32),
        "maskf": maskf,
        "maskb": maskb,
        "sel4": sel,
        "rope_cos": cos,
        "rope_sin": sin,
    }
    x = f(inputs["x"])
    ctx = f(inputs["ctx"])
    c = f(inputs["c"])
    c_ctx = f(inputs["c_ctx"])
    maps = []
    for b in cores:
        cc = np.stack([c[b].reshape(8, 128).T, c_ctx.reshape(8, 128).T], axis=-1)
        m = dict(shared)
        m["x"] = x[b]
        m["ctx"] = ctx[b]
        m["cc"] = np.ascontiguousarray(cc)
        maps.append(m)
    return maps


_NC_CACHE = {}


def kernel(**inputs):
    if "nc" not in _NC_CACHE:
        _NC_CACHE["nc"] = build()[0]
    nc = _NC_CACHE["nc"]
    maps = host_inputs(inputs, list(range(8)))
    res = run_bass_kernel_spmd(nc, maps, core_ids=list(range(8)))
    out = np.stack([np.asarray(r["out"], dtype=np.float32) for r in res.results], axis=0)
    return out
```
